# Optimizing a Trainium2 kernel written in Bass

```python
import math
import jax, jax.numpy as jnp
from jax import lax
import numpy as np

D_MODEL = 2048
BATCH = 16
SEQ = 2048
DEPTH = 2

GRID_W = 64
CTX_LEN = 256
N_BRANCH = 3
W_HY = 768
HY_ORDER = 2
HY_BANDS = 16
HY_FEAT = 1 + 2 * HY_BANDS
HY_HIDDEN = 64
HY_DECAY_MIN = -math.log(1e-2) / 1.5
HY_DECAY_MAX = -math.log(1e-2) / 0.3
W_ML = 768
ML_HEADS = 4
ML_DH = W_ML // ML_HEADS
ML_CHUNK = 64
W_S5 = 768
S5_GROUP = 16
S5_GROUPS = W_S5 // S5_GROUP
S5_STATE = 64
D_FF = 4 * D_MODEL
SPLIT_SIZES = ((HY_ORDER + 1) * W_HY, 2 * W_ML, W_ML, W_ML, 4 * ML_HEADS, W_S5, N_BRANCH * D_MODEL)
SPLIT_POINTS = tuple(int(v) for v in np.cumsum(SPLIT_SIZES)[:-1])
N_IN = int(sum(SPLIT_SIZES))
DEEPNORM_ALPHA = (2 * DEPTH) ** 0.25
DEEPNORM_BETA = (8 * DEPTH) ** -0.25
LN_EPS = 1e-5
F32 = jnp.float32

kernel_name = 'hybrid_hyena_mlstm_s5_diffusion_block'


def layer_norm(x, g, b):
    xf = x.astype(F32)
    mu = jnp.mean(xf, axis=-1, keepdims=True)
    var = jnp.mean(jnp.square(xf - mu), axis=-1, keepdims=True)
    y = (xf - mu) * lax.rsqrt(var + LN_EPS)
    return (y * g.astype(F32) + b.astype(F32)).astype(x.dtype)


def modulate(h, shift, scale):
    return h * (1 + scale) + shift


def short_conv(x, w, b, rows):
    B, L, C = x.shape
    xp = jnp.pad(x.reshape(B, rows, L // rows, C), ((0, 0), (0, 0), (1, 1), (0, 0)))
    y = xp[:, :, :-2] * w[0] + xp[:, :, 1:-1] * w[1] + xp[:, :, 2:] * w[2] + b
    return y.reshape(B, L, C)


def hyena_filters(L, w1, b1, w2, b2, w3, freq, decay):
    pos = jnp.arange(L, dtype=F32)
    t = pos / (L - 1)
    bands = jnp.linspace(1e-4, HY_BANDS - 1, HY_BANDS, dtype=F32)
    ang = (2.0 * math.pi / L) * pos[:, None] * bands[None, :]
    feats = jnp.concatenate([t[:, None], jnp.cos(ang), jnp.sin(ang)], axis=-1)
    freq = freq.astype(F32)
    h = jnp.sin(freq * (feats @ w1.astype(F32) + b1.astype(F32)))
    h = jnp.sin(freq * (h @ w2.astype(F32) + b2.astype(F32)))
    h = (h @ w3.astype(F32)).reshape(L, HY_ORDER, 2, W_HY)
    window = jnp.exp(-t[:, None, None] * jnp.abs(decay.astype(F32))[None])
    h = h * window[:, :, None, :]
    k_circ = jnp.concatenate([h[:, :, 0], jnp.zeros((1, HY_ORDER, W_HY), F32), h[:0:-1, :, 1]], axis=0)
    return jnp.fft.rfft(k_circ, axis=0)


def fft_long_conv(u, k_freq, bias):
    L = u.shape[1]
    y = jnp.fft.irfft(jnp.fft.rfft(u, n=2 * L, axis=1) * k_freq, n=2 * L, axis=1)[:, :L]
    return y + u * bias


def hyena_branch(z, rows, conv_w, conv_b, w1, b1, w2, b2, w3, freq, decay, bias):
    z = short_conv(z, conv_w, conv_b, rows)
    v, x1, x2 = jnp.split(z.astype(F32), HY_ORDER + 1, axis=-1)
    k_freq = hyena_filters(z.shape[1], w1, b1, w2, b2, w3, freq, decay)
    bias = bias.astype(F32)
    y = x1 * fft_long_conv(v, k_freq[:, 0], bias[0])
    y = x2 * fft_long_conv(y, k_freq[:, 1], bias[1])
    return y.astype(z.dtype)


def mlstm_zero_state(B):
    return (jnp.zeros((B, ML_HEADS, ML_DH, ML_DH), F32), jnp.zeros((B, ML_HEADS, ML_DH), F32),
            jnp.zeros((B, ML_HEADS), F32))


def mlstm_chunk_scan(q, k, v, li, lf, state):
    B, H, L, Dh = q.shape
    nc = L // ML_CHUNK

    def chunks(a):
        return jnp.moveaxis(a.reshape((B, H, nc, ML_CHUNK) + a.shape[3:]), 2, 0)

    causal = jnp.tril(jnp.ones((ML_CHUNK, ML_CHUNK), bool))

    def step(carry, inp):
        C, n, m = carry
        qc, kc, vc, lic, lfc = inp
        b = jnp.cumsum(lfc, axis=-1)
        d = jnp.where(causal, b[..., :, None] - b[..., None, :] + lic[..., None, :], -jnp.inf)
        g = b + m[..., None]
        m_t = jnp.maximum(g, d.max(-1))
        w_inter = jnp.exp(g - m_t)
        s = jnp.einsum('bhtd,bhsd->bhts', qc, kc) * jnp.exp(d - m_t[..., None])
        num = w_inter[..., None] * jnp.einsum('bhvd,bhtd->bhtv', C, qc) + jnp.einsum('bhts,bhsv->bhtv', s, vc)
        den = w_inter * jnp.einsum('bhd,bhtd->bht', n, qc) + s.sum(-1)
        h = num / jnp.maximum(jnp.abs(den), jnp.exp(-m_t))[..., None]
        a = b[..., -1:] - b + lic
        m_new = jnp.maximum(b[..., -1] + m, a.max(-1))
        wk = jnp.exp(a - m_new[..., None])
        dec = jnp.exp(b[..., -1] + m - m_new)
        C = dec[..., None, None] * C + jnp.einsum('bhs,bhsv,bhsd->bhvd', wk, vc, kc)
        n = dec[..., None] * n + jnp.einsum('bhs,bhsd->bhd', wk, kc)
        return (C, n, m_new), h

    state, h = lax.scan(step, state, (chunks(q), chunks(k), chunks(v), chunks(li), chunks(lf)))
    return jnp.moveaxis(h, 0, 2).reshape(B, H, L, Dh), state


def mlstm_prep(qk_pre, v_pre, gate_pre, rows, conv_w, conv_b, gate_b):
    B, L, _ = qk_pre.shape
    qk = jax.nn.silu(short_conv(qk_pre, conv_w, conv_b, rows)).astype(F32)

    def heads(a):
        return a.reshape(B, L, ML_HEADS, ML_DH).transpose(0, 2, 1, 3)

    q = heads(qk[..., :W_ML])
    k = heads(qk[..., W_ML:]) * (ML_DH ** -0.5)
    v = heads(v_pre.astype(F32))
    gates = (gate_pre.astype(F32).reshape(B, L, 4, ML_HEADS) + gate_b.astype(F32)).transpose(2, 0, 3, 1)
    return (q, k, v, gates[0], jax.nn.log_sigmoid(gates[1]), gates[2], jax.nn.log_sigmoid(gates[3]))


def mlstm_bidir(prep, state_f, state_b):
    q, k, v, li_f, lf_f, li_b, lf_b = prep
    h_f, st_f = mlstm_chunk_scan(q, k, v, li_f, lf_f, state_f)
    fl = lambda a: jnp.flip(a, axis=2)
    h_b, st_b = mlstm_chunk_scan(fl(q), fl(k), fl(v), fl(li_b), fl(lf_b), state_b)
    return h_f + fl(h_b), st_f, st_b


def mlstm_out(h, o_pre, norm_g):
    B, H, L, Dh = h.shape
    mu = jnp.mean(h, axis=-1, keepdims=True)
    var = jnp.mean(jnp.square(h - mu), axis=-1, keepdims=True)
    hn = ((h - mu) * lax.rsqrt(var + LN_EPS)).transpose(0, 2, 1, 3).reshape(B, L, W_ML)
    return (hn * norm_g.astype(F32) * jax.nn.sigmoid(o_pre.astype(F32))).astype(o_pre.dtype)


def s5_zero_state(B):
    return (jnp.zeros((B, S5_GROUPS, S5_STATE), F32), jnp.zeros((B, S5_GROUPS, S5_STATE), F32))


def s5_discretise(a_re, a_im, log_dt, b_re, b_im):
    a_re, a_im, b_re, b_im = (t.astype(F32) for t in (a_re, a_im, b_re, b_im))
    dt = jnp.exp(log_dt.astype(F32))[:, None]
    mag = jnp.exp(dt * a_re)
    ab_re, ab_im = mag * jnp.cos(dt * a_im), mag * jnp.sin(dt * a_im)
    den = jnp.square(a_re) + jnp.square(a_im)
    co_re = ((ab_re - 1.0) * a_re + ab_im * a_im) / den
    co_im = (ab_im * a_re - (ab_re - 1.0) * a_im) / den
    bb_re = co_re[..., None] * b_re - co_im[..., None] * b_im
    bb_im = co_re[..., None] * b_im + co_im[..., None] * b_re
    return ab_re, ab_im, bb_re, bb_im


def s5_combine(e1, e2):
    a1r, a1i, b1r, b1i = e1
    a2r, a2i, b2r, b2i = e2
    return (a1r * a2r - a1i * a2i, a1r * a2i + a1i * a2r,
            a2r * b1r - a2i * b1i + b2r, a2r * b1i + a2i * b1r + b2i)


def s5_direction(u, ab_re, ab_im, bb_re, bb_im, c_re, c_im, x0, readout):
    L = u.shape[1]
    bu_re = jnp.einsum('blgc,gpc->blgp', u, bb_re)
    bu_im = jnp.einsum('blgc,gpc->blgp', u, bb_im)
    x0_re, x0_im = x0
    bu_re = bu_re.at[:, 0].add(ab_re * x0_re - ab_im * x0_im)
    bu_im = bu_im.at[:, 0].add(ab_re * x0_im + ab_im * x0_re)
    a_re = jnp.broadcast_to(ab_re[None, None], (1, L) + ab_re.shape)
    a_im = jnp.broadcast_to(ab_im[None, None], (1, L) + ab_im.shape)
    _, _, s_re, s_im = lax.associative_scan(s5_combine, (a_re, a_im, bu_re, bu_im), axis=1)
    final = (s_re[:, -1], s_im[:, -1])
    if not readout:
        return None, final
    y = (jnp.einsum('blgp,gcp->blgc', s_re, c_re.astype(F32))
         - jnp.einsum('blgp,gcp->blgc', s_im, c_im.astype(F32)))
    return y, final


def s5_bidir(u, disc, c_re, c_im, x0_f, x0_b, readout):
    B, L, _ = u.shape
    ug = u.astype(F32).reshape(B, L, S5_GROUPS, S5_GROUP)
    y_f, st_f = s5_direction(ug, *disc[0], c_re[0], c_im[0], x0_f, readout)
    y_b, st_b = s5_direction(jnp.flip(ug, 1), *disc[1], c_re[1], c_im[1], x0_b, readout)
    y = y_f + jnp.flip(y_b, 1) if readout else None
    return y, st_f, st_b


def s5_out(y, u, d, glu_w, glu_b):
    B, L, _ = u.shape
    y = y.reshape(B, L, W_S5) + d.astype(F32) * u.astype(F32)
    z = jax.nn.gelu(y).astype(u.dtype)
    return z * jax.nn.sigmoid(z @ glu_w + glu_b)


def merge_branches(y_hy, y_ml, y_s5, gate_pre, w_hy_out, w_ml_out, w_s5_out, w_out):
    g_hy, g_ml, g_s5 = jnp.split(jax.nn.sigmoid(gate_pre), N_BRANCH, axis=-1)
    merged = g_hy * (y_hy @ w_hy_out) + g_ml * (y_ml @ w_ml_out) + g_s5 * (y_s5 @ w_s5_out)
    return merged @ w_out


def token_mixer(h_ctx, h_lat, rows, need_ctx, w_in, hy_p, ml_p, s5_p, out_p):
    hy_c, qk_c, v_c, o_c, gt_c, u_c, mg_c = jnp.split(h_ctx @ w_in, SPLIT_POINTS, axis=-1)
    hy_l, qk_l, v_l, o_l, gt_l, u_l, mg_l = jnp.split(h_lat @ w_in, SPLIT_POINTS, axis=-1)
    B = h_lat.shape[0]
    ml_conv_w, ml_conv_b, ml_gate_b, ml_norm_g = ml_p
    zm = mlstm_zero_state(B)
    hm_c, mst_f, mst_b = mlstm_bidir(mlstm_prep(qk_c, v_c, gt_c, 1, ml_conv_w, ml_conv_b, ml_gate_b), zm, zm)
    hm_l, _, _ = mlstm_bidir(mlstm_prep(qk_l, v_l, gt_l, rows, ml_conv_w, ml_conv_b, ml_gate_b), mst_f, mst_b)
    a_re, a_im, log_dt, b_re, b_im, c_re, c_im, s5_d, glu_w, glu_b = s5_p
    disc = [s5_discretise(a_re[i], a_im[i], log_dt[i], b_re[i], b_im[i]) for i in range(2)]
    zs = s5_zero_state(B)
    ys_c, sst_f, sst_b = s5_bidir(u_c, disc, c_re, c_im, zs, zs, need_ctx)
    ys_l, _, _ = s5_bidir(u_l, disc, c_re, c_im, sst_f, sst_b, True)
    y_lat = merge_branches(hyena_branch(hy_l, rows, *hy_p), mlstm_out(hm_l, o_l, ml_norm_g),
                           s5_out(ys_l, u_l, s5_d, glu_w, glu_b), mg_l, *out_p)
    if not need_ctx:
        return None, y_lat
    y_ctx = merge_branches(hyena_branch(hy_c, 1, *hy_p), mlstm_out(hm_c, o_c, ml_norm_g),
                           s5_out(ys_c, u_c, s5_d, glu_w, glu_b), mg_c, *out_p)
    return y_ctx, y_lat


def sq_relu_mlp(h, w1, w2):
    return jnp.square(jax.nn.relu(h @ w1)) @ w2


def setup_inputs(seed: int = 0) -> dict:
    key = jax.random.key(seed)
    ks = iter(jax.random.split(key, 64))

    def nrm(shape, scale=1.0):
        return scale * jax.random.normal(next(ks), shape, F32)

    H = ML_HEADS
    G, P, CG = S5_GROUPS, S5_STATE, S5_GROUP
    f_bias = jnp.linspace(3.0, 6.0, H, dtype=F32)
    gate_base = jnp.stack([jnp.zeros((H,), F32), f_bias, jnp.zeros((H,), F32), f_bias])
    n_idx = jnp.arange(P, dtype=F32)
    return {
        'x': nrm((BATCH, SEQ, D_MODEL)),
        'c': nrm((BATCH, D_MODEL)),
        'ctx': nrm((BATCH, CTX_LEN, D_MODEL)),
        'c_ctx': nrm((D_MODEL,)),
        'w_mod': nrm((DEPTH, D_MODEL, 6 * D_MODEL), 0.5 * D_MODEL ** -0.5),
        'b_mod': nrm((DEPTH, 6 * D_MODEL), 0.02),
        'w_in': nrm((DEPTH, D_MODEL, N_IN), D_MODEL ** -0.5),
        'hy_conv_w': nrm((DEPTH, 3, (HY_ORDER + 1) * W_HY), 0.5),
        'hy_conv_b': nrm((DEPTH, (HY_ORDER + 1) * W_HY), 0.02),
        'hy_ffn_w1': nrm((DEPTH, HY_FEAT, HY_HIDDEN), HY_FEAT ** -0.5),
        'hy_ffn_b1': nrm((DEPTH, HY_HIDDEN), 0.1),
        'hy_ffn_w2': nrm((DEPTH, HY_HIDDEN, HY_HIDDEN), HY_HIDDEN ** -0.5),
        'hy_ffn_b2': nrm((DEPTH, HY_HIDDEN), 0.1),
        'hy_ffn_w3': nrm((DEPTH, HY_HIDDEN, HY_ORDER * 2 * W_HY), 0.05 * HY_HIDDEN ** -0.5),
        'hy_sin_freq': 1.0 + nrm((DEPTH, HY_HIDDEN), 0.01),
        'hy_decay': jnp.linspace(HY_DECAY_MIN, HY_DECAY_MAX, W_HY, dtype=F32) + nrm((DEPTH, HY_ORDER, W_HY), 0.01),
        'hy_bias': nrm((DEPTH, HY_ORDER, W_HY)),
        'ml_conv_w': nrm((DEPTH, 3, 2 * W_ML), 0.5),
        'ml_conv_b': nrm((DEPTH, 2 * W_ML), 0.02),
        'ml_gate_b': gate_base + nrm((DEPTH, 4, H), 0.1),
        'ml_norm_g': 1.0 + nrm((DEPTH, W_ML), 0.02),
        's5_a_re': -0.5 + nrm((DEPTH, 2, G, P), 0.01),
        's5_a_im': math.pi * n_idx + nrm((DEPTH, 2, G, P), 0.01),
        's5_log_dt': jax.random.uniform(next(ks), (DEPTH, 2, G), F32, math.log(1e-3), math.log(1e-1)),
        's5_b_re': nrm((DEPTH, 2, G, P, CG), (2 * CG) ** -0.5),
        's5_b_im': nrm((DEPTH, 2, G, P, CG), (2 * CG) ** -0.5),
        's5_c_re': nrm((DEPTH, 2, G, CG, P), P ** -0.5),
        's5_c_im': nrm((DEPTH, 2, G, CG, P), P ** -0.5),
        's5_d': nrm((DEPTH, W_S5)),
        's5_glu_w': nrm((DEPTH, W_S5, W_S5), W_S5 ** -0.5),
        's5_glu_b': nrm((DEPTH, W_S5), 0.02),
        'w_hy_out': nrm((DEPTH, W_HY, D_MODEL), DEEPNORM_BETA * W_HY ** -0.5),
        'w_ml_out': nrm((DEPTH, W_ML, D_MODEL), DEEPNORM_BETA * W_ML ** -0.5),
        'w_s5_out': nrm((DEPTH, W_S5, D_MODEL), DEEPNORM_BETA * W_S5 ** -0.5),
        'w_out': nrm((DEPTH, D_MODEL, D_MODEL), DEEPNORM_BETA * D_MODEL ** -0.5),
        'ln1_g': 1.0 + nrm((DEPTH, D_MODEL), 0.02),
        'ln1_b': nrm((DEPTH, D_MODEL), 0.02),
        'ln2_g': 1.0 + nrm((DEPTH, D_MODEL), 0.02),
        'ln2_b': nrm((DEPTH, D_MODEL), 0.02),
        'w_ff1': nrm((DEPTH, D_MODEL, D_FF), DEEPNORM_BETA * D_MODEL ** -0.5),
        'w_ff2': nrm((DEPTH, D_FF, D_MODEL), DEEPNORM_BETA * D_FF ** -0.5),
    }


def reference(x, c, ctx, c_ctx, w_mod, b_mod, w_in, hy_conv_w, hy_conv_b, hy_ffn_w1, hy_ffn_b1,
              hy_ffn_w2, hy_ffn_b2, hy_ffn_w3, hy_sin_freq, hy_decay, hy_bias, ml_conv_w, ml_conv_b,
              ml_gate_b, ml_norm_g, s5_a_re, s5_a_im, s5_log_dt, s5_b_re, s5_b_im, s5_c_re, s5_c_im,
              s5_d, s5_glu_w, s5_glu_b, w_hy_out, w_ml_out, w_s5_out, w_out, ln1_g, ln1_b, ln2_g, ln2_b,
              w_ff1, w_ff2):
    rows = x.shape[1] // GRID_W
    silu_c = jax.nn.silu(c)
    silu_cc = jax.nn.silu(c_ctx)
    for l in range(DEPTH):
        need_ctx = l < DEPTH - 1
        mod_l = jnp.split((silu_c @ w_mod[l] + b_mod[l])[:, None, :], 6, axis=-1)
        mod_c = jnp.split(silu_cc @ w_mod[l] + b_mod[l], 6, axis=-1)
        hy_p = (hy_conv_w[l], hy_conv_b[l], hy_ffn_w1[l], hy_ffn_b1[l], hy_ffn_w2[l], hy_ffn_b2[l],
                hy_ffn_w3[l], hy_sin_freq[l], hy_decay[l], hy_bias[l])
        ml_p = (ml_conv_w[l], ml_conv_b[l], ml_gate_b[l], ml_norm_g[l])
        s5_p = (s5_a_re[l], s5_a_im[l], s5_log_dt[l], s5_b_re[l], s5_b_im[l], s5_c_re[l], s5_c_im[l],
                s5_d[l], s5_glu_w[l], s5_glu_b[l])
        out_p = (w_hy_out[l], w_ml_out[l], w_s5_out[l], w_out[l])
        y_ctx, y_lat = token_mixer(modulate(ctx, mod_c[0], mod_c[1]), modulate(x, mod_l[0], mod_l[1]),
                                   rows, need_ctx, w_in[l], hy_p, ml_p, s5_p, out_p)
        x = layer_norm(DEEPNORM_ALPHA * x + mod_l[2] * y_lat, ln1_g[l], ln1_b[l])
        x = layer_norm(DEEPNORM_ALPHA * x + mod_l[5] * sq_relu_mlp(modulate(x, mod_l[3], mod_l[4]), w_ff1[l], w_ff2[l]),
                       ln2_g[l], ln2_b[l])
        if need_ctx:
            ctx = layer_norm(DEEPNORM_ALPHA * ctx + mod_c[2] * y_ctx, ln1_g[l], ln1_b[l])
            ctx = layer_norm(DEEPNORM_ALPHA * ctx + mod_c[5] * sq_relu_mlp(modulate(ctx, mod_c[3], mod_c[4]), w_ff1[l], w_ff2[l]),
                             ln2_g[l], ln2_b[l])
    return x
```

```python
import math
from contextlib import ExitStack
import numpy as np
import concourse.bass as bass
import concourse.mybir as mybir
from concourse.bass_utils import run_bass_kernel_spmd

F32 = mybir.dt.float32
BF16 = mybir.dt.bfloat16
I32 = mybir.dt.int32
ALU = mybir.AluOpType
AF = mybir.ActivationFunctionType
AX = mybir.AxisListType

LN_EPS = 1e-5
EPOCH = 12000
NRING = 12


class Cfg:
    def __init__(s, D=2048, SEQ=2048, CTX=256, W=768, H=4, DEPTH=2, NB=2, GRID_W=64):
        s.D, s.SEQ, s.CTX, s.W, s.H, s.DEPTH, s.NB, s.GRID_W = D, SEQ, CTX, W, H, DEPTH, NB, GRID_W
        s.T = SEQ + CTX
        s.DH = W // H
        s.DT = s.DH // 2
        s.DFF = 4 * D
        s.G = W // 16
        s.P = 64
        s.HID = 64
        s.FEAT = 33
        s.NIN = 8 * W + 4 * H + 3 * D
        s.o_hy, s.o_qk, s.o_v, s.o_o = 0, 3 * W, 5 * W, 6 * W
        s.o_gt = 7 * W
        s.o_u = 7 * W + 4 * H
        s.o_mg = 8 * W + 4 * H
        s.alpha = (2 * DEPTH) ** 0.25
        s.KD = D // 128
        s.KW = W // 128
        s.NCH = s.T // 128
        s.ROWS = SEQ // GRID_W


def tgroups(n0, n1, step=512):
    out = []
    a = n0
    while a < n1:
        b = min(a + step, n1)
        out.append((a, b))
        a = b
    return out


_UID = [0]


def sbt(nc, name, shape, dtype):
    _UID[0] += 1
    return nc.sbuf_tensor(f"{name}_u{_UID[0]}", shape, dtype)


class Eng:
    def __init__(s, name, h):
        s.name, s.h = name, h
        s.sems = []
        s.cnt = 0
        s.seen = {}
        s.ring = []
        s.dcount = 0


class KB:
    def __init__(s, nc, stack):
        s.nc, s.stack = nc, stack
        s.E = {}
        for name, h in (("pe", nc.tensor), ("dve", nc.vector), ("act", nc.scalar),
                        ("pool", nc.gpsimd), ("sp", nc.sync)):
            s.E[name] = Eng(name, h)
        s.nsem = 0
        s.last_w = {}
        s.readers = {}
        s.ninst = 0

    def newsem(s, name):
        s.nsem += 1
        return s.stack.enter_context(s.nc.semaphore(f"{name}_{s.nsem}"))

    def _wait(s, eng, tok):
        if tok is None:
            return
        sid, sem, val = tok
        if eng.seen.get(sid, 0) >= val:
            return
        eng.h.wait_ge(sem, val)
        eng.seen[sid] = val

    def _deps(s, eng, R, W):
        for key in R:
            s._wait(eng, s.last_w.get(key))
        for key in W:
            s._wait(eng, s.last_w.get(key))
            rd = s.readers.get(key)
            if rd:
                for t in rd.values():
                    s._wait(eng, t)

    def _record(s, tok, R, W):
        for key in W:
            s.last_w[key] = tok
            s.readers[key] = {}
        for key in R:
            d = s.readers.setdefault(key, {})
            d[tok[0]] = tok

    def op(s, en, fn, R=(), W=()):
        eng = s.E[en]
        s._deps(eng, R, W)
        ep = eng.cnt // EPOCH
        while len(eng.sems) <= ep:
            eng.sems.append(s.newsem(en))
            if len(eng.sems) > 1:
                eng.seen[(en, len(eng.sems) - 2)] = EPOCH
        sem = eng.sems[ep]
        ins = fn(eng.h)
        eng.cnt += 1
        val = eng.cnt - ep * EPOCH
        ins.then_inc(sem, 1)
        sid = (en, ep)
        if en == "pe":
            eng.seen[sid] = val
        s._record((sid, sem, val), R, W)
        s.ninst += 1

    def dma(s, qn, out, in_, R=(), W=(), **kw):
        q = s.E[qn]
        s._deps(q, R, W)
        if not q.ring:
            q.ring = [s.newsem(qn + "d") for _ in range(NRING)]
        i = q.dcount
        slot, gen = i % NRING, i // NRING
        sem = q.ring[slot]
        sid = (qn + "d", slot)
        if gen > 0:
            s._wait(q, (sid, sem, 16 * gen))
        ins = q.h.dma_start(out=out, in_=in_, **kw)
        ins.then_inc(sem, 16)
        q.dcount += 1
        s._record((sid, sem, 16 * (gen + 1)), R, W)
        s.ninst += 1

    def barrier(s):
        toks = []
        for e in s.E.values():
            if e.cnt > 0:
                ep = (e.cnt - 1) // EPOCH
                toks.append(((e.name, ep), e.sems[ep], e.cnt - ep * EPOCH))
            for slot, sem in enumerate(e.ring):
                n = (e.dcount - slot + NRING - 1) // NRING
                if n > 0:
                    toks.append(((e.name + "d", slot), sem, 16 * n))
        for e in s.E.values():
            for t in toks:
                s._wait(e, t)

    def scope(s):
        return Scope(s)

    def finish(s):
        sp = s.E["sp"]
        for q in s.E.values():
            for slot, sem in enumerate(q.ring):
                n = (q.dcount - slot + NRING - 1) // NRING
                if n > 0:
                    s._wait(sp, ((q.name + "d", slot), sem, 16 * n))


class Scope:
    def __init__(s, kb):
        s.kb = kb
        s.st = ExitStack()

    def __enter__(s):
        s.st.__enter__()
        return s.st

    def __exit__(s, *a):
        s.kb.barrier()
        return s.st.__exit__(*a)


class PS:
    def __init__(s, kb, stack):
        s.kb = kb
        s.banks = [stack.enter_context(kb.nc.psum_tensor(f"psb{i}", [128, 512], F32)) for i in range(8)]
        s.i = 0
        s.held = set()

    def get(s):
        while s.i in s.held:
            s.i = (s.i + 1) % 8
        i = s.i
        s.i = (s.i + 1) % 8
        return s.banks[i], ("ps", i)

    def hold(s):
        b, k = s.get()
        s.held.add(k[1])
        assert len(s.held) <= 6
        return b, k

    def release(s, k):
        s.held.discard(k[1])


class Pool:
    def __init__(s, kb, stack, name, n, shape, dtype):
        s.tiles = [stack.enter_context(sbt(kb.nc, f"{name}{i}", shape, dtype)) for i in range(n)]
        s.name, s.n, s.i = name, n, 0

    def get(s):
        i = s.i
        s.i = (s.i + 1) % s.n
        return s.tiles[i], (s.name, i)


def host_consts(c):
    TC = min(256, c.CTX)
    io = np.zeros((128, 2, TC), np.float32)
    io[:, 0, :] = np.arange(1, TC + 1, dtype=np.float32)[None]
    io[:, 1, :] = (TC - np.arange(TC, dtype=np.float32))[None]
    p = np.arange(128)
    cm = np.stack([((p // 16) % 2 == 0), ((p // 16) % 2 == 1)], 1).astype(np.float32)
    qm = np.stack([(p // 32 == q) for q in range(4)], 1).astype(np.float32)
    sidx = np.arange(128)[:, None]
    tidx = np.arange(128)[None, :]
    ng = np.stack([np.where(sidx <= tidx, 0.0, -30000.0), np.where(sidx >= tidx, 0.0, -30000.0)], 1).astype(np.float32)
    dm = np.stack([(p < c.H), (p >= c.H)], 1).astype(np.float32)
    out = {"ident": np.eye(128, dtype=np.float32), "ciota": io, "cmask": cm, "qmask": qm, "negmask": ng, "dirmask": dm}
    out.update(hy_consts(c.SEQ))
    out.update(hy_consts(c.CTX))
    return out


def build_program(cfg, phases=("A", "P", "S5", "ML", "HY", "M", "F"), debug_out=False):
    c = cfg
    nc = bass.Bass("TRN2", target_bir_lowering=False)
    D, T, W, NB, KD, NIN = c.D, c.T, c.W, c.NB, c.KD, c.NIN
    stack = ExitStack()
    kb = KB(nc, stack)

    def din(name, shape, dt=F32):
        return nc.dram_tensor(name, list(shape), dt, kind="ExternalInput").ap()

    def dscr(name, shape, dt=F32):
        if debug_out:
            return nc.dram_tensor(name, list(shape), dt, kind="ExternalOutput").ap()
        return nc.dram_tensor(name, list(shape), dt).ap()

    L = c.DEPTH
    I = {}
    I["xin"] = din("xin", [NB, T, D])
    I["cloc"] = din("cloc", [3, D])
    for nm, shp in (("w_mod", [L, D, 6 * D]), ("b_mod", [L, 6 * D]), ("w_in", [L, D, NIN]),
                    ("hy_conv_w", [L, 3, 3 * W]), ("hy_conv_b", [L, 3 * W]),
                    ("hy_ffn_w1", [L, c.FEAT, c.HID]), ("hy_ffn_b1", [L, c.HID]),
                    ("hy_ffn_w2", [L, c.HID, c.HID]), ("hy_ffn_b2", [L, c.HID]),
                    ("hy_ffn_w3", [L, c.HID, 4 * W]), ("hy_sin_freq", [L, c.HID]),
                    ("hy_decay", [L, 2, W]), ("hy_bias", [L, 2, W]),
                    ("ml_conv_w", [L, 3, 2 * W]), ("ml_conv_b", [L, 2 * W]),
                    ("ml_gate_b", [L, 4, c.H]), ("ml_norm_g", [L, W]),
                    ("s5_a_re", [L, 2, c.G, c.P]), ("s5_a_im", [L, 2, c.G, c.P]),
                    ("s5_log_dt", [L, 2, c.G]), ("s5_b_re", [L, 2, c.G, c.P, 16]),
                    ("s5_b_im", [L, 2, c.G, c.P, 16]), ("s5_c_re", [L, 2, c.G, 16, c.P]),
                    ("s5_c_im", [L, 2, c.G, 16, c.P]), ("s5_d", [L, W]),
                    ("s5_glu_w", [L, W, W]), ("s5_glu_b", [L, W]),
                    ("w_hy_out", [L, W, D]), ("w_ml_out", [L, W, D]), ("w_s5_out", [L, W, D]),
                    ("w_out", [L, D, D]), ("ln1_g", [L, D]), ("ln1_b", [L, D]),
                    ("ln2_g", [L, D]), ("ln2_b", [L, D]),
                    ("w_ff1", [L, D, c.DFF]), ("w_ff2", [L, c.DFF, D])):
        I[nm] = din(nm, shp)
    I["ident"] = din("ident", [128, 128])
    out = nc.dram_tensor("out", [NB, c.SEQ, D], F32, kind="ExternalOutput").ap()

    S = {}
    S["xres"] = dscr("xres", [NB, T, D])
    S["x1"] = dscr("x1", [NB, T, D])
    S["modrow"] = dscr("modrow", [L, 3, 6 * D])
    S["zT"] = dscr("zT", [NB, NIN, T])
    S["mlg"] = dscr("mlg", [NB, 2 * c.H, T])
    S["s5z"] = dscr("s5z", [NB, W, T], BF16)
    S["hyc"] = dscr("hyc", [NB, 3 * W, T])
    S["hyy1"] = dscr("hyy1", [NB, W, T])
    for L_ in (c.SEQ, c.CTX):
        NL_ = L_ // 128
        TG_ = min(512, L_)
        S[f"khat{L_}"] = dscr(f"khat{L_}", [L_, 2, 2, W])
        I[f"dftF{L_}"] = din(f"dftF{L_}", [NL_, 128, 2, NL_, 128], BF16)
        I[f"dftI{L_}"] = din(f"dftI{L_}", [L_ // TG_, 128, 2, NL_, TG_], BF16)
        I[f"hyfeat{L_}"] = din(f"hyfeat{L_}", [c.FEAT, L_])
        I[f"lagt{L_}"] = din(f"lagt{L_}", [128, NL_])
        I[f"lagmask{L_}"] = din(f"lagmask{L_}", [128, NL_])
    for nm in ("ys5T", "ymlT", "yhyT"):
        S[nm] = dscr(nm, [NB, W, T], BF16)

    ps = PS(kb, stack)
    ident = stack.enter_context(sbt(nc, "ident_sb", [128, 128], F32))
    identb = stack.enter_context(sbt(nc, "identb_sb", [128, 128], BF16))
    kb.dma("sp", ident[:], I["ident"], W=["ident"])
    kb.op("dve", lambda h: h.tensor_copy(out=identb[:], in_=ident[:]), R=["ident"], W=["identb"])
    modcol = stack.enter_context(sbt(nc, "modcol", [128, L * 6 * KD * 3], F32))
    modcol_v = modcol[:].rearrange("p (l m k r) -> p l m k r", l=L, m=6, k=KD, r=3)

    TCs = min(256, c.CTX)
    I["ciota"] = din("ciota", [128, 2, TCs])
    I["cmask"] = din("cmask", [128, 2])
    I["qmask"] = din("qmask", [128, 4])
    I["negmask"] = din("negmask", [128, 2, 128])
    I["dirmask"] = din("dirmask", [128, 2])
    negmask = stack.enter_context(sbt(nc, "negmask_sb", [128, 2, 128], F32))
    dirmask = stack.enter_context(sbt(nc, "dirmask_sb", [128, 2], F32))
    kb.dma("sp", negmask[:], I["negmask"], W=["negmask"])
    kb.dma("sp", dirmask[:], I["dirmask"], W=["dirmask"])
    qmask = stack.enter_context(sbt(nc, "qmask_sb", [128, 4], F32))
    kb.dma("sp", qmask[:], I["qmask"], W=["qmask"])
    ciota = stack.enter_context(sbt(nc, "ciota_sb", [128, 2, TCs], F32))
    cmask = stack.enter_context(sbt(nc, "cmask_sb", [128, 2], F32))
    kb.dma("sp", ciota[:], I["ciota"], W=["ciota"])
    kb.dma("sp", cmask[:], I["cmask"], W=["cmask"])
    ctxo = dict(c=c, nc=nc, kb=kb, ciota=ciota, cmask=cmask, qmask=qmask, negmask=negmask, dirmask=dirmask, I=I, S=S, ps=ps, ident=ident, identb=identb, modcol=modcol_v, out=out)

    for l in range(L):
        xsrc = I["xin"] if l == 0 else S["xres"]
        last = (l == L - 1)
        if "A" in phases:
            phase_A(ctxo, l)
        if "P" in phases:
            phase_P(ctxo, l, xsrc)
        if "S5" in phases:
            phase_S5(ctxo, l)
        if "ML" in phases:
            phase_ML(ctxo, l)
        if "HY" in phases:
            phase_HY(ctxo, l, last)
        if "M" in phases:
            phase_M(ctxo, l, xsrc)
        if "F" in phases:
            phase_F(ctxo, l, last)
    kb.finish()
    stack.close()
    return nc


def row_of(c, b, tok):
    return 2 if tok < c.CTX else b


def tok_tiles(c, step=512):
    return tgroups(0, c.CTX, step) + tgroups(c.CTX, c.T, step)


def phase_A(o, l):
    c, nc, kb, I, S, ps = o["c"], o["nc"], o["kb"], o["I"], o["S"], o["ps"]
    D, KD = c.D, c.KD
    with kb.scope() as st:
        crow = st.enter_context(sbt(nc, "A_crow", [3, D], F32))
        sig = st.enter_context(sbt(nc, "A_sig", [3, D], F32))
        scT = st.enter_context(sbt(nc, "A_scT", [128, KD, 3], F32))
        mrow = st.enter_context(sbt(nc, "A_mrow", [3, 6 * D], F32))
        brow = st.enter_context(sbt(nc, "A_brow", [3, 6 * D], F32))
        wp = Pool(kb, st, "A_w", 2, [128, KD, 512], F32)
        kb.dma("sp", crow[:], I["cloc"], W=["A_crow"])
        kb.dma("sp", brow[:], I["b_mod"][l:l + 1, :].broadcast_to([3, 6 * D]), W=["A_brow"])
        kb.op("act", lambda h: h.activation(out=sig[:], in_=crow[:], func=AF.Sigmoid), R=["A_crow"], W=["A_sig"])
        kb.op("dve", lambda h: h.tensor_tensor(out=sig[:], in0=sig[:], in1=crow[:], op=ALU.mult),
              R=["A_crow", "A_sig"], W=["A_sig"])
        pt, pk = ps.get()
        for k in range(KD):
            kb.op("pe", lambda h, k=k: h.transpose(out=pt[:, k * 3:(k + 1) * 3], in_=sig[:, k * 128:(k + 1) * 128],
                                                   identity=o["ident"][0:3, 0:3]), R=["A_sig", "ident"], W=[pk])
        kb.op("dve", lambda h: h.tensor_copy(out=scT[:].rearrange("p k r -> p (k r)"), in_=pt[:, 0:KD * 3]),
              R=[pk], W=["A_scT"])
        wv = I["w_mod"][l].rearrange("(k p) n -> p k n", p=128)
        for gi, (n0, n1) in enumerate(tgroups(0, 6 * D)):
            wt, wk = wp.get()
            kb.dma("sp", wt[:, :, 0:n1 - n0], wv[:, :, n0:n1], W=[wk])
            pt, pk = ps.get()
            for k in range(KD):
                kb.op("pe", lambda h, k=k: h.matmul(out=pt[0:3, 0:n1 - n0], lhsT=scT[:, k, :], rhs=wt[:, k, 0:n1 - n0],
                                                    start=(k == 0), stop=(k == KD - 1)), R=["A_scT", wk], W=[pk])
            kb.op("dve", lambda h: h.tensor_tensor(out=mrow[:, n0:n1], in0=pt[0:3, 0:n1 - n0], in1=brow[:, n0:n1],
                                                   op=ALU.add), R=[pk, "A_brow"], W=["A_mrow"])
        kb.dma("sp", S["modrow"][l], mrow[:], R=["A_mrow"], W=[("modrow", l)])
        nblk = 6 * KD
        mc = o["modcol"]
        for b0 in range(0, nblk, 128):
            b1 = min(nblk, b0 + 128)
            pt, pk = ps.get()
            for j in range(b0, b1):
                kb.op("pe", lambda h, j=j: h.transpose(out=pt[:, (j - b0) * 3:(j - b0 + 1) * 3],
                                                       in_=mrow[:, j * 128:(j + 1) * 128],
                                                       identity=o["ident"][0:3, 0:3]), R=["A_mrow", "ident"], W=[pk])
            m0, m1 = b0 // KD, b1 // KD
            kb.op("dve", lambda h: h.tensor_copy(out=mc[:, l, m0:m1].rearrange("p m k r -> p (m k r)"),
                                                 in_=pt[:, 0:(b1 - b0) * 3]), R=[pk], W=[("modcol", l)])
        for m in (1, 4):
            kb.op("dve", lambda h, m=m: h.tensor_scalar_add(out=mc[:, l, m].rearrange("p k r -> p (k r)"),
                                                            in0=mc[:, l, m].rearrange("p k r -> p (k r)"), scalar1=1.0),
                  R=[("modcol", l)], W=[("modcol", l)])


def build_hT(o, st, l, b, xsrc, mshift, mscale, name):
    c, nc, kb, ps = o["c"], o["nc"], o["kb"], o["ps"]
    D, KD, T = c.D, c.KD, c.T
    hT = st.enter_context(sbt(nc, name, [128, KD, T], BF16))
    with kb.scope() as st2:
        xp = Pool(kb, st2, name + "_x", 2, [128, 4, D], F32)
        for (t0, t1) in tok_tiles(c):
            nt = (t1 - t0) // 128
            r = row_of(c, b, t0)
            xt, xk = xp.get()
            kb.dma("sp", xt[:, 0:nt, :], xsrc[b, t0:t1, :].rearrange("(n p) d -> p n d", p=128),
                   R=[("xsrc", l, b, tt_) for tt_ in range(t0 // 128, t1 // 128)], W=[xk])
            for kd in range(KD):
                pt, pk = ps.get()
                for n in range(nt):
                    kb.op("pe", lambda h, n=n, kd=kd: h.transpose(out=pt[:, n * 128:(n + 1) * 128],
                                                                  in_=xt[:, n, kd * 128:(kd + 1) * 128],
                                                                  identity=o["ident"][:]), R=[xk, "ident"], W=[pk])
                kb.op("act", lambda h, kd=kd: h.activation(out=hT[:, kd, t0:t1], in_=pt[:, 0:t1 - t0], func=AF.Identity,
                                                           bias=o["modcol"][:, l, mshift, kd, r:r + 1],
                                                           scale=o["modcol"][:, l, mscale, kd, r:r + 1]),
                      R=[pk, ("modcol", l)], W=[(name, kd)])
    return hT


def phase_P(o, l, xsrc):
    c, nc, kb, I, S, ps = o["c"], o["nc"], o["kb"], o["I"], o["S"], o["ps"]
    D, KD, T, NIN = c.D, c.KD, c.T, c.NIN
    wv = I["w_in"][l].rearrange("(k p) n -> p k n", p=128)
    segs = [(0, c.o_gt), (c.o_gt, c.o_u), (c.o_u, NIN)]
    for b in range(c.NB):
        with kb.scope() as st:
            hT = build_hT(o, st, l, b, xsrc, 0, 1, "P_hT")
            wp = Pool(kb, st, "P_w", 3, [128, KD, 512], BF16)
            sp_ = Pool(kb, st, "P_stg", 3, [128, T], F32)
            for (s0, s1) in segs:
                for (g0, g1) in tgroups(s0, s1, 512):
                    wt, wk = wp.get()
                    kb.dma("pool", wt[:, :, 0:g1 - g0], wv[:, :, g0:g1], W=[wk])
                    for (c0, c1) in tgroups(g0, g1, 128):
                        m = c1 - c0
                        stg, sk = sp_.get()
                        for (t0, t1) in tgroups(0, T):
                            pt, pk = ps.get()
                            for k in range(KD):
                                kb.op("pe", lambda h, k=k: h.matmul(out=pt[0:m, 0:t1 - t0], lhsT=wt[:, k, c0 - g0:c1 - g0],
                                                                    rhs=hT[:, k, t0:t1], start=(k == 0), stop=(k == KD - 1)),
                                      R=[wk, ("P_hT", k)], W=[pk])
                            eng = "act" if ((t0 // 512) % 2 == 0) else "dve"
                            if eng == "act":
                                kb.op("act", lambda h: h.activation(out=stg[0:m, t0:t1], in_=pt[0:m, 0:t1 - t0], func=AF.Identity),
                                      R=[pk], W=[sk])
                            else:
                                kb.op("dve", lambda h: h.tensor_copy(out=stg[0:m, t0:t1], in_=pt[0:m, 0:t1 - t0]), R=[pk], W=[sk])
                        kb.dma("sp", S["zT"][b, c0:c1, :], stg[0:m, :], R=[sk], W=[("zT", b, c0)])


def ln_epilogue(o, st, name):
    c, nc, kb = o["c"], o["nc"], o["kb"]
    D = c.D
    nst = (D + 511) // 512
    stats = Pool(kb, st, name + "_st", 2, [128, nst, 6], F32)
    mv = Pool(kb, st, name + "_mv", 2, [128, 4], F32)

    def fn(rt, rkey, grow, gkey, brow, bkey, stores):
        stt, stk = stats.get()
        mvt, mvk = mv.get()
        for i in range(nst):
            kb.op("dve", lambda h, i=i: h.bn_stats(out=stt[:, i, :], in_=rt[:, i * 512:min(D, (i + 1) * 512)]), R=[rkey], W=[stk])
        kb.op("dve", lambda h: h.bn_aggr(out=mvt[:, 0:2], in_=stt[:].rearrange("p n s -> p (n s)")), R=[stk], W=[mvk])
        kb.op("dve", lambda h: h.tensor_scalar_add(out=mvt[:, 2:3], in0=mvt[:, 1:2], scalar1=LN_EPS), R=[mvk], W=[mvk])
        kb.op("act", lambda h: h.activation(out=mvt[:, 2:3], in_=mvt[:, 2:3], func=AF.Ln), R=[mvk], W=[mvk])
        kb.op("act", lambda h: h.activation(out=mvt[:, 2:3], in_=mvt[:, 2:3], func=AF.Exp, scale=-0.5), R=[mvk], W=[mvk])
        kb.op("dve", lambda h: h.scalar_tensor_tensor(out=mvt[:, 3:4], in0=mvt[:, 0:1], scalar=-1.0, in1=mvt[:, 2:3], op0=ALU.mult, op1=ALU.mult),
              R=[mvk], W=[mvk])
        kb.op("act", lambda h: h.activation(out=rt, in_=rt, func=AF.Identity, bias=mvt[:, 3:4], scale=mvt[:, 2:3]), R=[rkey, mvk], W=[rkey])
        kb.op("pool", lambda h: h.tensor_tensor(out=rt, in0=rt, in1=grow, op=ALU.mult), R=[rkey, gkey], W=[rkey])
        kb.op("dve", lambda h: h.tensor_tensor(out=rt, in0=rt, in1=brow, op=ALU.add), R=[rkey, bkey], W=[rkey])
        for (dst, dkey, src) in stores:
            kb.dma("sp", dst, src, R=[rkey], W=[dkey])
    return fn


def load_rows(o, st, name, srcs):
    nc, kb, c = o["nc"], o["kb"], o["c"]
    outl = []
    for i, (src, rkeys) in enumerate(srcs):
        t = st.enter_context(sbt(nc, f"{name}{i}", [128, c.D], F32))
        kb.dma("sp", t[:], src.broadcast_to([128, c.D]), R=rkeys, W=[(name, i)])
        outl.append((t, (name, i)))
    return outl


def phase_M(o, l, xsrc):
    c, nc, kb, I, S, ps = o["c"], o["nc"], o["kb"], o["I"], o["S"], o["ps"]
    D, KD, T, W, KW = c.D, c.KD, c.T, c.W, c.KW
    TM = 384
    last = (l == c.DEPTH - 1)
    brs = (("yhyT", "w_hy_out"), ("ymlT", "w_ml_out"), ("ys5T", "w_s5_out"))
    with kb.scope() as st:
        wbr = []
        for i, (yn, wn) in enumerate(brs):
            wt = st.enter_context(sbt(nc, f"M_wbr{i}", [128, KW, D], BF16))
            kb.dma("pool", wt[:], I[wn][l].rearrange("(k p) n -> p k n", p=128), W=[("M_wbr", i)])
            wbr.append(wt)
        wop = Pool(kb, st, "M_wo", 2, [128, KD, 512], BF16)
        wov = I["w_out"][l].rearrange("(k p) n -> p k n", p=128)
        rows = load_rows(o, st, "M_row", [(I["ln1_g"][l:l + 1, :], []), (I["ln1_b"][l:l + 1, :], [])])
        growp = Pool(kb, st, "M_grow", 1, [128, D], F32)
        cur_r = [None, None, None]
        ln = ln_epilogue(o, st, "M_ln")
        yp = Pool(kb, st, "M_y", 1, [128, 3, KW, TM], BF16)
        gp = Pool(kb, st, "M_g", 2, [128, 3, TM], F32)
        tp = Pool(kb, st, "M_t", 3, [128, 512], F32)
        mT = st.enter_context(sbt(nc, "M_mT", [128, KD, TM], BF16))
        xp = Pool(kb, st, "M_x", 1, [128, TM // 128, D], F32)
        for b in range(c.NB):
            for (t0, t1) in tok_tiles(c, TM):
                if last and t1 <= c.CTX:
                    continue
                n = t1 - t0
                nt = n // 128
                r = row_of(c, b, t0)
                if cur_r[0] != r:
                    gtile, gkey = growp.get()
                    kb.dma("sp", gtile[:], S["modrow"][l, r:r + 1, 2 * D:3 * D].broadcast_to([128, D]), R=[("modrow", l)], W=[gkey])
                    cur_r[0], cur_r[1], cur_r[2] = r, gtile, gkey
                yt, yk = yp.get()
                for i, (yn, wn) in enumerate(brs):
                    kb.dma("sp", yt[:, i, :, 0:n], S[yn][b, :, t0:t1].rearrange("(k p) t -> p k t", p=128),
                           R=[(yn, b, k_) for k_ in range(KW)], W=[yk])
                xt, xk = xp.get()
                kb.dma("sp", xt[:, 0:nt, :], xsrc[b, t0:t1, :].rearrange("(n p) d -> p n d", p=128), R=[("xsrc", l, b, tt_) for tt_ in range(t0 // 128, t1 // 128)], W=[xk])
                for kd in range(KD):
                    gt, gk = gp.get()
                    sgt, sgk = gt, gk
                    kb.dma("sp", gt[:, :, 0:n],
                           S["zT"][b, c.o_mg:c.o_mg + 3 * D, t0:t1].rearrange("(i k p) t -> k p i t", i=3, p=128)[kd],
                           R=[("zT", b, c.o_mg + i * D + kd * 128) for i in range(3)], W=[gk])
                    kb.op("act", lambda h: h.activation(out=gt[:, :, 0:n], in_=gt[:, :, 0:n], func=AF.Sigmoid), R=[gk], W=[gk])
                    tt, tk = tp.get()
                    for i in range(3):
                        pt, pk = ps.get()
                        for k in range(KW):
                            kb.op("pe", lambda h, i=i, k=k: h.matmul(out=pt[:, 0:n], lhsT=wbr[i][:, k, kd * 128:(kd + 1) * 128],
                                                                     rhs=yt[:, i, k, 0:n], start=(k == 0), stop=(k == KW - 1)),
                                  R=[("M_wbr", i), yk], W=[pk])
                        if i == 0:
                            kb.op("dve", lambda h: h.tensor_tensor(out=tt[:, 0:n], in0=pt[:, 0:n], in1=sgt[:, 0, 0:n], op=ALU.mult),
                                  R=[pk, sgk], W=[tk])
                        else:
                            kb.op("dve", lambda h, i=i: h.tensor_tensor(out=sgt[:, i, 0:n], in0=pt[:, 0:n], in1=sgt[:, i, 0:n], op=ALU.mult),
                                  R=[pk, sgk], W=[sgk])
                            if i == 1:
                                kb.op("pool", lambda h: h.tensor_tensor(out=tt[:, 0:n], in0=tt[:, 0:n], in1=sgt[:, 1, 0:n], op=ALU.add),
                                      R=[tk, sgk], W=[tk])
                            else:
                                kb.op("pool", lambda h, kd=kd: h.tensor_tensor(out=mT[:, kd, 0:n], in0=tt[:, 0:n], in1=sgt[:, 2, 0:n], op=ALU.add),
                                      R=[tk, sgk], W=[("M_mT", kd)])
                for gi, (d0, d1) in enumerate(tgroups(0, D)):
                    wt, wk = wop.get()
                    kb.dma("pool", wt[:, :, 0:d1 - d0], wov[:, :, d0:d1], W=[wk])
                    for ci in range(nt):
                        pt, pk = ps.get()
                        for k in range(KD):
                            kb.op("pe", lambda h, k=k, ci=ci: h.matmul(out=pt[:, 0:d1 - d0], lhsT=mT[:, k, ci * 128:(ci + 1) * 128],
                                                                       rhs=wt[:, k, 0:d1 - d0], start=(k == 0), stop=(k == KD - 1)),
                                  R=[("M_mT", k), wk], W=[pk])
                        tt, tk = tp.get()
                        kb.op("dve", lambda h: h.tensor_tensor(out=tt[:, 0:d1 - d0], in0=pt[:, 0:d1 - d0], in1=cur_r[1][:, d0:d1], op=ALU.mult),
                              R=[pk, cur_r[2]], W=[tk])
                        kb.op("dve", lambda h, ci=ci: h.scalar_tensor_tensor(out=xt[:, ci, d0:d1], in0=xt[:, ci, d0:d1], scalar=c.alpha,
                                                                             in1=tt[:, 0:d1 - d0], op0=ALU.mult, op1=ALU.add),
                              R=[xk, tk], W=[xk])
                for ci in range(nt):
                    ta = t0 + ci * 128
                    ln(xt[:, ci, :], xk, rows[0][0][:], rows[0][1], rows[1][0][:], rows[1][1],
                       [(S["x1"][b, ta:ta + 128, :], ("x1", b, ta // 128), xt[:, ci, :])])


def phase_F(o, l, last):
    c, nc, kb, I, S, ps = o["c"], o["nc"], o["kb"], o["I"], o["S"], o["ps"]
    D, KD, T, DFF = c.D, c.KD, c.T, c.DFF
    KF = DFF // 128
    w1v = I["w_ff1"][l].rearrange("(k p) n -> p k n", p=128)
    w2v = I["w_ff2"][l].rearrange("(k p) n -> p k n", p=128)
    with kb.scope() as st:
        rows = load_rows(o, st, "F_row", [(I["ln2_g"][l:l + 1, :], []), (I["ln2_b"][l:l + 1, :], [])])
        growp = Pool(kb, st, "F_grow", 1, [128, D], F32)
        cur_r = [None, None, None]
        ln = ln_epilogue(o, st, "F_ln")
        wp = Pool(kb, st, "F_w", 3, [128, 16, 512], BF16)
        xp = Pool(kb, st, "F_x", 1, [128, 4, D], F32)
        hT = st.enter_context(sbt(nc, "F_hT", [128, KD, 512], BF16))
        hidT = st.enter_context(sbt(nc, "F_hidT", [128, KF, 512], BF16))
        rp = Pool(kb, st, "F_r", 2, [128, 512], F32)
        tp = Pool(kb, st, "F_t", 2, [128, 512], F32)
        for b in range(c.NB):
            for (t0, t1) in tok_tiles(c):
                if last and t1 <= c.CTX:
                    continue
                n = t1 - t0
                nt = n // 128
                r = row_of(c, b, t0)
                if cur_r[0] != r:
                    gtile, gkey = growp.get()
                    kb.dma("sp", gtile[:], S["modrow"][l, r:r + 1, 5 * D:6 * D].broadcast_to([128, D]), R=[("modrow", l)], W=[gkey])
                    cur_r[0], cur_r[1], cur_r[2] = r, gtile, gkey
                xt, xk = xp.get()
                kb.dma("sp", xt[:, 0:nt, :], S["x1"][b, t0:t1, :].rearrange("(n p) d -> p n d", p=128), R=[("x1", b, tt_) for tt_ in range(t0 // 128, t1 // 128)], W=[xk])
                for kd in range(KD):
                    pt, pk = ps.get()
                    for ci in range(nt):
                        kb.op("pe", lambda h, ci=ci, kd=kd: h.transpose(out=pt[:, ci * 128:(ci + 1) * 128], in_=xt[:, ci, kd * 128:(kd + 1) * 128],
                                                                        identity=o["ident"][:]), R=[xk, "ident"], W=[pk])
                    kb.op("act", lambda h, kd=kd: h.activation(out=hT[:, kd, 0:n], in_=pt[:, 0:n], func=AF.Identity,
                                                               bias=o["modcol"][:, l, 3, kd, r:r + 1], scale=o["modcol"][:, l, 4, kd, r:r + 1]),
                          R=[pk, ("modcol", l)], W=[("F_hT", kd)])
                for (f0, f1) in tgroups(0, DFF, 512):
                    wt, wk = wp.get()
                    kb.dma("pool", wt[:, 0:KD, 0:f1 - f0], w1v[:, :, f0:f1], W=[wk])
                    for (c0, c1) in tgroups(f0, f1, 128):
                        pt, pk = ps.get()
                        for k in range(KD):
                            kb.op("pe", lambda h, k=k: h.matmul(out=pt[:, 0:n], lhsT=wt[:, k, c0 - f0:c1 - f0], rhs=hT[:, k, 0:n],
                                                                start=(k == 0), stop=(k == KD - 1)), R=[wk, ("F_hT", k)], W=[pk])
                        rt, rk = rp.get()
                        kb.op("act", lambda h: h.activation(out=rt[:, 0:n], in_=pt[:, 0:n], func=AF.Relu), R=[pk], W=[rk])
                        kb.op("dve", lambda h, c0=c0: h.tensor_tensor(out=hidT[:, c0 // 128, 0:n], in0=rt[:, 0:n], in1=rt[:, 0:n], op=ALU.mult),
                              R=[rk], W=[("F_hidT", c0 // 128)])
                for (d0, d1) in tgroups(0, D):
                    banks = [ps.get() for _ in range(nt)]
                    for q0 in range(0, KF, 16):
                        wt, wk = wp.get()
                        kq = min(16, KF - q0)
                        kb.dma("pool", wt[:, 0:kq, 0:d1 - d0], w2v[:, q0:q0 + kq, d0:d1], W=[wk])
                        for ci in range(nt):
                            pt, pk = banks[ci]
                            for k in range(kq):
                                kb.op("pe", lambda h, k=k, ci=ci: h.matmul(out=pt[:, 0:d1 - d0], lhsT=hidT[:, q0 + k, ci * 128:(ci + 1) * 128],
                                                                           rhs=wt[:, k, 0:d1 - d0], start=(q0 + k == 0), stop=(q0 + k == KF - 1)),
                                      R=[("F_hidT", q0 + k), wk], W=[pk])
                    for ci in range(nt):
                        pt, pk = banks[ci]
                        tt, tk = tp.get()
                        kb.op("dve", lambda h: h.tensor_tensor(out=tt[:, 0:d1 - d0], in0=pt[:, 0:d1 - d0], in1=cur_r[1][:, d0:d1], op=ALU.mult),
                              R=[pk, cur_r[2]], W=[tk])
                        kb.op("dve", lambda h, ci=ci: h.scalar_tensor_tensor(out=xt[:, ci, d0:d1], in0=xt[:, ci, d0:d1], scalar=c.alpha,
                                                                             in1=tt[:, 0:d1 - d0], op0=ALU.mult, op1=ALU.add),
                              R=[xk, tk], W=[xk])
                for ci in range(nt):
                    ta = t0 + ci * 128
                    if last:
                        stores = [(o["out"][b, ta - c.CTX:ta - c.CTX + 128, :], ("out", b), xt[:, ci, :])]
                    else:
                        stores = [(S["xres"][b, ta:ta + 128, :], ("xsrc", l + 1, b, ta // 128), xt[:, ci, :])]
                    ln(xt[:, ci, :], xk, rows[0][0][:], rows[0][1], rows[1][0][:], rows[1][1], stores)


def load_cols(o, dst, vec, key, n):
    o["kb"].dma("sp", dst, vec.rearrange("(k p) -> p k", p=128), W=[key], allow_slow_non_contiguous=True)


TWO_PI = 2.0 * math.pi


def range_reduce(kb, x, kf, ki, keys):
    R = list(keys)
    kb.op("dve", lambda h: h.tensor_scalar_mul(out=kf, in0=x, scalar1=1.0 / TWO_PI), R=R, W=R)
    kb.op("dve", lambda h: h.tensor_copy(out=ki, in_=kf), R=R, W=R)
    kb.op("dve", lambda h: h.tensor_copy(out=kf, in_=ki), R=R, W=R)
    kb.op("dve", lambda h: h.scalar_tensor_tensor(out=x, in0=kf, scalar=-TWO_PI, in1=x, op0=ALU.mult, op1=ALU.add), R=R, W=R)
    kb.op("dve", lambda h: h.tensor_scalar(out=kf, in0=x, scalar1=0.0, scalar2=TWO_PI, op0=ALU.is_lt, op1=ALU.mult), R=R, W=R)
    kb.op("dve", lambda h: h.scalar_tensor_tensor(out=x, in0=x, scalar=-math.pi, in1=kf, op0=ALU.add, op1=ALU.add), R=R, W=R)
    kb.op("dve", lambda h: h.tensor_scalar(out=x, in0=x, scalar1=-math.pi, scalar2=math.pi, op0=ALU.max, op1=ALU.min), R=R, W=R)


def phase_S5(o, l):
    c, nc, kb, I, S, ps = o["c"], o["nc"], o["kb"], o["I"], o["S"], o["ps"]
    T, W, KW, NB = c.T, c.W, c.KW, c.NB
    NT = c.G // 2
    TC = min(256, c.CTX)
    NC = T // TC
    NCC = c.CTX // TC
    n2 = 2 * NT
    ident = o["ident"]
    with kb.scope() as st:
        wk_ = st.enter_context(sbt(nc, "S_wk", [128, 12, n2], F32))
        BTz = st.enter_context(sbt(nc, "S_BTz", [128, 2, 2, KW, 4, 128], BF16))
        CTz = st.enter_context(sbt(nc, "S_CTz", [128, 2, 2, KW, 4, 128], BF16))
        rri = st.enter_context(sbt(nc, "S_rri", [128, 1, max(n2, 2 * min(256, c.CTX))], I32))
        rrf = st.enter_context(sbt(nc, "S_rrf", [128, 2 * min(256, c.CTX)], F32))
        spc = kb.scope()
        sp = spc.__enter__()
        ari = sp.enter_context(sbt(nc, "S_ari", [n2, 3, 128], F32))
        lds = sp.enter_context(sbt(nc, "S_lds", [n2, 2], F32))
        prm = sp.enter_context(sbt(nc, "S_prm", [128, 3, n2], F32))
        kb.dma("sp", ari[:, 0, :], I["s5_a_re"][l].rearrange("d (t g) p -> (d t) (g p)", g=2), W=["S_ari"])
        kb.dma("sp", ari[:, 1, :], I["s5_a_im"][l].rearrange("d (t g) p -> (d t) (g p)", g=2), W=["S_ari"])
        kb.dma("sp", lds[:], I["s5_log_dt"][l].rearrange("d (t g) -> (d t) g", g=2), W=["S_lds"])
        kb.op("dve", lambda h: h.tensor_copy(out=ari[:, 2, :].rearrange("r (g p) -> r g p", g=2),
                                             in_=lds[:].unsqueeze(2).broadcast_to([n2, 2, 64])), R=["S_lds", "S_ari"], W=["S_ari"])
        for i in range(3):
            pt, pk = ps.get()
            kb.op("pe", lambda h, i=i: h.transpose(out=pt[:, 0:n2], in_=ari[:, i, :], identity=ident[0:n2, 0:n2]), R=["S_ari", "ident"], W=[pk])
            kb.op("dve", lambda h, i=i: h.tensor_copy(out=prm[:, i, :], in_=pt[:, 0:n2]), R=[pk], W=["S_prm"])
        a_re, a_im, ld = prm[:, 0, :], prm[:, 1, :], prm[:, 2, :]
        dt_, mag, ang, thr, sn, cs, abm1, den, t1, t2, co_re, co_im = [wk_[:, i, :] for i in range(12)]
        K_ = "S_wk"

        def dv(fn, R=(K_, "S_prm"), W=(K_,), en="dve"):
            kb.op(en, fn, R=list(R), W=list(W))
        dv(lambda h: h.activation(out=dt_, in_=ld, func=AF.Exp), en="act")
        dv(lambda h: h.tensor_tensor(out=mag, in0=dt_, in1=a_re, op=ALU.mult))
        dv(lambda h: h.activation(out=mag, in_=mag, func=AF.Exp), en="act")
        dv(lambda h: h.tensor_tensor(out=ang, in0=dt_, in1=a_im, op=ALU.mult))
        dv(lambda h: h.tensor_scalar_add(out=thr, in0=ang, scalar1=math.pi + TWO_PI))
        range_reduce(kb, thr, t1, rri[:, 0, 0:n2], [K_, "S_rri"])
        dv(lambda h: h.activation(out=sn, in_=thr, func=AF.Sin), en="act")
        dv(lambda h: h.tensor_scalar_add(out=cs, in0=ang, scalar1=1.5 * math.pi + TWO_PI))
        range_reduce(kb, cs, t1, rri[:, 0, 0:n2], [K_, "S_rri"])
        dv(lambda h: h.activation(out=cs, in_=cs, func=AF.Sin), en="act")
        dv(lambda h: h.tensor_tensor(out=cs, in0=cs, in1=mag, op=ALU.mult))
        dv(lambda h: h.tensor_tensor(out=sn, in0=sn, in1=mag, op=ALU.mult))
        dv(lambda h: h.tensor_scalar_add(out=abm1, in0=cs, scalar1=-1.0))
        dv(lambda h: h.tensor_tensor(out=den, in0=a_re, in1=a_re, op=ALU.mult))
        dv(lambda h: h.tensor_tensor(out=t1, in0=a_im, in1=a_im, op=ALU.mult))
        dv(lambda h: h.tensor_tensor(out=den, in0=den, in1=t1, op=ALU.add))
        dv(lambda h: h.reciprocal(out=den, in_=den))
        dv(lambda h: h.tensor_tensor(out=t1, in0=abm1, in1=a_re, op=ALU.mult))
        dv(lambda h: h.tensor_tensor(out=t2, in0=sn, in1=a_im, op=ALU.mult))
        dv(lambda h: h.tensor_tensor(out=t1, in0=t1, in1=t2, op=ALU.add))
        dv(lambda h: h.tensor_tensor(out=co_re, in0=t1, in1=den, op=ALU.mult))
        dv(lambda h: h.tensor_tensor(out=t1, in0=sn, in1=a_re, op=ALU.mult))
        dv(lambda h: h.tensor_tensor(out=t2, in0=abm1, in1=a_im, op=ALU.mult))
        dv(lambda h: h.tensor_tensor(out=t1, in0=t1, in1=t2, op=ALU.subtract))
        dv(lambda h: h.tensor_tensor(out=co_im, in0=t1, in1=den, op=ALU.mult))
        bnat = sp.enter_context(sbt(nc, "S_bnat", [128, 2, n2, 16], F32))
        bb = sp.enter_context(sbt(nc, "S_bb", [128, 4, n2, 16], F32))
        bblk = sp.enter_context(sbt(nc, "S_bblk", [128, 2, 2, KW, 128], F32))
        BT = sp.enter_context(sbt(nc, "S_BT", [128, 2, 2, KW, 128], BF16))
        CT = sp.enter_context(sbt(nc, "S_CT", [128, 2, 2, KW, 128], BF16))
        for d_ in range(2):
            kb.dma("sp", bnat[:, 0, d_ * NT:(d_ + 1) * NT, :], I["s5_b_re"][l, d_].rearrange("(t g) p c -> (g p) t c", g=2), W=["S_bnat"])
            kb.dma("sp", bnat[:, 1, d_ * NT:(d_ + 1) * NT, :], I["s5_b_im"][l, d_].rearrange("(t g) p c -> (g p) t c", g=2), W=["S_bnat"])
        cor = co_re.unsqueeze(2).broadcast_to([128, n2, 16])
        coi = co_im.unsqueeze(2).broadcast_to([128, n2, 16])
        RB = ["S_bnat", K_, "S_bb"]
        kb.op("dve", lambda h: h.tensor_tensor(out=bb[:, 0], in0=bnat[:, 0], in1=cor, op=ALU.mult), R=RB, W=["S_bb"])
        kb.op("dve", lambda h: h.tensor_tensor(out=bb[:, 1], in0=bnat[:, 1], in1=coi, op=ALU.mult), R=RB, W=["S_bb"])
        kb.op("dve", lambda h: h.tensor_tensor(out=bb[:, 0], in0=bb[:, 0], in1=bb[:, 1], op=ALU.subtract), R=RB, W=["S_bb"])
        kb.op("dve", lambda h: h.tensor_tensor(out=bb[:, 2], in0=bnat[:, 1], in1=cor, op=ALU.mult), R=RB, W=["S_bb"])
        kb.op("dve", lambda h: h.tensor_tensor(out=bb[:, 3], in0=bnat[:, 0], in1=coi, op=ALU.mult), R=RB, W=["S_bb"])
        kb.op("dve", lambda h: h.tensor_tensor(out=bb[:, 2], in0=bb[:, 2], in1=bb[:, 3], op=ALU.add), R=RB, W=["S_bb"])
        kb.op("pool", lambda h: h.memset(bblk[:], 0.0), W=["S_bblk"])
        for ri, src in ((0, 0), (1, 2)):
            for g in range(2):
                pr = slice(64 * g, 64 * g + 64)
                kb.op("dve", lambda h, ri=ri, src=src, g=g, pr=pr: h.tensor_copy(
                    out=bblk[pr, ri].rearrange("p d k (q g c) -> p d k q g c", q=4, g=2)[:, :, :, :, g, :],
                    in_=bb[pr, src].rearrange("p (d k q) c -> p d k q c", d=2, q=4)), R=["S_bb", "S_bblk"], W=["S_bblk"])
        for ri in range(2):
            for d in range(2):
                for k in range(KW):
                    pt, pk = ps.get()
                    kb.op("pe", lambda h, ri=ri, d=d, k=k: h.transpose(out=pt[:, 0:128], in_=bblk[:, ri, d, k, :], identity=ident[:]),
                          R=["S_bblk", "ident"], W=[pk])
                    kb.op("act", lambda h, ri=ri, d=d, k=k: h.activation(out=BT[:, ri, d, k, :], in_=pt[:, 0:128], func=AF.Identity), R=[pk], W=["S_BT"])
        cnat = sp.enter_context(sbt(nc, "S_cnat", [128, 2, 2, KW, 64], F32))
        kb.dma("sp", cnat[:, 0], I["s5_c_re"][l].rearrange("d (k g) c p -> (g c) d k p", g=8), W=["S_cnat"])
        kb.dma("sp", cnat[:, 1], I["s5_c_im"][l].rearrange("d (k g) c p -> (g c) d k p", g=8), W=["S_cnat"])
        cmask = o["cmask"]
        for ri in range(2):
            for g in range(2):
                kb.op("dve", lambda h, ri=ri, g=g: h.tensor_scalar(
                    out=bblk[:, ri].rearrange("p d k (g m) -> p d k g m", g=2)[:, :, :, g, :],
                    in0=cnat[:, ri], scalar1=cmask[:, g:g + 1], scalar2=None, op0=ALU.mult), R=["S_cnat", "S_bblk", "cmask"], W=["S_bblk"])
        for ri in range(2):
            for d in range(2):
                for k in range(KW):
                    pt, pk = ps.get()
                    kb.op("pe", lambda h, ri=ri, d=d, k=k: h.transpose(out=pt[:, 0:128], in_=bblk[:, ri, d, k, :], identity=ident[:]),
                          R=["S_bblk", "ident"], W=[pk])
                    kb.op("act", lambda h, ri=ri, d=d, k=k: h.activation(out=CT[:, ri, d, k, :], in_=pt[:, 0:128], func=AF.Identity), R=[pk], W=["S_CT"])
        kb.op("pool", lambda h: h.memset(CTz[:].rearrange("p a b k q m -> p (a b k q m)"), 0.0), W=["S_CTz"])
        for q in range(4):
            for ri in range(2):
                kb.op("dve", lambda h, q=q, ri=ri: h.tensor_scalar(out=BTz[:, ri, :, :, q, :], in0=BT[:, ri], scalar1=o["qmask"][:, q:q + 1], scalar2=None, op0=ALU.mult),
                      R=["S_BT", "qmask", "S_BTz"], W=["S_BTz"])
                kb.op("dve", lambda h, q=q, ri=ri: h.tensor_copy(out=CTz[:, ri, :, :, q, 32 * q:32 * q + 32], in_=CT[:, ri, :, :, 32 * q:32 * q + 32]),
                      R=["S_CT", "S_CTz"], W=["S_CTz"])
        spc.__exit__(None, None, None)
        dcol = st.enter_context(sbt(nc, "S_dcol", [128, KW], F32))
        gbcol = st.enter_context(sbt(nc, "S_gbcol", [128, KW], F32))
        load_cols(o, dcol[:], I["s5_d"][l], "S_dcol", KW)
        load_cols(o, gbcol[:], I["s5_glu_b"][l], "S_gbcol", KW)
        gw = st.enter_context(sbt(nc, "S_gw", [128, KW, W], BF16))
        kb.dma("pool", gw[:], I["s5_glu_w"][l].rearrange("(k p) n -> p k n", p=128), W=["S_gw"])
        smc = kb.scope()
        sm = smc.__enter__()
        uT = [sm.enter_context(sbt(nc, f"S_u{b}", [128, T], BF16)) for b in range(NB)]
        yacc = [sm.enter_context(sbt(nc, f"S_y{b}", [128, T], F32)) for b in range(NB)]
        wre = [sm.enter_context(sbt(nc, f"S_wre{b}", [128, T], F32)) for b in range(NB)]
        wim = [sm.enter_context(sbt(nc, f"S_wim{b}", [128, T], F32)) for b in range(NB)]
        zop = Pool(kb, sm, "S_zo", 2, [128, T], BF16)
        tmpp = Pool(kb, sm, "S_tmp", 8, [128, 512], F32)
        pr_ = [sm.enter_context(sbt(nc, f"S_p{i}", [128, T], BF16)) for i in range(4)]
        tab = sm.enter_context(sbt(nc, "S_tab", [128, 2, TC], F32))
        tabs = sm.enter_context(sbt(nc, "S_tabs", [128, 2, TC], F32))
        xst = [sm.enter_context(sbt(nc, f"S_xst{b}", [128, 4], F32)) for b in range(NB)]
        ebf = sm.enter_context(sbt(nc, "S_ebf", [128, 3, T], BF16))
        zbf = sm.enter_context(sbt(nc, "S_zbf", [128, 2, T], BF16))
        iota = o["ciota"]
        for kc in range(KW):
            for b in range(NB):
                kb.dma("pool", uT[b][:], S["zT"][b, c.o_u + kc * 128:c.o_u + (kc + 1) * 128, :], R=[("zT", b, c.o_u + kc * 128)], W=[("S_u", b)])
            for d in range(2):
                for q in range(4):
                    ti = kc * 4 + q
                    col = d * NT + ti
                    th = thr[:, col:col + 1]
                    kb.op("dve", lambda h, d=d, th=th: h.tensor_scalar(out=tabs[:, 0, :], in0=iota[:, d, :], scalar1=th, scalar2=1.5 * math.pi + 130 * TWO_PI,
                                                                       op0=ALU.mult, op1=ALU.add), R=[K_, "ciota", "S_tabs"], W=["S_tabs"])
                    kb.op("dve", lambda h, d=d, th=th: h.tensor_scalar(out=tabs[:, 1, :], in0=iota[:, d, :], scalar1=th, scalar2=-1.0,
                                                                       op0=ALU.mult, op1=ALU.mult), R=[K_, "ciota", "S_tabs"], W=["S_tabs"])
                    kb.op("dve", lambda h: h.tensor_scalar_add(out=tabs[:, 1, :], in0=tabs[:, 1, :], scalar1=math.pi + 130 * TWO_PI), R=["S_tabs"], W=["S_tabs"])
                    range_reduce(kb, tabs[:].rearrange("p a t -> p (a t)"), rrf[:, 0:2 * TC], rri[:, 0, 0:2 * TC], ["S_tabs", "S_rri", "S_rrf"])
                    kb.op("act", lambda h: h.activation(out=tab[:], in_=tabs[:], func=AF.Sin), R=["S_tabs", "S_tab"], W=["S_tab"])
                    Er, Ei = tab[:, 0, :], tab[:, 1, :]
                    for i_, (src_, sc_) in enumerate(((Er, 1.0), (Ei, 1.0), (Er, -1.0))):
                        kb.op("act", lambda h, i_=i_, src_=src_, sc_=sc_: h.activation(out=ebf[:, i_, :].rearrange("p (n t) -> p n t", t=TC),
                                                                                       in_=src_.unsqueeze(1).broadcast_to([128, NC, TC]), func=AF.Identity, scale=sc_),
                              R=["S_tab", "S_ebf"], W=["S_ebf"])
                    lastc = TC - 1 if d == 0 else 0
                    rho = mag[:, col:col + 1]
                    if d == 0:
                        order = list(range(NC))
                    else:
                        order = list(range(NCC - 1, -1, -1)) + list(range(NC - 1, NCC - 1, -1))
                    rhob = rho.broadcast_to([128, TC])
                    v3 = lambda ap: ap.rearrange("p (n t) -> p n t", t=TC)

                    def stage_w(b):
                        kwr, kwi = ("S_wre", b), ("S_wim", b)
                        for (g0, g1) in tgroups(0, T, 1024):
                            n = g1 - g0
                            pre = [ps.get() for _ in range((n + 511) // 512)]
                            pim = [ps.get() for _ in range((n + 511) // 512)]
                            for ri, banks in ((0, pre), (1, pim)):
                                for j, (pt, pk) in enumerate(banks):
                                    a0 = g0 + j * 512
                                    a1 = min(a0 + 512, g1)
                                    kb.op("pe", lambda h, ri=ri, pt=pt, a0=a0, a1=a1: h.matmul(
                                        out=pt[:, 0:a1 - a0], lhsT=BTz[:, ri, d, kc, q, :], rhs=uT[b][:, a0:a1],
                                        start=True, stop=True), R=["S_BTz", ("S_u", b)], W=[pk])
                            for j in range(len(pre)):
                                a0 = g0 + j * 512
                                a1 = min(a0 + 512, g1)
                                m = a1 - a0
                                nk = m // TC
                                (pr0, pk0), (pi0, pk1) = pre[j], pim[j]
                                Erb = Er.unsqueeze(1).broadcast_to([128, nk, TC])
                                Eib = Ei.unsqueeze(1).broadcast_to([128, nk, TC])
                                (ta, tak), (tb, tbk) = tmpp.get(), tmpp.get()
                                (br, brk), (bi, bik) = tmpp.get(), tmpp.get()
                                ta_, tb_, br_, bi_ = ta[:, 0:m], tb[:, 0:m], br[:, 0:m], bi[:, 0:m]
                                kb.op("act", lambda h: h.activation(out=br_, in_=pr0[:, 0:m], func=AF.Identity), R=[pk0, brk], W=[brk])
                                kb.op("act", lambda h: h.activation(out=bi_, in_=pi0[:, 0:m], func=AF.Identity), R=[pk1, bik], W=[bik])
                                kb.op("dve", lambda h: h.tensor_tensor(out=v3(ta_), in0=v3(bi_), in1=Eib, op=ALU.mult), R=[bik, "S_tab", tak], W=[tak])
                                kb.op("dve", lambda h: h.tensor_tensor(out=v3(wre[b][:, a0:a1]), in0=v3(br_), in1=Erb, op=ALU.mult), R=[brk, "S_tab", kwr], W=[kwr])
                                kb.op("dve", lambda h: h.tensor_tensor(out=wre[b][:, a0:a1], in0=wre[b][:, a0:a1], in1=ta_, op=ALU.subtract), R=[kwr, tak], W=[kwr])
                                kb.op("dve", lambda h: h.tensor_tensor(out=v3(tb_), in0=v3(br_), in1=Eib, op=ALU.mult), R=[brk, "S_tab", tbk], W=[tbk])
                                kb.op("dve", lambda h: h.tensor_tensor(out=v3(wim[b][:, a0:a1]), in0=v3(bi_), in1=Erb, op=ALU.mult), R=[bik, "S_tab", kwi], W=[kwi])
                                kb.op("dve", lambda h: h.tensor_tensor(out=wim[b][:, a0:a1], in0=wim[b][:, a0:a1], in1=tb_, op=ALU.add), R=[kwi, tbk], W=[kwi])

                    def stage_scan(b, oi, ch):
                        kwr, kwi, kx = ("S_wre", b), ("S_wim", b), ("S_xst", b)
                        xs_ = xst[b]
                        sl = slice(ch * TC, (ch + 1) * TC)
                        zr, zi = wre[b][:, sl], wim[b][:, sl]
                        if d == 1:
                            zr, zi = zr[:, ::-1], zi[:, ::-1]
                        if oi == 0:
                            ir, ii = 0.0, 0.0
                        else:
                            ir, ii = xs_[:, 0:1], xs_[:, 1:2]
                        kb.op("dve", lambda h: h.tensor_tensor_scan(out=zr, data0=rhob, data1=zr, initial=ir, op0=ALU.mult, op1=ALU.add), R=[kwr, K_, kx], W=[kwr])
                        kb.op("dve", lambda h: h.tensor_tensor_scan(out=zi, data0=rhob, data1=zi, initial=ii, op0=ALU.mult, op1=ALU.add), R=[kwi, K_, kx], W=[kwi])
                        if oi < len(order) - 1:
                            pos = ch * TC + (TC - 1 if d == 0 else 0)
                            zrl, zil = wre[b][:, pos:pos + 1], wim[b][:, pos:pos + 1]
                            Erl, Eil = Er[:, lastc:lastc + 1], Ei[:, lastc:lastc + 1]
                            kb.op("dve", lambda h: h.tensor_tensor(out=xs_[:, 2:3], in0=zil, in1=Eil, op=ALU.mult), R=[kwi, "S_tab", kx], W=[kx])
                            kb.op("dve", lambda h: h.tensor_tensor(out=xs_[:, 3:4], in0=zrl, in1=Eil, op=ALU.mult), R=[kwr, "S_tab", kx], W=[kx])
                            kb.op("dve", lambda h: h.scalar_tensor_tensor(out=xs_[:, 0:1], in0=zrl, scalar=Erl, in1=xs_[:, 2:3], op0=ALU.mult, op1=ALU.add), R=[kwr, "S_tab", kx], W=[kx])
                            kb.op("dve", lambda h: h.scalar_tensor_tensor(out=xs_[:, 1:2], in0=zil, scalar=Erl, in1=xs_[:, 3:4], op0=ALU.mult, op1=ALU.subtract), R=[kwi, "S_tab", kx], W=[kx])

                    def stage_out(b):
                        kwr, kwi = ("S_wre", b), ("S_wim", b)
                        kb.op("act", lambda h: h.activation(out=zbf[:, 0, :], in_=wre[b][:], func=AF.Identity), R=[kwr, "S_zbf"], W=["S_zbf"])
                        kb.op("act", lambda h: h.activation(out=zbf[:, 1, :], in_=wim[b][:], func=AF.Identity), R=[kwi, "S_zbf"], W=["S_zbf"])
                        for i_, (zi_, ei_) in enumerate(((0, 0), (1, 1), (1, 2), (0, 1))):
                            kb.op("dve", lambda h, i_=i_, zi_=zi_, ei_=ei_: h.tensor_tensor(out=pr_[i_][:], in0=zbf[:, zi_, :], in1=ebf[:, ei_, :], op=ALU.mult),
                                  R=["S_zbf", "S_ebf", ("S_p", i_)], W=[("S_p", i_)])
                        for (a0, a1) in tgroups(0, T):
                            pt, pk = ps.get()
                            for i in range(4):
                                ri = 0 if i < 2 else 1
                                kb.op("pe", lambda h, i=i, ri=ri: h.matmul(out=pt[:, 0:a1 - a0], lhsT=CTz[:, ri, d, kc, q, :], rhs=pr_[i][:, a0:a1],
                                                                           start=(i == 0), stop=(i == 3)), R=["S_CTz", ("S_p", i)], W=[pk])
                            if d == 0 and q == 0:
                                kb.op("act", lambda h: h.activation(out=yacc[b][:, a0:a1], in_=pt[:, 0:a1 - a0], func=AF.Identity), R=[pk, ("S_y", b)], W=[("S_y", b)])
                            else:
                                kb.op("dve", lambda h: h.tensor_tensor(out=yacc[b][:, a0:a1], in0=yacc[b][:, a0:a1], in1=pt[:, 0:a1 - a0], op=ALU.add),
                                      R=[pk, ("S_y", b)], W=[("S_y", b)])

                    for b in range(NB):
                        stage_w(b)
                    for oi, ch in enumerate(order):
                        for b in range(NB):
                            stage_scan(b, oi, ch)
                    for b in range(NB):
                        stage_out(b)
            for b in range(NB):
                ya = yacc[b][:]
                kw_ = ("S_wre", b)
                kb.op("dve", lambda h: h.scalar_tensor_tensor(out=ya, in0=uT[b][:], scalar=dcol[:, kc:kc + 1], in1=ya, op0=ALU.mult, op1=ALU.add),
                      R=[("S_u", b), ("S_y", b), "S_dcol"], W=[("S_y", b)])
                kb.op("dve", lambda h: h.tensor_tensor(out=wre[b][:], in0=ya, in1=ya, op=ALU.mult), R=[("S_y", b), kw_], W=[kw_])
                kb.op("dve", lambda h: h.tensor_scalar(out=wre[b][:], in0=wre[b][:], scalar1=0.044715, scalar2=1.0, op0=ALU.mult, op1=ALU.add), R=[kw_], W=[kw_])
                kb.op("dve", lambda h: h.tensor_tensor(out=wre[b][:], in0=wre[b][:], in1=ya, op=ALU.mult), R=[("S_y", b), kw_], W=[kw_])
                kb.op("act", lambda h: h.activation(out=wre[b][:], in_=wre[b][:], func=AF.Sigmoid, scale=2.0 * math.sqrt(2.0 / math.pi)), R=[kw_], W=[kw_])
                zo, zok = zop.get()
                kb.op("dve", lambda h: h.tensor_tensor(out=zo[:], in0=wre[b][:], in1=ya, op=ALU.mult), R=[("S_y", b), kw_, zok], W=[zok])
                kb.dma("sp", S["s5z"][b, kc * 128:(kc + 1) * 128, :], zo[:], R=[zok], W=[("s5z", b, kc)])
        smc.__exit__(None, None, None)
        zTt = [[st.enter_context(sbt(nc, f"S_z{b}_{k}", [128, T], BF16)) for k in range(KW)] for b in range(NB)]
        for b in range(NB):
            for k in range(KW):
                kb.dma("sp", zTt[b][k][:], S["s5z"][b, k * 128:(k + 1) * 128, :], R=[("s5z", b, k)], W=[("S_z", b, k)])
        op_ = Pool(kb, st, "S_o", 2, [128, T], BF16)
        sg_ = Pool(kb, st, "S_sg", 2, [128, 512], F32)
        for b in range(NB):
            for co in range(KW):
                ot, ok = op_.get()
                for (a0, a1) in tgroups(0, T):
                    pt, pk = ps.get()
                    for k in range(KW):
                        kb.op("pe", lambda h, k=k: h.matmul(out=pt[:, 0:a1 - a0], lhsT=gw[:, k, co * 128:(co + 1) * 128], rhs=zTt[b][k][:, a0:a1],
                                                            start=(k == 0), stop=(k == KW - 1)), R=["S_gw", ("S_z", b, k)], W=[pk])
                    sgt, sgk = sg_.get()
                    kb.op("act", lambda h: h.activation(out=sgt[:, 0:a1 - a0], in_=pt[:, 0:a1 - a0], func=AF.Sigmoid, bias=gbcol[:, co:co + 1]), R=[pk, "S_gbcol"], W=[sgk])
                    kb.op("dve", lambda h: h.tensor_tensor(out=ot[:, a0:a1], in0=sgt[:, 0:a1 - a0], in1=zTt[b][co][:, a0:a1], op=ALU.mult), R=[sgk, ("S_z", b, co)], W=[ok])
                kb.dma("sp", S["ys5T"][b, co * 128:(co + 1) * 128, :], ot[:], R=[ok], W=[("ys5T", b, co)])


def conv3(o, y, x, wcol, bcol, np_, rkeys, ykey):
    c, kb = o["c"], o["kb"]
    CTX, T, GW = c.CTX, c.T, c.GRID_W
    kb.op("act", lambda h: h.activation(out=y[0:np_, :], in_=x[0:np_, :], func=AF.Identity, bias=bcol, scale=wcol[1]), R=rkeys + [ykey], W=[ykey])
    lat = lambda ap: ap[0:np_, CTX:T].rearrange("p (r w) -> p r w", w=GW)
    for (tap, dst0, src0) in ((0, 1, 0), (2, 0, 1)):
        wc = wcol[tap]
        kb.op("dve", lambda h: h.scalar_tensor_tensor(out=y[0:np_, dst0:dst0 + CTX - 1], in0=x[0:np_, src0:src0 + CTX - 1], scalar=wc,
                                                      in1=y[0:np_, dst0:dst0 + CTX - 1], op0=ALU.mult, op1=ALU.add), R=rkeys + [ykey], W=[ykey])
        kb.op("dve", lambda h: h.scalar_tensor_tensor(out=lat(y)[:, :, dst0:dst0 + GW - 1], in0=lat(x)[:, :, src0:src0 + GW - 1], scalar=wc,
                                                      in1=lat(y)[:, :, dst0:dst0 + GW - 1], op0=ALU.mult, op1=ALU.add), R=rkeys + [ykey], W=[ykey])


def phase_ML(o, l):
    c, nc, kb, I, S, ps = o["c"], o["nc"], o["kb"], o["I"], o["S"], o["ps"]
    T, W, KW, NB, H, DH, DT, NCH, CTX = c.T, c.W, c.KW, c.NB, c.H, c.DH, c.DT, c.NCH, c.CTX
    NCC = CTX // 128
    H2 = 2 * H
    ident, identb = o["ident"], o["identb"]
    DV = DH + 1
    with kb.scope() as st:
        cw = st.enter_context(sbt(nc, "ML_cw", [DT, 3, 4 * H], F32))
        cb = st.enter_context(sbt(nc, "ML_cb", [DT, 4 * H], F32))
        for t_ in range(3):
            kb.dma("sp", cw[:, t_, :], I["ml_conv_w"][l, t_, :].rearrange("(m p) -> p m", p=DT), W=["ML_cw"], allow_slow_non_contiguous=True)
        kb.dma("sp", cb[:], I["ml_conv_b"][l].rearrange("(m p) -> p m", p=DT), W=["ML_cb"], allow_slow_non_contiguous=True)
        ngcol = st.enter_context(sbt(nc, "ML_ng", [128, KW], F32))
        load_cols(o, ngcol[:], I["ml_norm_g"][l], "ML_ng", KW)
        gbi = st.enter_context(sbt(nc, "ML_gbi", [H2, 1], F32))
        gbf = st.enter_context(sbt(nc, "ML_gbf", [H2, 1], F32))
        for d in range(2):
            kb.dma("sp", gbi[d * H:(d + 1) * H, :], I["ml_gate_b"][l, 2 * d, :].rearrange("(h o) -> h o", o=1), W=["ML_gb"])
            kb.dma("sp", gbf[d * H:(d + 1) * H, :], I["ml_gate_b"][l, 2 * d + 1, :].rearrange("(h o) -> h o", o=1), W=["ML_gb"])
        negm = o["negmask"]
        dirm = o["dirmask"]
        for b in range(NB):
            with kb.scope() as sb_:
                colA = sb_.enter_context(sbt(nc, "ML_colA", [128, NCH, H2], F32))
                expm = sb_.enter_context(sbt(nc, "ML_expm", [128, NCH, H2], F32))
                vtok = sb_.enter_context(sbt(nc, "ML_vtok", [128, NCH, H, DV], BF16))
                hn = sb_.enter_context(sbt(nc, "ML_hn", [128, NCH, W], BF16))
                with kb.scope() as sg:
                    Gi = sg.enter_context(sbt(nc, "ML_Gi", [H2, T], F32))
                    Gf = sg.enter_context(sbt(nc, "ML_Gf", [H2, T], F32))
                    Fn = sg.enter_context(sbt(nc, "ML_Fn", [H2, T], F32))
                    Fr = sg.enter_context(sbt(nc, "ML_Fr", [H2, T], F32))
                    ones = sg.enter_context(sbt(nc, "ML_ones", [H2, T], F32))
                    for d in range(2):
                        kb.dma("sp", Gi[d * H:(d + 1) * H, :], S["zT"][b, c.o_gt + 2 * d * H:c.o_gt + (2 * d + 1) * H, :], R=[("zT", b, c.o_gt)], W=["ML_Gi"])
                        kb.dma("sp", Gf[d * H:(d + 1) * H, :], S["zT"][b, c.o_gt + (2 * d + 1) * H:c.o_gt + (2 * d + 2) * H, :], R=[("zT", b, c.o_gt)], W=["ML_Gf"])
                    kb.op("pool", lambda h: h.memset(ones[:], 1.0), W=["ML_ones"])
                    kb.op("dve", lambda h: h.tensor_scalar_add(out=Gi[:], in0=Gi[:], scalar1=gbi[:, 0:1]), R=["ML_Gi", "ML_gb"], W=["ML_Gi"])
                    kb.op("dve", lambda h: h.tensor_scalar(out=Gf[:], in0=Gf[:], scalar1=gbf[:, 0:1], scalar2=-1.0, op0=ALU.add, op1=ALU.mult), R=["ML_Gf", "ML_gb"], W=["ML_Gf"])
                    kb.op("act", lambda h: h.activation(out=Gf[:], in_=Gf[:], func=AF.Exp), R=["ML_Gf"], W=["ML_Gf"])
                    kb.op("act", lambda h: h.activation(out=Gf[:], in_=Gf[:], func=AF.Ln, bias=1.0), R=["ML_Gf"], W=["ML_Gf"])
                    kb.op("dve", lambda h: h.tensor_scalar_mul(out=Gf[:], in0=Gf[:], scalar1=-1.0), R=["ML_Gf"], W=["ML_Gf"])

                    def scan2(outn, outr, src, op1, kn, kr, ksrc):
                        kb.op("dve", lambda h: h.tensor_tensor_scan(out=outn[:], data0=ones[:], data1=src[:], initial=0.0, op0=ALU.mult, op1=op1),
                              R=[ksrc, "ML_ones", kn], W=[kn])
                        kb.op("dve", lambda h: h.tensor_tensor_scan(out=outr[:, 0:CTX][:, ::-1], data0=ones[:, 0:CTX], data1=src[:, 0:CTX][:, ::-1], initial=0.0,
                                                                    op0=ALU.mult, op1=op1), R=[ksrc, "ML_ones", kr], W=[kr])
                        kb.op("dve", lambda h: h.tensor_tensor_scan(out=outr[:, CTX:T][:, ::-1], data0=ones[:, CTX:T], data1=src[:, CTX:T][:, ::-1], initial=outr[:, 0:1],
                                                                    op0=ALU.mult, op1=op1), R=[ksrc, "ML_ones", kr], W=[kr])

                    def select(dst, fn_, fr_, kd, kn, kr):
                        kb.op("dve", lambda h: h.tensor_scalar(out=fn_[:], in0=fn_[:], scalar1=dirm[0:H2, 0:1], scalar2=None, op0=ALU.mult), R=[kn, "dirmask"], W=[kn])
                        kb.op("dve", lambda h: h.scalar_tensor_tensor(out=dst[:], in0=fr_[:], scalar=dirm[0:H2, 1:2], in1=fn_[:], op0=ALU.mult, op1=ALU.add),
                              R=[kn, kr, "dirmask", kd], W=[kd])
                    scan2(Fn, Fr, Gf, ALU.add, "ML_Fn", "ML_Fr", "ML_Gf")
                    select(Gf, Fn, Fr, "ML_Gf", "ML_Fn", "ML_Fr")
                    kb.op("dve", lambda h: h.tensor_tensor(out=Gi[:], in0=Gi[:], in1=Gf[:], op=ALU.subtract), R=["ML_Gi", "ML_Gf"], W=["ML_Gi"])
                    scan2(Fn, Fr, Gi, ALU.max, "ML_Fn", "ML_Fr", "ML_Gi")
                    select(Fn, Fn, Fr, "ML_Fn", "ML_Fn", "ML_Fr")
                    kb.op("dve", lambda h: h.scalar_tensor_tensor(out=Fr[:], in0=Gf[:], scalar=-1.0, in1=Fn[:], op0=ALU.mult, op1=ALU.subtract),
                          R=["ML_Gf", "ML_Fn", "ML_Fr"], W=["ML_Fr"])
                    kb.op("dve", lambda h: h.tensor_scalar_mul(out=Fn[:], in0=Fn[:], scalar1=-1.0), R=["ML_Fn"], W=["ML_Fn"])
                    kb.dma("sp", S["mlg"][b], Fn[:], R=["ML_Fn"], W=[("mlg", b)])
                    for (src, dst, func, ks) in ((Gi, colA, AF.Identity, "ML_Gi"), (Fr, expm, AF.Exp, "ML_Fr")):
                        for n0 in range(0, NCH, 32):
                            n1 = min(NCH, n0 + 32)
                            pt, pk = ps.get()
                            for n in range(n0, n1):
                                kb.op("pe", lambda h, n=n: h.transpose(out=pt[:, (n - n0) * H2:(n - n0 + 1) * H2], in_=src[:, n * 128:(n + 1) * 128],
                                                                       identity=ident[0:H2, 0:H2]), R=[ks, "ident"], W=[pk])
                            kb.op("act", lambda h: h.activation(out=dst[:, n0:n1, :].rearrange("p n r -> p (n r)"), in_=pt[:, 0:(n1 - n0) * H2], func=func),
                                  R=[pk], W=[("ML_col", id(dst))])
                with kb.scope() as sv:
                    vp = Pool(kb, sv, "ML_vT", 2, [128, T], F32)
                    kb.op("pool", lambda h: h.memset(vtok[:].rearrange("p n h v -> p (n h v)"), 1.0), W=["ML_vtok"])
                    vflat = vtok[:].rearrange("p n h v -> p n (h v)")
                    for kc in range(KW):
                        vt, vk = vp.get()
                        kb.dma("sp", vt[:], S["zT"][b, c.o_v + kc * 128:c.o_v + (kc + 1) * 128, :], R=[("zT", b, c.o_v + kc * 128)], W=[vk])
                        for n0 in range(0, NCH, 4):
                            n1 = min(NCH, n0 + 4)
                            pt, pk = ps.get()
                            for n in range(n0, n1):
                                kb.op("pe", lambda h, n=n: h.transpose(out=pt[:, (n - n0) * 128:(n - n0 + 1) * 128], in_=vt[:, n * 128:(n + 1) * 128], identity=ident[:]),
                                      R=[vk, "ident"], W=[pk])
                            c0 = kc * 128
                            while c0 < (kc + 1) * 128:
                                hh = c0 // DH
                                c1 = min((kc + 1) * 128, (hh + 1) * DH)
                                w0 = c0 - hh * DH
                                kb.op("dve", lambda h, c0=c0, c1=c1, hh=hh, w0=w0: h.tensor_copy(
                                    out=vflat[:, n0:n1, hh * DV + w0:hh * DV + w0 + (c1 - c0)],
                                    in_=pt[:, 0:(n1 - n0) * 128].rearrange("p (n c) -> p n c", c=128)[:, :, c0 - kc * 128:c1 - kc * 128]), R=[pk, "ML_vtok"], W=["ML_vtok"])
                                c0 = c1
                for hd in range(H):
                    with kb.scope() as sh:
                        qT = [sh.enter_context(sbt(nc, f"ML_q{j}", [DT, T], BF16)) for j in range(2)]
                        kT = [sh.enter_context(sbt(nc, f"ML_k{j}", [DT, T], BF16)) for j in range(2)]
                        hsum = sh.enter_context(sbt(nc, "ML_hsum", [128, NCH, DH], F32))
                        with kb.scope() as sc:
                            xp = Pool(kb, sc, "ML_xp", 2, [DT, T], F32)
                            yp = Pool(kb, sc, "ML_yp", 2, [DT, T], F32)
                            sgp = Pool(kb, sc, "ML_sgp", 2, [DT, T], F32)
                            for a in range(2):
                                for j in range(2):
                                    r0 = c.o_qk + a * W + hd * DH + j * DT
                                    xt, xk = xp.get()
                                    yt, yk = yp.get()
                                    sgt, sgk = sgp.get()
                                    kb.dma("sp", xt[:], S["zT"][b, r0:r0 + DT, :], R=[("zT", b, (r0 // 128) * 128), ("zT", b, ((r0 + DT - 1) // 128) * 128)], W=[xk])
                                    mi = (a * H + hd) * 2 + j
                                    conv3(o, yt, xt, [cw[:, t_, mi:mi + 1] for t_ in range(3)], cb[:, mi:mi + 1], DT, [xk, "ML_cw", "ML_cb"], yk)
                                    kb.op("act", lambda h: h.activation(out=sgt[:], in_=yt[:], func=AF.Sigmoid), R=[yk], W=[sgk])
                                    dst = qT[j] if a == 0 else kT[j]
                                    sc_ = 1.0 if a == 0 else DH ** -0.5
                                    kb.op("dve", lambda h, dst=dst, sc_=sc_: h.scalar_tensor_tensor(out=dst[:], in0=yt[:], scalar=sc_, in1=sgt[:], op0=ALU.mult, op1=ALU.mult),
                                          R=[yk, sgk], W=[("ML_qk", a, j)])
                        gbp = Pool(kb, sh, "ML_GB", 1, [128, T], F32)
                        dtp = Pool(kb, sh, "ML_DT", 3, [128, 512], F32)
                        ptp = Pool(kb, sh, "ML_PT", 3, [128, 512], BF16)
                        e1p = Pool(kb, sh, "ML_E1", 2, [128, 128], F32)
                        rcp = Pool(kb, sh, "ML_rc", 2, [128, 2], F32)
                        for d in range(2):
                            r_ = d * H + hd
                            gb, gbk = gbp.get()
                            kb.dma("sp", gb[:], S["mlg"][b, r_:r_ + 1, :].broadcast_to([128, T]), R=[("mlg", b)], W=[gbk])
                            if d == 0:
                                pos = {n: n for n in range(NCH)}
                            else:
                                pos = {}
                                for i_, n in enumerate(list(range(NCC - 1, -1, -1)) + list(range(NCH - 1, NCC - 1, -1))):
                                    pos[n] = i_
                            for (t0, t1) in tok_tiles(c):
                                cts = list(range(t0 // 128, t1 // 128))
                                accs = {ct: ps.hold() for ct in cts}
                                keys_for = {ct: [cs for cs in range(NCH) if pos[cs] <= pos[ct]] for ct in cts}
                                all_cs = sorted(set(sum(keys_for.values(), [])), key=lambda n: pos[n])
                                pend = []
                                for cs in all_cs:
                                    al = [ct for ct in cts if pos[cs] <= pos[ct]]
                                    a0, a1 = min(al) * 128, (max(al) + 1) * 128
                                    n = a1 - a0
                                    spt, spk = ps.get()
                                    for j in range(2):
                                        kb.op("pe", lambda h, j=j: h.matmul(out=spt[:, 0:n], lhsT=kT[j][:, cs * 128:(cs + 1) * 128], rhs=qT[j][:, a0:a1],
                                                                            start=(j == 0), stop=(j == 1)), R=[("ML_qk", 1, j), ("ML_qk", 0, j)], W=[spk])
                                    while pend:
                                        pend.pop(0)()
                                    dtt, dtk = dtp.get()
                                    acol = colA[:, cs, r_:r_ + 1]
                                    for ct in al:
                                        o0 = ct * 128 - a0
                                        if ct == cs:
                                            e1, e1k = e1p.get()
                                            kb.op("dve", lambda h, ct=ct: h.scalar_tensor_tensor(out=e1[:], in0=gb[:, ct * 128:(ct + 1) * 128], scalar=acol, in1=negm[:, d, :],
                                                                                                op0=ALU.add, op1=ALU.add), R=[gbk, ("ML_col", id(colA)), "negmask"], W=[e1k])
                                            kb.op("act", lambda h, o0=o0: h.activation(out=dtt[:, o0:o0 + 128], in_=e1[:], func=AF.Exp), R=[e1k, dtk], W=[dtk])
                                    nd = [ct for ct in al if ct != cs]
                                    if nd:
                                        b0, b1 = min(nd) * 128, (max(nd) + 1) * 128
                                        kb.op("act", lambda h: h.activation(out=dtt[:, b0 - a0:b1 - a0], in_=gb[:, b0:b1], func=AF.Exp, bias=acol),
                                              R=[gbk, ("ML_col", id(colA)), dtk], W=[dtk])
                                    ptt, ptk = ptp.get()
                                    kb.op("dve", lambda h: h.tensor_tensor(out=ptt[:, 0:n], in0=spt[:, 0:n], in1=dtt[:, 0:n], op=ALU.mult), R=[spk, dtk], W=[ptk])
                                    def mkpv(al=al, a0=a0, cs=cs, ptt=ptt, ptk=ptk):
                                        def f():
                                            for ct in al:
                                                o0 = ct * 128 - a0
                                                at, ak = accs[ct]
                                                kb.op("pe", lambda h, o0=o0, at=at: h.matmul(out=at[:, 0:DV], lhsT=ptt[:, o0:o0 + 128], rhs=vtok[:, cs, hd, :],
                                                                                             start=(pos[cs] == 0), stop=(cs == ct)), R=[ptk, "ML_vtok"], W=[ak])
                                        return f
                                    pend.append(mkpv())
                                while pend:
                                    pend.pop(0)()
                                for ct in cts:
                                    ps.release(accs[ct][1])
                                for ct in cts:
                                    at, ak = accs[ct]
                                    rc, rck = rcp.get()
                                    kb.op("act", lambda h: h.activation(out=rc[:, 0:1], in_=at[:, DH:DH + 1], func=AF.Abs), R=[ak, rck], W=[rck])
                                    kb.op("dve", lambda h: h.tensor_tensor(out=rc[:, 0:1], in0=rc[:, 0:1], in1=expm[:, ct, r_:r_ + 1], op=ALU.max),
                                          R=[rck, ("ML_col", id(expm))], W=[rck])
                                    kb.op("dve", lambda h: h.reciprocal(out=rc[:, 1:2], in_=rc[:, 0:1]), R=[rck], W=[rck])
                                    if d == 0:
                                        kb.op("act", lambda h, ct=ct: h.activation(out=hsum[:, ct, :], in_=at[:, 0:DH], func=AF.Identity, scale=rc[:, 1:2]), R=[ak, rck, "ML_hsum"], W=["ML_hsum"])
                                    else:
                                        kb.op("dve", lambda h, ct=ct: h.scalar_tensor_tensor(out=hsum[:, ct, :], in0=at[:, 0:DH], scalar=rc[:, 1:2], in1=hsum[:, ct, :],
                                                                                            op0=ALU.mult, op1=ALU.add), R=[ak, rck, "ML_hsum"], W=["ML_hsum"])
                        stp = Pool(kb, sh, "ML_st", 2, [128, 8], F32)
                        for n in range(NCH):
                            stt, stk = stp.get()
                            kb.op("dve", lambda h, n=n: h.bn_stats(out=stt[:, 0:6], in_=hsum[:, n, :]), R=["ML_hsum"], W=[stk])
                            kb.op("dve", lambda h: h.bn_aggr(out=stt[:, 6:8], in_=stt[:, 0:6]), R=[stk], W=[stk])
                            kb.op("dve", lambda h: h.tensor_scalar_add(out=stt[:, 0:1], in0=stt[:, 7:8], scalar1=LN_EPS), R=[stk], W=[stk])
                            kb.op("act", lambda h: h.activation(out=stt[:, 0:1], in_=stt[:, 0:1], func=AF.Ln), R=[stk], W=[stk])
                            kb.op("act", lambda h: h.activation(out=stt[:, 0:1], in_=stt[:, 0:1], func=AF.Exp, scale=-0.5), R=[stk], W=[stk])
                            kb.op("dve", lambda h, n=n: h.tensor_scalar(out=hn[:, n, hd * DH:(hd + 1) * DH], in0=hsum[:, n, :], scalar1=stt[:, 6:7], scalar2=stt[:, 0:1],
                                                                        op0=ALU.subtract, op1=ALU.mult), R=["ML_hsum", stk], W=["ML_hn"])
                with kb.scope() as so:
                    op_ = Pool(kb, so, "ML_o", 2, [128, T], F32)
                    yo_ = Pool(kb, so, "ML_yo", 2, [128, T], BF16)
                    for kc in range(KW):
                        ot, ok = op_.get()
                        yo, yok = yo_.get()
                        kb.dma("sp", ot[:], S["zT"][b, c.o_o + kc * 128:c.o_o + (kc + 1) * 128, :], R=[("zT", b, c.o_o + kc * 128)], W=[ok])
                        kb.op("act", lambda h: h.activation(out=ot[:], in_=ot[:], func=AF.Sigmoid), R=[ok], W=[ok])
                        for n0 in range(0, NCH, 4):
                            n1 = min(NCH, n0 + 4)
                            pt, pk = ps.get()
                            ptb = pt[:].bitcast(BF16)
                            for n in range(n0, n1):
                                kb.op("pe", lambda h, n=n: h.transpose(out=ptb[:, (n - n0) * 128:(n - n0 + 1) * 128], in_=hn[:, n, kc * 128:(kc + 1) * 128], identity=identb[:]),
                                      R=["ML_hn", "identb"], W=[pk])
                            kb.op("dve", lambda h: h.scalar_tensor_tensor(out=yo[:, n0 * 128:n1 * 128], in0=ptb[:, 0:(n1 - n0) * 128], scalar=ngcol[:, kc:kc + 1],
                                                                          in1=ot[:, n0 * 128:n1 * 128], op0=ALU.mult, op1=ALU.mult), R=[pk, ok, "ML_ng"], W=[yok])
                        kb.dma("sp", S["ymlT"][b, kc * 128:(kc + 1) * 128, :], yo[:], R=[yok], W=[("ymlT", b, kc)])


def hy_consts(L):
    import ml_dtypes
    NL = L // 128
    t = np.arange(L, dtype=np.float64)
    f = np.arange(L, dtype=np.float64)
    th = np.pi * np.outer(t, 2 * f + 1) / (2 * L)
    C, Sn = np.cos(th), np.sin(th)
    def ftile(M):
        return M.reshape(NL, 128, NL, 128).transpose(2, 1, 0, 3)
    F = np.stack([ftile(C), ftile(Sn)], 2)
    TG = min(512, L)
    NG = L // TG
    def itile(M):
        return M.T.reshape(NL, 128, NG, TG).transpose(2, 1, 0, 3)
    Iv = np.stack([itile(C), itile(Sn)], 2)
    pos = np.arange(L, dtype=np.float64)
    tt = pos / (L - 1)
    bands = np.linspace(1e-4, 15.0, 16)
    ang = (2.0 * np.pi / L) * pos[:, None] * bands[None, :]
    feats = np.concatenate([tt[:, None], np.cos(ang), np.sin(ang)], -1)
    lagt = tt.reshape(NL, 128).T
    lagmask = np.ones((128, NL), np.float32)
    lagmask[0, 0] = 0.0
    return {f"dftF{L}": F.astype(ml_dtypes.bfloat16), f"dftI{L}": Iv.astype(ml_dtypes.bfloat16),
            f"hyfeat{L}": np.ascontiguousarray(feats.T).astype(np.float32), f"lagt{L}": np.ascontiguousarray(lagt).astype(np.float32),
            f"lagmask{L}": lagmask}


def phase_HY(o, l, last):
    c, nc, kb, I, S, ps = o["c"], o["nc"], o["kb"], o["I"], o["S"], o["ps"]
    T, W, KW, NB, CTX, SEQ = c.T, c.W, c.KW, c.NB, c.CTX, c.SEQ
    ident = o["ident"]
    insts = [(SEQ, CTX)] + ([] if (last and not getattr(c, "force_ctx", False)) else [(CTX, 0)])
    with kb.scope() as st:
        cw = st.enter_context(sbt(nc, "HY_cw", [128, 3, 3 * KW], F32))
        cb = st.enter_context(sbt(nc, "HY_cb", [128, 3 * KW], F32))
        for t_ in range(3):
            kb.dma("sp", cw[:, t_, :], I["hy_conv_w"][l, t_, :].rearrange("(m p) -> p m", p=128), W=["HY_cw"], allow_slow_non_contiguous=True)
        kb.dma("sp", cb[:], I["hy_conv_b"][l].rearrange("(m p) -> p m", p=128), W=["HY_cb"], allow_slow_non_contiguous=True)
        xp = Pool(kb, st, "HY_xp", 2, [128, T], F32)
        yp = Pool(kb, st, "HY_yp", 2, [128, T], F32)
        for b in range(NB):
            for m in range(3 * KW):
                xt, xk = xp.get()
                yt, yk = yp.get()
                kb.dma("sp", xt[:], S["zT"][b, m * 128:(m + 1) * 128, :], R=[("zT", b, m * 128)], W=[xk])
                conv3(o, yt, xt, [cw[:, t_, m:m + 1] for t_ in range(3)], cb[:, m:m + 1], 128, [xk, "HY_cw", "HY_cb"], yk)
                kb.dma("sp", S["hyc"][b, m * 128:(m + 1) * 128, :], yt[:], R=[yk], W=[("hyc", b, m)])
    for (L, toff) in insts:
        NL = L // 128
        khat = S[f"khat{L}"]
        with kb.scope() as st:
            w1 = st.enter_context(sbt(nc, "HY_w1", [c.FEAT, c.HID], F32))
            w2 = st.enter_context(sbt(nc, "HY_w2", [c.HID, c.HID], F32))
            w3 = st.enter_context(sbt(nc, "HY_w3", [c.HID, 4 * W], F32))
            cols = st.enter_context(sbt(nc, "HY_cols", [c.HID, 8], F32))
            featT = st.enter_context(sbt(nc, "HY_featT", [c.FEAT, L], F32))
            h1T = st.enter_context(sbt(nc, "HY_h1T", [c.HID, L], F32))
            h2T = st.enter_context(sbt(nc, "HY_h2T", [c.HID, L], F32))
            lagt = st.enter_context(sbt(nc, "HY_lagt", [128, NL], F32))
            lagm = st.enter_context(sbt(nc, "HY_lagm", [128, NL], F32))
            adec = st.enter_context(sbt(nc, "HY_adec", [128, 2, W], F32))
            winf = st.enter_context(sbt(nc, "HY_winf", [128, 2, W], F32))
            hk = st.enter_context(sbt(nc, "HY_hk", [128, 2, W], F32))
            HS = st.enter_context(sbt(nc, "HY_HS", [128, NL, 2, W], BF16))
            kb.dma("sp", w1[:], I["hy_ffn_w1"][l], W=["HY_w"])
            kb.dma("sp", w2[:], I["hy_ffn_w2"][l], W=["HY_w"])
            kb.dma("sp", w3[:], I["hy_ffn_w3"][l], W=["HY_w"])
            for i_, nm in enumerate(("hy_ffn_b1", "hy_ffn_b2", "hy_sin_freq")):
                kb.dma("sp", cols[:, i_:i_ + 1], I[nm][l].rearrange("(h o) -> h o", o=1), W=["HY_cols"])
            kb.dma("sp", featT[:], I[f"hyfeat{L}"], W=["HY_featT"])
            kb.dma("sp", lagt[:], I[f"lagt{L}"], W=["HY_lagt"])
            kb.dma("sp", lagm[:], I[f"lagmask{L}"], W=["HY_lagm"])
            kb.dma("sp", adec[:].rearrange("p o w -> p (o w)"), I["hy_decay"][l].rearrange("o w -> (o w)").rearrange("(a n) -> a n", a=1).broadcast_to([128, 2 * W]), W=["HY_adec"])
            kb.op("act", lambda h: h.activation(out=adec[:], in_=adec[:], func=AF.Abs), R=["HY_adec"], W=["HY_adec"])
            kb.op("dve", lambda h: h.tensor_scalar_mul(out=lagt[:], in0=lagt[:], scalar1=-1.0), R=["HY_lagt"], W=["HY_lagt"])
            for i_ in range(2):
                kb.op("dve", lambda h, i_=i_: h.tensor_tensor(out=cols[:, 3 + i_:4 + i_], in0=cols[:, i_:i_ + 1], in1=cols[:, 2:3], op=ALU.mult), R=["HY_cols"], W=["HY_cols"])
                kb.op("dve", lambda h, i_=i_: h.tensor_scalar_add(out=cols[:, 3 + i_:4 + i_], in0=cols[:, 3 + i_:4 + i_], scalar1=9.0 * math.pi), R=["HY_cols"], W=["HY_cols"])
            tmpp = Pool(kb, st, "HY_tmp", 2, [c.HID, 512], F32)
            hrf = st.enter_context(sbt(nc, "HY_rrf", [c.HID, 512], F32))
            hri = st.enter_context(sbt(nc, "HY_rri", [c.HID, 512], I32))
            for (wm, kdim, src, dst, fbc, ksrc, kdst) in ((w1, c.FEAT, featT, h1T, 3, "HY_featT", "HY_h1T"), (w2, c.HID, h1T, h2T, 4, "HY_h1T", "HY_h2T")):
                for (a0, a1) in tgroups(0, L):
                    n = a1 - a0
                    pt, pk = ps.get()
                    kb.op("pe", lambda h: h.matmul(out=pt[0:c.HID, 0:n], lhsT=wm[0:kdim, :], rhs=src[0:kdim, a0:a1], start=True, stop=True), R=["HY_w", ksrc], W=[pk])
                    tt, tk = tmpp.get()
                    kb.op("dve", lambda h: h.tensor_scalar(out=tt[:, 0:n], in0=pt[0:c.HID, 0:n], scalar1=cols[:, 2:3], scalar2=cols[:, fbc:fbc + 1], op0=ALU.mult, op1=ALU.add),
                          R=[pk, "HY_cols"], W=[tk])
                    range_reduce(kb, tt[:, 0:n], hrf[:, 0:n], hri[:, 0:n], [tk, "HY_rr"])
                    kb.op("act", lambda h: h.activation(out=dst[:, a0:a1], in_=tt[:, 0:n], func=AF.Sin), R=[tk, kdst], W=[kdst])
            fp_ = Pool(kb, st, "HY_F", 2, [128, 2, NL, 128], BF16)
            ksp = Pool(kb, st, "HY_ks", 2, [128, 2, W], F32)
            for o_ in range(2):
                for dc in range(NL):
                    for dr in range(2):
                        kb.op("act", lambda h, dr=dr: h.activation(out=winf[:, dr, :], in_=adec[:, o_, :], func=AF.Exp, scale=lagt[:, dc:dc + 1]),
                              R=["HY_adec", "HY_lagt", "HY_winf"], W=["HY_winf"])
                    hkf = hk[:].rearrange("p d w -> p (d w)")
                    wff = winf[:].rearrange("p d w -> p (d w)")
                    for (a0, a1) in tgroups(0, 2 * W):
                        pt, pk = ps.get()
                        kb.op("pe", lambda h: h.matmul(out=pt[:, 0:a1 - a0], lhsT=h2T[:, dc * 128:(dc + 1) * 128], rhs=w3[:, o_ * 2 * W + a0:o_ * 2 * W + a1], start=True, stop=True),
                              R=["HY_h2T", "HY_w"], W=[pk])
                        kb.op("dve", lambda h: h.tensor_tensor(out=hkf[:, a0:a1], in0=pt[:, 0:a1 - a0], in1=wff[:, a0:a1], op=ALU.mult), R=[pk, "HY_winf", "HY_hk"], W=["HY_hk"])
                    mcol = lagm[:, dc:dc + 1]
                    kb.op("dve", lambda h: h.scalar_tensor_tensor(out=HS[:, dc, 0, :], in0=hk[:, 1, :], scalar=mcol, in1=hk[:, 0, :], op0=ALU.mult, op1=ALU.add),
                          R=["HY_hk", "HY_lagm", "HY_HS"], W=["HY_HS"])
                    kb.op("dve", lambda h: h.scalar_tensor_tensor(out=HS[:, dc, 1, :], in0=hk[:, 1, :], scalar=mcol, in1=hk[:, 0, :], op0=ALU.mult, op1=ALU.subtract),
                          R=["HY_hk", "HY_lagm", "HY_HS"], W=["HY_HS"])
                for fc in range(NL):
                    ft, fk = fp_.get()
                    kb.dma("sp", ft[:], I[f"dftF{L}"][fc], W=[fk])
                    kst, kk = ksp.get()
                    for ri in range(2):
                        for (a0, a1) in tgroups(0, W):
                            pt, pk = ps.get()
                            for dc in range(NL):
                                kb.op("pe", lambda h, dc=dc: h.matmul(out=pt[:, 0:a1 - a0], lhsT=ft[:, ri, dc, :], rhs=HS[:, dc, ri, a0:a1], start=(dc == 0), stop=(dc == NL - 1)),
                                      R=[fk, "HY_HS"], W=[pk])
                            kb.op("act", lambda h: h.activation(out=kst[:, ri, a0:a1], in_=pt[:, 0:a1 - a0], func=AF.Identity, scale=1.0 / L), R=[pk, kk], W=[kk])
                    kb.dma("sp", khat[fc * 128:(fc + 1) * 128, o_], kst[:], R=[kk], W=[("khat", L, fc, o_)])
    with kb.scope() as st0:
        bcol = st0.enter_context(sbt(nc, "HY_bcol", [128, 2, KW], F32))
        for o_ in range(2):
            load_cols(o, bcol[:, o_, :], I["hy_bias"][l, o_, :], "HY_bcol", KW)
        for (L, toff) in insts:
            NL = L // 128
            TG = min(512, L)
            NG = L // TG
            khat = S[f"khat{L}"]
            for o_ in range(2):
                src = S["hyc"] if o_ == 0 else S["hyy1"]
                srck = (lambda b, k: ("hyc", b, k)) if o_ == 0 else (lambda b, k: ("hyy1", b, k, L))
                for b in range(NB):
                    with kb.scope() as st:
                        Y = st.enter_context(sbt(nc, "HY_Y", [128, NL, 2, W], BF16))
                        with kb.scope() as s1:
                            utok = s1.enter_context(sbt(nc, "HY_utok", [128, NL, W], BF16))
                            xin = Pool(kb, s1, "HY_xin", 2, [128, L], F32)
                            for kc in range(KW):
                                xt, xk = xin.get()
                                kb.dma("sp", xt[:], src[b, kc * 128:(kc + 1) * 128, toff:toff + L], R=[srck(b, kc)], W=[xk])
                                for n0 in range(0, NL, 4):
                                    n1 = min(NL, n0 + 4)
                                    pt, pk = ps.get()
                                    for n in range(n0, n1):
                                        kb.op("pe", lambda h, n=n: h.transpose(out=pt[:, (n - n0) * 128:(n - n0 + 1) * 128], in_=xt[:, n * 128:(n + 1) * 128], identity=ident[:]),
                                              R=[xk, "ident"], W=[pk])
                                    kb.op("act", lambda h: h.activation(out=utok[:, n0:n1, kc * 128:(kc + 1) * 128], in_=pt[:, 0:(n1 - n0) * 128].rearrange("p (n c) -> p n c", c=128),
                                                                        func=AF.Identity), R=[pk, "HY_utok"], W=["HY_utok"])
                            fp_ = Pool(kb, s1, "HY_F2", 2, [128, 2, NL, 128], BF16)
                            kp_ = Pool(kb, s1, "HY_K", 2, [128, 2, W], F32)
                            tp_ = Pool(kb, s1, "HY_t", 4, [128, 512], F32)
                            for fc in range(NL):
                                ft, fk = fp_.get()
                                kb.dma("sp", ft[:], I[f"dftF{L}"][fc], W=[fk])
                                kt, kk = kp_.get()
                                kb.dma("sp", kt[:], khat[fc * 128:(fc + 1) * 128, o_], R=[("khat", L, fc, o_)], W=[kk])
                                for (a0, a1) in tgroups(0, W):
                                    n = a1 - a0
                                    (pa, pak), (pb, pbk) = ps.get(), ps.get()
                                    for cs_, (pt, pk) in enumerate(((pa, pak), (pb, pbk))):
                                        for tc in range(NL):
                                            kb.op("pe", lambda h, tc=tc, cs_=cs_, pt=pt: h.matmul(out=pt[:, 0:n], lhsT=ft[:, cs_, tc, :], rhs=utok[:, tc, a0:a1],
                                                                                              start=(tc == 0), stop=(tc == NL - 1)), R=[fk, "HY_utok"], W=[pk])
                                    t1, t1k = tp_.get()
                                    t2, t2k = tp_.get()
                                    kr, ki = kt[:, 0, a0:a1], kt[:, 1, a0:a1]
                                    kb.op("dve", lambda h: h.tensor_tensor(out=t1[:, 0:n], in0=pa[:, 0:n], in1=kr, op=ALU.mult), R=[pak, kk], W=[t1k])
                                    kb.op("dve", lambda h: h.tensor_tensor(out=t2[:, 0:n], in0=pb[:, 0:n], in1=ki, op=ALU.mult), R=[pbk, kk], W=[t2k])
                                    kb.op("pool", lambda h: h.tensor_tensor(out=Y[:, fc, 0, a0:a1], in0=t1[:, 0:n], in1=t2[:, 0:n], op=ALU.add), R=[t1k, t2k], W=[("HY_Y", fc)])
                                    t3, t3k = tp_.get()
                                    t4, t4k = tp_.get()
                                    kb.op("dve", lambda h: h.tensor_tensor(out=t3[:, 0:n], in0=pb[:, 0:n], in1=kr, op=ALU.mult), R=[pbk, kk], W=[t3k])
                                    kb.op("dve", lambda h: h.tensor_tensor(out=t4[:, 0:n], in0=pa[:, 0:n], in1=ki, op=ALU.mult), R=[pak, kk], W=[t4k])
                                    kb.op("pool", lambda h: h.tensor_tensor(out=Y[:, fc, 1, a0:a1], in0=t3[:, 0:n], in1=t4[:, 0:n], op=ALU.subtract), R=[t3k, t4k], W=[("HY_Y", fc)])
                        ip_ = Pool(kb, st, "HY_I", 2, [128, 2, NL, TG], BF16)
                        up_ = Pool(kb, st, "HY_u", 3, [128, TG], F32)
                        gp_ = Pool(kb, st, "HY_g", 3, [128, TG], F32)
                        op_ = Pool(kb, st, "HY_o", 3, [128, TG], F32 if o_ == 0 else BF16)
                        for tg in range(NG):
                            it, ik = ip_.get()
                            kb.dma("sp", it[:], I[f"dftI{L}"][tg], W=[ik])
                            q0 = toff + tg * TG
                            for kc in range(KW):
                                pt, pk = ps.get()
                                i_ = 0
                                for fc in range(NL):
                                    for cs_ in range(2):
                                        kb.op("pe", lambda h, fc=fc, cs_=cs_: h.matmul(out=pt[:, 0:TG], lhsT=Y[:, fc, cs_, kc * 128:(kc + 1) * 128], rhs=it[:, cs_, fc, :],
                                                                                       start=(i_ == 0), stop=(i_ == 2 * NL - 1)), R=[("HY_Y", fc), ik], W=[pk])
                                        i_ += 1
                                ut, uk = up_.get()
                                gt, gk = gp_.get()
                                ot, ok = op_.get()
                                kb.dma("sp", ut[:], src[b, kc * 128:(kc + 1) * 128, q0:q0 + TG], R=[srck(b, kc)], W=[uk])
                                grow = (1 + o_) * W + kc * 128
                                kb.dma("sp", gt[:], S["hyc"][b, grow:grow + 128, q0:q0 + TG], R=[("hyc", b, grow // 128)], W=[gk])
                                kb.op("dve", lambda h: h.scalar_tensor_tensor(out=ut[:], in0=ut[:], scalar=bcol[:, o_, kc:kc + 1], in1=pt[:, 0:TG], op0=ALU.mult, op1=ALU.add),
                                      R=[uk, pk, "HY_bcol"], W=[uk])
                                kb.op("pool", lambda h: h.tensor_tensor(out=ot[:], in0=ut[:], in1=gt[:], op=ALU.mult), R=[uk, gk, ok], W=[ok])
                                if o_ == 0:
                                    kb.dma("sp", S["hyy1"][b, kc * 128:(kc + 1) * 128, q0:q0 + TG], ot[:], R=[ok], W=[("hyy1", b, kc, L, tg)])
                                else:
                                    kb.dma("sp", S["yhyT"][b, kc * 128:(kc + 1) * 128, q0:q0 + TG], ot[:], R=[ok], W=[("yhyT", b, kc, L, tg)])


WEIGHT_KEYS = ("w_mod", "b_mod", "w_in", "hy_conv_w", "hy_conv_b", "hy_ffn_w1", "hy_ffn_b1", "hy_ffn_w2", "hy_ffn_b2",
               "hy_ffn_w3", "hy_sin_freq", "hy_decay", "hy_bias", "ml_conv_w", "ml_conv_b", "ml_gate_b", "ml_norm_g",
               "s5_a_re", "s5_a_im", "s5_log_dt", "s5_b_re", "s5_b_im", "s5_c_re", "s5_c_im", "s5_d", "s5_glu_w", "s5_glu_b",
               "w_hy_out", "w_ml_out", "w_s5_out", "w_out", "ln1_g", "ln1_b", "ln2_g", "ln2_b", "w_ff1", "w_ff2")


def make_in_maps(cfg, inputs, ncores):
    consts = host_consts(cfg)
    x = np.asarray(inputs["x"], np.float32)
    ctx = np.asarray(inputs["ctx"], np.float32)
    cvec = np.asarray(inputs["c"], np.float32)
    cctx = np.asarray(inputs["c_ctx"], np.float32)
    wts = {k: np.ascontiguousarray(np.asarray(inputs[k], np.float32)) for k in WEIGHT_KEYS}
    maps = []
    for i in range(ncores):
        bs = [cfg.NB * i + j for j in range(cfg.NB)]
        m = dict(consts)
        m.update(wts)
        m["xin"] = np.ascontiguousarray(np.stack([np.concatenate([ctx[b], x[b]], 0) for b in bs]))
        m["cloc"] = np.ascontiguousarray(np.stack([cvec[bs[0]], cvec[bs[1]], cctx]))
        maps.append(m)
    return maps


_PROG = {}


def kernel(**inputs):
    cfg = Cfg()
    if "nc" not in _PROG:
        _PROG["nc"] = build_program(cfg)
    nc = _PROG["nc"]
    ncores = 8
    in_maps = make_in_maps(cfg, inputs, ncores)
    res = run_bass_kernel_spmd(nc, in_maps, core_ids=list(range(ncores)))
    out = np.concatenate([np.asarray(r["out"], np.float32) for r in res.results], axis=0)
    return out
```

```python
import math
from contextlib import ExitStack
import numpy as np
import concourse.bass as bass
import concourse.mybir as mybir
from concourse.bass_utils import run_bass_kernel_spmd

F32 = mybir.dt.float32
BF16 = mybir.dt.bfloat16
I32 = mybir.dt.int32
ALU = mybir.AluOpType
AF = mybir.ActivationFunctionType
AX = mybir.AxisListType

LN_EPS = 1e-5
EPOCH = 12000
NRING = 12


class Cfg:
    def __init__(s, D=2048, SEQ=2048, CTX=256, W=768, H=4, DEPTH=2, NB=2, GRID_W=64):
        s.D, s.SEQ, s.CTX, s.W, s.H, s.DEPTH, s.NB, s.GRID_W = D, SEQ, CTX, W, H, DEPTH, NB, GRID_W
        s.T = SEQ + CTX
        s.DH = W // H
        s.DT = s.DH // 2
        s.DFF = 4 * D
        s.G = W // 16
        s.P = 64
        s.HID = 64
        s.FEAT = 33
        s.NIN = 8 * W + 4 * H + 3 * D
        s.o_hy, s.o_qk, s.o_v, s.o_o = 0, 3 * W, 5 * W, 6 * W
        s.o_gt = 7 * W
        s.o_u = 7 * W + 4 * H
        s.o_mg = 8 * W + 4 * H
        s.alpha = (2 * DEPTH) ** 0.25
        s.KD = D // 128
        s.KW = W // 128
        s.NCH = s.T // 128
        s.ROWS = SEQ // GRID_W


def tgroups(n0, n1, step=512):
    out = []
    a = n0
    while a < n1:
        b = min(a + step, n1)
        out.append((a, b))
        a = b
    return out


_UID = [0]


def sbt(nc, name, shape, dtype):
    _UID[0] += 1
    return nc.sbuf_tensor(f"{name}_u{_UID[0]}", shape, dtype)


class Eng:
    def __init__(s, name, h):
        s.name, s.h = name, h
        s.sems = []
        s.cnt = 0
        s.seen = {}
        s.ring = []
        s.dcount = 0


class KB:
    def __init__(s, nc, stack):
        s.nc, s.stack = nc, stack
        s.E = {}
        for name, h in (("pe", nc.tensor), ("dve", nc.vector), ("act", nc.scalar),
                        ("pool", nc.gpsimd), ("sp", nc.sync)):
            s.E[name] = Eng(name, h)
        s.nsem = 0
        s.last_w = {}
        s.readers = {}
        s.ninst = 0

    def newsem(s, name):
        s.nsem += 1
        return s.stack.enter_context(s.nc.semaphore(f"{name}_{s.nsem}"))

    def _wait(s, eng, tok):
        if tok is None:
            return
        sid, sem, val = tok
        if eng.seen.get(sid, 0) >= val:
            return
        eng.h.wait_ge(sem, val)
        eng.seen[sid] = val

    def _deps(s, eng, R, W):
        for key in R:
            s._wait(eng, s.last_w.get(key))
        for key in W:
            s._wait(eng, s.last_w.get(key))
            rd = s.readers.get(key)
            if rd:
                for t in rd.values():
                    s._wait(eng, t)

    def _record(s, tok, R, W):
        for key in W:
            s.last_w[key] = tok
            s.readers[key] = {}
        for key in R:
            d = s.readers.setdefault(key, {})
            d[tok[0]] = tok

    def op(s, en, fn, R=(), W=()):
        eng = s.E[en]
        s._deps(eng, R, W)
        ep = eng.cnt // EPOCH
        while len(eng.sems) <= ep:
            eng.sems.append(s.newsem(en))
            if len(eng.sems) > 1:
                eng.seen[(en, len(eng.sems) - 2)] = EPOCH
        sem = eng.sems[ep]
        ins = fn(eng.h)
        eng.cnt += 1
        val = eng.cnt - ep * EPOCH
        ins.then_inc(sem, 1)
        sid = (en, ep)
        if en == "pe":
            eng.seen[sid] = val
        s._record((sid, sem, val), R, W)
        s.ninst += 1

    def dma(s, qn, out, in_, R=(), W=(), **kw):
        q = s.E[qn]
        s._deps(q, R, W)
        if not q.ring:
            q.ring = [s.newsem(qn + "d") for _ in range(NRING)]
        i = q.dcount
        slot, gen = i % NRING, i // NRING
        sem = q.ring[slot]
        sid = (qn + "d", slot)
        if gen > 0:
            s._wait(q, (sid, sem, 16 * gen))
        ins = q.h.dma_start(out=out, in_=in_, **kw)
        ins.then_inc(sem, 16)
        q.dcount += 1
        s._record((sid, sem, 16 * (gen + 1)), R, W)
        s.ninst += 1

    def barrier(s):
        toks = []
        for e in s.E.values():
            if e.cnt > 0:
                ep = (e.cnt - 1) // EPOCH
                toks.append(((e.name, ep), e.sems[ep], e.cnt - ep * EPOCH))
            for slot, sem in enumerate(e.ring):
                n = (e.dcount - slot + NRING - 1) // NRING
                if n > 0:
                    toks.append(((e.name + "d", slot), sem, 16 * n))
        for e in s.E.values():
            for t in toks:
                s._wait(e, t)

    def scope(s):
        return Scope(s)

    def finish(s):
        sp = s.E["sp"]
        for q in s.E.values():
            for slot, sem in enumerate(q.ring):
                n = (q.dcount - slot + NRING - 1) // NRING
                if n > 0:
                    s._wait(sp, ((q.name + "d", slot), sem, 16 * n))


class Scope:
    def __init__(s, kb):
        s.kb = kb
        s.st = ExitStack()

    def __enter__(s):
        s.st.__enter__()
        return s.st

    def __exit__(s, *a):
        s.kb.barrier()
        return s.st.__exit__(*a)


class PS:
    def __init__(s, kb, stack):
        s.kb = kb
        s.banks = [stack.enter_context(kb.nc.psum_tensor(f"psb{i}", [128, 512], F32)) for i in range(8)]
        s.i = 0
        s.held = set()

    def get(s):
        while s.i in s.held:
            s.i = (s.i + 1) % 8
        i = s.i
        s.i = (s.i + 1) % 8
        return s.banks[i], ("ps", i)

    def hold(s):
        b, k = s.get()
        s.held.add(k[1])
        assert len(s.held) <= 6
        return b, k

    def release(s, k):
        s.held.discard(k[1])


class Pool:
    def __init__(s, kb, stack, name, n, shape, dtype):
        s.tiles = [stack.enter_context(sbt(kb.nc, f"{name}{i}", shape, dtype)) for i in range(n)]
        s.name, s.n, s.i = name, n, 0

    def get(s):
        i = s.i
        s.i = (s.i + 1) % s.n
        return s.tiles[i], (s.name, i)


def host_consts(c):
    TC = min(256, c.CTX)
    io = np.zeros((128, 2, TC), np.float32)
    io[:, 0, :] = np.arange(1, TC + 1, dtype=np.float32)[None]
    io[:, 1, :] = (TC - np.arange(TC, dtype=np.float32))[None]
    p = np.arange(128)
    cm = np.stack([((p // 16) % 2 == 0), ((p // 16) % 2 == 1)], 1).astype(np.float32)
    qm = np.stack([(p // 32 == q) for q in range(4)], 1).astype(np.float32)
    sidx = np.arange(128)[:, None]
    tidx = np.arange(128)[None, :]
    ng = np.stack([np.where(sidx <= tidx, 0.0, -30000.0), np.where(sidx >= tidx, 0.0, -30000.0)], 1).astype(np.float32)
    dm = np.stack([(p < c.H), (p >= c.H)], 1).astype(np.float32)
    out = {"ident": np.eye(128, dtype=np.float32), "ciota": io, "cmask": cm, "qmask": qm, "negmask": ng, "dirmask": dm}
    out.update(hy_consts(c.SEQ))
    out.update(hy_consts(c.CTX))
    return out


def build_program(cfg, phases=("A", "P", "S5", "ML", "HY", "M", "F"), debug_out=False):
    c = cfg
    nc = bass.Bass("TRN2", target_bir_lowering=False)
    D, T, W, NB, KD, NIN = c.D, c.T, c.W, c.NB, c.KD, c.NIN
    stack = ExitStack()
    kb = KB(nc, stack)

    def din(name, shape, dt=F32):
        return nc.dram_tensor(name, list(shape), dt, kind="ExternalInput").ap()

    def dscr(name, shape, dt=F32):
        if debug_out:
            return nc.dram_tensor(name, list(shape), dt, kind="ExternalOutput").ap()
        return nc.dram_tensor(name, list(shape), dt).ap()

    L = c.DEPTH
    I = {}
    I["xin"] = din("xin", [NB, T, D])
    I["cloc"] = din("cloc", [3, D])
    for nm, shp in (("w_mod", [L, D, 6 * D]), ("b_mod", [L, 6 * D]), ("w_in", [L, D, NIN]),
                    ("hy_conv_w", [L, 3, 3 * W]), ("hy_conv_b", [L, 3 * W]),
                    ("hy_ffn_w1", [L, c.FEAT, c.HID]), ("hy_ffn_b1", [L, c.HID]),
                    ("hy_ffn_w2", [L, c.HID, c.HID]), ("hy_ffn_b2", [L, c.HID]),
                    ("hy_ffn_w3", [L, c.HID, 4 * W]), ("hy_sin_freq", [L, c.HID]),
                    ("hy_decay", [L, 2, W]), ("hy_bias", [L, 2, W]),
                    ("ml_conv_w", [L, 3, 2 * W]), ("ml_conv_b", [L, 2 * W]),
                    ("ml_gate_b", [L, 4, c.H]), ("ml_norm_g", [L, W]),
                    ("s5_a_re", [L, 2, c.G, c.P]), ("s5_a_im", [L, 2, c.G, c.P]),
                    ("s5_log_dt", [L, 2, c.G]), ("s5_b_re", [L, 2, c.G, c.P, 16]),
                    ("s5_b_im", [L, 2, c.G, c.P, 16]), ("s5_c_re", [L, 2, c.G, 16, c.P]),
                    ("s5_c_im", [L, 2, c.G, 16, c.P]), ("s5_d", [L, W]),
                    ("s5_glu_w", [L, W, W]), ("s5_glu_b", [L, W]),
                    ("w_hy_out", [L, W, D]), ("w_ml_out", [L, W, D]), ("w_s5_out", [L, W, D]),
                    ("w_out", [L, D, D]), ("ln1_g", [L, D]), ("ln1_b", [L, D]),
                    ("ln2_g", [L, D]), ("ln2_b", [L, D]),
                    ("w_ff1", [L, D, c.DFF]), ("w_ff2", [L, c.DFF, D])):
        I[nm] = din(nm, shp)
    I["ident"] = din("ident", [128, 128])
    out = nc.dram_tensor("out", [NB, c.SEQ, D], F32, kind="ExternalOutput").ap()

    S = {}
    S["xres"] = dscr("xres", [NB, T, D])
    S["x1"] = dscr("x1", [NB, T, D])
    S["modrow"] = dscr("modrow", [L, 3, 6 * D])
    S["zT"] = dscr("zT", [NB, NIN, T])
    S["mlg"] = dscr("mlg", [NB, 2 * c.H, T])
    S["s5z"] = dscr("s5z", [NB, W, T], BF16)
    S["hyc"] = dscr("hyc", [NB, 3 * W, T])
    S["hyy1"] = dscr("hyy1", [NB, W, T])
    for L_ in (c.SEQ, c.CTX):
        NL_ = L_ // 128
        TG_ = min(512, L_)
        S[f"khat{L_}"] = dscr(f"khat{L_}", [L_, 2, 2, W])
        I[f"dftF{L_}"] = din(f"dftF{L_}", [NL_, 128, 2, NL_, 128], BF16)
        I[f"dftI{L_}"] = din(f"dftI{L_}", [L_ // TG_, 128, 2, NL_, TG_], BF16)
        I[f"hyfeat{L_}"] = din(f"hyfeat{L_}", [c.FEAT, L_])
        I[f"lagt{L_}"] = din(f"lagt{L_}", [128, NL_])
        I[f"lagmask{L_}"] = din(f"lagmask{L_}", [128, NL_])
    for nm in ("ys5T", "ymlT", "yhyT"):
        S[nm] = dscr(nm, [NB, W, T], BF16)

    ps = PS(kb, stack)
    ident = stack.enter_context(sbt(nc, "ident_sb", [128, 128], F32))
    identb = stack.enter_context(sbt(nc, "identb_sb", [128, 128], BF16))
    kb.dma("sp", ident[:], I["ident"], W=["ident"])
    kb.op("dve", lambda h: h.tensor_copy(out=identb[:], in_=ident[:]), R=["ident"], W=["identb"])
    modcol = stack.enter_context(sbt(nc, "modcol", [128, L * 6 * KD * 3], F32))
    modcol_v = modcol[:].rearrange("p (l m k r) -> p l m k r", l=L, m=6, k=KD, r=3)

    TCs = min(256, c.CTX)
    I["ciota"] = din("ciota", [128, 2, TCs])
    I["cmask"] = din("cmask", [128, 2])
    I["qmask"] = din("qmask", [128, 4])
    I["negmask"] = din("negmask", [128, 2, 128])
    I["dirmask"] = din("dirmask", [128, 2])
    negmask = stack.enter_context(sbt(nc, "negmask_sb", [128, 2, 128], F32))
    dirmask = stack.enter_context(sbt(nc, "dirmask_sb", [128, 2], F32))
    kb.dma("sp", negmask[:], I["negmask"], W=["negmask"])
    kb.dma("sp", dirmask[:], I["dirmask"], W=["dirmask"])
    qmask = stack.enter_context(sbt(nc, "qmask_sb", [128, 4], F32))
    kb.dma("sp", qmask[:], I["qmask"], W=["qmask"])
    ciota = stack.enter_context(sbt(nc, "ciota_sb", [128, 2, TCs], F32))
    cmask = stack.enter_context(sbt(nc, "cmask_sb", [128, 2], F32))
    kb.dma("sp", ciota[:], I["ciota"], W=["ciota"])
    kb.dma("sp", cmask[:], I["cmask"], W=["cmask"])
    ctxo = dict(c=c, nc=nc, kb=kb, ciota=ciota, cmask=cmask, qmask=qmask, negmask=negmask, dirmask=dirmask, I=I, S=S, ps=ps, ident=ident, identb=identb, modcol=modcol_v, out=out)

    for l in range(L):
        xsrc = I["xin"] if l == 0 else S["xres"]
        last = (l == L - 1)
        if "A" in phases:
            phase_A(ctxo, l)
        if "P" in phases:
            phase_P(ctxo, l, xsrc)
        if "S5" in phases:
            phase_S5(ctxo, l)
        if "ML" in phases:
            phase_ML(ctxo, l)
        if "HY" in phases:
            phase_HY(ctxo, l, last)
        if "M" in phases:
            phase_M(ctxo, l, xsrc)
        if "F" in phases:
            phase_F(ctxo, l, last)
    kb.finish()
    stack.close()
    return nc


def row_of(c, b, tok):
    return 2 if tok < c.CTX else b


def tok_tiles(c, step=512):
    return tgroups(0, c.CTX, step) + tgroups(c.CTX, c.T, step)


def phase_A(o, l):
    c, nc, kb, I, S, ps = o["c"], o["nc"], o["kb"], o["I"], o["S"], o["ps"]
    D, KD = c.D, c.KD
    with kb.scope() as st:
        crow = st.enter_context(sbt(nc, "A_crow", [3, D], F32))
        sig = st.enter_context(sbt(nc, "A_sig", [3, D], F32))
        scT = st.enter_context(sbt(nc, "A_scT", [128, KD, 3], F32))
        mrow = st.enter_context(sbt(nc, "A_mrow", [3, 6 * D], F32))
        brow = st.enter_context(sbt(nc, "A_brow", [3, 6 * D], F32))
        wp = Pool(kb, st, "A_w", 2, [128, KD, 512], F32)
        kb.dma("sp", crow[:], I["cloc"], W=["A_crow"])
        kb.dma("sp", brow[:], I["b_mod"][l:l + 1, :].broadcast_to([3, 6 * D]), W=["A_brow"])
        kb.op("act", lambda h: h.activation(out=sig[:], in_=crow[:], func=AF.Sigmoid), R=["A_crow"], W=["A_sig"])
        kb.op("dve", lambda h: h.tensor_tensor(out=sig[:], in0=sig[:], in1=crow[:], op=ALU.mult),
              R=["A_crow", "A_sig"], W=["A_sig"])
        pt, pk = ps.get()
        for k in range(KD):
            kb.op("pe", lambda h, k=k: h.transpose(out=pt[:, k * 3:(k + 1) * 3], in_=sig[:, k * 128:(k + 1) * 128],
                                                   identity=o["ident"][0:3, 0:3]), R=["A_sig", "ident"], W=[pk])
        kb.op("dve", lambda h: h.tensor_copy(out=scT[:].rearrange("p k r -> p (k r)"), in_=pt[:, 0:KD * 3]),
              R=[pk], W=["A_scT"])
        wv = I["w_mod"][l].rearrange("(k p) n -> p k n", p=128)
        for gi, (n0, n1) in enumerate(tgroups(0, 6 * D)):
            wt, wk = wp.get()
            kb.dma("sp", wt[:, :, 0:n1 - n0], wv[:, :, n0:n1], W=[wk])
            pt, pk = ps.get()
            for k in range(KD):
                kb.op("pe", lambda h, k=k: h.matmul(out=pt[0:3, 0:n1 - n0], lhsT=scT[:, k, :], rhs=wt[:, k, 0:n1 - n0],
                                                    start=(k == 0), stop=(k == KD - 1)), R=["A_scT", wk], W=[pk])
            kb.op("dve", lambda h: h.tensor_tensor(out=mrow[:, n0:n1], in0=pt[0:3, 0:n1 - n0], in1=brow[:, n0:n1],
                                                   op=ALU.add), R=[pk, "A_brow"], W=["A_mrow"])
        kb.dma("sp", S["modrow"][l], mrow[:], R=["A_mrow"], W=[("modrow", l)])
        nblk = 6 * KD
        mc = o["modcol"]
        for b0 in range(0, nblk, 128):
            b1 = min(nblk, b0 + 128)
            pt, pk = ps.get()
            for j in range(b0, b1):
                kb.op("pe", lambda h, j=j: h.transpose(out=pt[:, (j - b0) * 3:(j - b0 + 1) * 3],
                                                       in_=mrow[:, j * 128:(j + 1) * 128],
                                                       identity=o["ident"][0:3, 0:3]), R=["A_mrow", "ident"], W=[pk])
            m0, m1 = b0 // KD, b1 // KD
            kb.op("dve", lambda h: h.tensor_copy(out=mc[:, l, m0:m1].rearrange("p m k r -> p (m k r)"),
                                                 in_=pt[:, 0:(b1 - b0) * 3]), R=[pk], W=[("modcol", l)])
        for m in (1, 4):
            kb.op("dve", lambda h, m=m: h.tensor_scalar_add(out=mc[:, l, m].rearrange("p k r -> p (k r)"),
                                                            in0=mc[:, l, m].rearrange("p k r -> p (k r)"), scalar1=1.0),
                  R=[("modcol", l)], W=[("modcol", l)])


def build_hT(o, st, l, b, xsrc, mshift, mscale, name):
    c, nc, kb, ps = o["c"], o["nc"], o["kb"], o["ps"]
    D, KD, T = c.D, c.KD, c.T
    hT = st.enter_context(sbt(nc, name, [128, KD, T], BF16))
    with kb.scope() as st2:
        xp = Pool(kb, st2, name + "_x", 2, [128, 4, D], F32)
        for (t0, t1) in tok_tiles(c):
            nt = (t1 - t0) // 128
            r = row_of(c, b, t0)
            xt, xk = xp.get()
            kb.dma("sp", xt[:, 0:nt, :], xsrc[b, t0:t1, :].rearrange("(n p) d -> p n d", p=128),
                   R=[("xsrc", l, b, tt_) for tt_ in range(t0 // 128, t1 // 128)], W=[xk])
            for kd in range(KD):
                pt, pk = ps.get()
                for n in range(nt):
                    kb.op("pe", lambda h, n=n, kd=kd: h.transpose(out=pt[:, n * 128:(n + 1) * 128],
                                                                  in_=xt[:, n, kd * 128:(kd + 1) * 128],
                                                                  identity=o["ident"][:]), R=[xk, "ident"], W=[pk])
                kb.op("act", lambda h, kd=kd: h.activation(out=hT[:, kd, t0:t1], in_=pt[:, 0:t1 - t0], func=AF.Identity,
                                                           bias=o["modcol"][:, l, mshift, kd, r:r + 1],
                                                           scale=o["modcol"][:, l, mscale, kd, r:r + 1]),
                      R=[pk, ("modcol", l)], W=[(name, kd)])
    return hT


def phase_P(o, l, xsrc):
    c, nc, kb, I, S, ps = o["c"], o["nc"], o["kb"], o["I"], o["S"], o["ps"]
    D, KD, T, NIN = c.D, c.KD, c.T, c.NIN
    wv = I["w_in"][l].rearrange("(k p) n -> p k n", p=128)
    segs = [(0, c.o_gt), (c.o_gt, c.o_u), (c.o_u, NIN)]
    for b in range(c.NB):
        with kb.scope() as st:
            hT = build_hT(o, st, l, b, xsrc, 0, 1, "P_hT")
            wp = Pool(kb, st, "P_w", 3, [128, KD, 512], BF16)
            sp_ = Pool(kb, st, "P_stg", 3, [128, T], F32)
            for (s0, s1) in segs:
                for (g0, g1) in tgroups(s0, s1, 512):
                    wt, wk = wp.get()
                    kb.dma("pool", wt[:, :, 0:g1 - g0], wv[:, :, g0:g1], W=[wk])
                    for (c0, c1) in tgroups(g0, g1, 128):
                        m = c1 - c0
                        stg, sk = sp_.get()
                        for (t0, t1) in tgroups(0, T):
                            pt, pk = ps.get()
                            for k in range(KD):
                                kb.op("pe", lambda h, k=k: h.matmul(out=pt[0:m, 0:t1 - t0], lhsT=wt[:, k, c0 - g0:c1 - g0],
                                                                    rhs=hT[:, k, t0:t1], start=(k == 0), stop=(k == KD - 1)),
                                      R=[wk, ("P_hT", k)], W=[pk])
                            eng = "act" if ((t0 // 512) % 2 == 0) else "dve"
                            if eng == "act":
                                kb.op("act", lambda h: h.activation(out=stg[0:m, t0:t1], in_=pt[0:m, 0:t1 - t0], func=AF.Identity),
                                      R=[pk], W=[sk])
                            else:
                                kb.op("dve", lambda h: h.tensor_copy(out=stg[0:m, t0:t1], in_=pt[0:m, 0:t1 - t0]), R=[pk], W=[sk])
                        kb.dma("sp", S["zT"][b, c0:c1, :], stg[0:m, :], R=[sk], W=[("zT", b, c0)])


def ln_epilogue(o, st, name):
    c, nc, kb = o["c"], o["nc"], o["kb"]
    D = c.D
    nst = (D + 511) // 512
    stats = Pool(kb, st, name + "_st", 2, [128, nst, 6], F32)
    mv = Pool(kb, st, name + "_mv", 2, [128, 4], F32)

    def fn(rt, rkey, grow, gkey, brow, bkey, stores):
        stt, stk = stats.get()
        mvt, mvk = mv.get()
        for i in range(nst):
            kb.op("dve", lambda h, i=i: h.bn_stats(out=stt[:, i, :], in_=rt[:, i * 512:min(D, (i + 1) * 512)]), R=[rkey], W=[stk])
        kb.op("dve", lambda h: h.bn_aggr(out=mvt[:, 0:2], in_=stt[:].rearrange("p n s -> p (n s)")), R=[stk], W=[mvk])
        kb.op("dve", lambda h: h.tensor_scalar_add(out=mvt[:, 2:3], in0=mvt[:, 1:2], scalar1=LN_EPS), R=[mvk], W=[mvk])
        kb.op("act", lambda h: h.activation(out=mvt[:, 2:3], in_=mvt[:, 2:3], func=AF.Ln), R=[mvk], W=[mvk])
        kb.op("act", lambda h: h.activation(out=mvt[:, 2:3], in_=mvt[:, 2:3], func=AF.Exp, scale=-0.5), R=[mvk], W=[mvk])
        kb.op("dve", lambda h: h.scalar_tensor_tensor(out=mvt[:, 3:4], in0=mvt[:, 0:1], scalar=-1.0, in1=mvt[:, 2:3], op0=ALU.mult, op1=ALU.mult),
              R=[mvk], W=[mvk])
        kb.op("act", lambda h: h.activation(out=rt, in_=rt, func=AF.Identity, bias=mvt[:, 3:4], scale=mvt[:, 2:3]), R=[rkey, mvk], W=[rkey])
        kb.op("dve", lambda h: h.tensor_tensor(out=rt, in0=rt, in1=grow, op=ALU.mult), R=[rkey, gkey], W=[rkey])
        kb.op("dve", lambda h: h.tensor_tensor(out=rt, in0=rt, in1=brow, op=ALU.add), R=[rkey, bkey], W=[rkey])
        for (dst, dkey, src) in stores:
            kb.dma("sp", dst, src, R=[rkey], W=[dkey])
    return fn


def load_rows(o, st, name, srcs):
    nc, kb, c = o["nc"], o["kb"], o["c"]
    outl = []
    for i, (src, rkeys) in enumerate(srcs):
        t = st.enter_context(sbt(nc, f"{name}{i}", [128, c.D], F32))
        kb.dma("sp", t[:], src.broadcast_to([128, c.D]), R=rkeys, W=[(name, i)])
        outl.append((t, (name, i)))
    return outl


def phase_M(o, l, xsrc):
    c, nc, kb, I, S, ps = o["c"], o["nc"], o["kb"], o["I"], o["S"], o["ps"]
    D, KD, T, W, KW = c.D, c.KD, c.T, c.W, c.KW
    TM = 384
    last = (l == c.DEPTH - 1)
    brs = (("yhyT", "w_hy_out"), ("ymlT", "w_ml_out"), ("ys5T", "w_s5_out"))
    with kb.scope() as st:
        wbr = []
        for i, (yn, wn) in enumerate(brs):
            wt = st.enter_context(sbt(nc, f"M_wbr{i}", [128, KW, D], BF16))
            kb.dma("pool", wt[:], I[wn][l].rearrange("(k p) n -> p k n", p=128), W=[("M_wbr", i)])
            wbr.append(wt)
        wop = Pool(kb, st, "M_wo", 2, [128, KD, 512], BF16)
        wov = I["w_out"][l].rearrange("(k p) n -> p k n", p=128)
        rows = load_rows(o, st, "M_row", [(I["ln1_g"][l:l + 1, :], []), (I["ln1_b"][l:l + 1, :], [])])
        growp = Pool(kb, st, "M_grow", 1, [128, D], F32)
        cur_r = [None, None, None]
        ln = ln_epilogue(o, st, "M_ln")
        yp = Pool(kb, st, "M_y", 1, [128, 3, KW, TM], BF16)
        gp = Pool(kb, st, "M_g", 2, [128, 3, TM], F32)
        tp = Pool(kb, st, "M_t", 3, [128, 512], F32)
        mT = st.enter_context(sbt(nc, "M_mT", [128, KD, TM], BF16))
        xp = Pool(kb, st, "M_x", 1, [128, TM // 128, D], F32)
        for b in range(c.NB):
            for (t0, t1) in tok_tiles(c, TM):
                if last and t1 <= c.CTX:
                    continue
                n = t1 - t0
                nt = n // 128
                r = row_of(c, b, t0)
                if cur_r[0] != r:
                    gtile, gkey = growp.get()
                    kb.dma("sp", gtile[:], S["modrow"][l, r:r + 1, 2 * D:3 * D].broadcast_to([128, D]), R=[("modrow", l)], W=[gkey])
                    cur_r[0], cur_r[1], cur_r[2] = r, gtile, gkey
                yt, yk = yp.get()
                for i, (yn, wn) in enumerate(brs):
                    kb.dma("sp", yt[:, i, :, 0:n], S[yn][b, :, t0:t1].rearrange("(k p) t -> p k t", p=128),
                           R=[(yn, b, k_) for k_ in range(KW)], W=[yk])
                xt, xk = xp.get()
                kb.dma("sp", xt[:, 0:nt, :], xsrc[b, t0:t1, :].rearrange("(n p) d -> p n d", p=128), R=[("xsrc", l, b, tt_) for tt_ in range(t0 // 128, t1 // 128)], W=[xk])
                for kd in range(KD):
                    gt, gk = gp.get()
                    sgt, sgk = gt, gk
                    kb.dma("sp", gt[:, :, 0:n],
                           S["zT"][b, c.o_mg:c.o_mg + 3 * D, t0:t1].rearrange("(i k p) t -> k p i t", i=3, p=128)[kd],
                           R=[("zT", b, c.o_mg + i * D + kd * 128) for i in range(3)], W=[gk])
                    kb.op("act", lambda h: h.activation(out=gt[:, :, 0:n], in_=gt[:, :, 0:n], func=AF.Sigmoid), R=[gk], W=[gk])
                    tt, tk = tp.get()
                    for i in range(3):
                        pt, pk = ps.get()
                        for k in range(KW):
                            kb.op("pe", lambda h, i=i, k=k: h.matmul(out=pt[:, 0:n], lhsT=wbr[i][:, k, kd * 128:(kd + 1) * 128],
                                                                     rhs=yt[:, i, k, 0:n], start=(k == 0), stop=(k == KW - 1)),
                                  R=[("M_wbr", i), yk], W=[pk])
                        if i == 0:
                            kb.op("dve", lambda h: h.tensor_tensor(out=tt[:, 0:n], in0=pt[:, 0:n], in1=sgt[:, 0, 0:n], op=ALU.mult),
                                  R=[pk, sgk], W=[tk])
                        else:
                            kb.op("dve", lambda h, i=i: h.tensor_tensor(out=sgt[:, i, 0:n], in0=pt[:, 0:n], in1=sgt[:, i, 0:n], op=ALU.mult),
                                  R=[pk, sgk], W=[sgk])
                            if i == 1:
                                kb.op("dve", lambda h: h.tensor_tensor(out=tt[:, 0:n], in0=tt[:, 0:n], in1=sgt[:, 1, 0:n], op=ALU.add),
                                      R=[tk, sgk], W=[tk])
                            else:
                                kb.op("dve", lambda h, kd=kd: h.tensor_tensor(out=mT[:, kd, 0:n], in0=tt[:, 0:n], in1=sgt[:, 2, 0:n], op=ALU.add),
                                      R=[tk, sgk], W=[("M_mT", kd)])
                for gi, (d0, d1) in enumerate(tgroups(0, D)):
                    wt, wk = wop.get()
                    kb.dma("pool", wt[:, :, 0:d1 - d0], wov[:, :, d0:d1], W=[wk])
                    for ci in range(nt):
                        pt, pk = ps.get()
                        for k in range(KD):
                            kb.op("pe", lambda h, k=k, ci=ci: h.matmul(out=pt[:, 0:d1 - d0], lhsT=mT[:, k, ci * 128:(ci + 1) * 128],
                                                                       rhs=wt[:, k, 0:d1 - d0], start=(k == 0), stop=(k == KD - 1)),
                                  R=[("M_mT", k), wk], W=[pk])
                        tt, tk = tp.get()
                        kb.op("dve", lambda h: h.tensor_tensor(out=tt[:, 0:d1 - d0], in0=pt[:, 0:d1 - d0], in1=cur_r[1][:, d0:d1], op=ALU.mult),
                              R=[pk, cur_r[2]], W=[tk])
                        kb.op("dve", lambda h, ci=ci: h.scalar_tensor_tensor(out=xt[:, ci, d0:d1], in0=xt[:, ci, d0:d1], scalar=c.alpha,
                                                                             in1=tt[:, 0:d1 - d0], op0=ALU.mult, op1=ALU.add),
                              R=[xk, tk], W=[xk])
                for ci in range(nt):
                    ta = t0 + ci * 128
                    ln(xt[:, ci, :], xk, rows[0][0][:], rows[0][1], rows[1][0][:], rows[1][1],
                       [(S["x1"][b, ta:ta + 128, :], ("x1", b, ta // 128), xt[:, ci, :])])


def phase_F(o, l, last):
    c, nc, kb, I, S, ps = o["c"], o["nc"], o["kb"], o["I"], o["S"], o["ps"]
    D, KD, T, DFF = c.D, c.KD, c.T, c.DFF
    KF = DFF // 128
    w1v = I["w_ff1"][l].rearrange("(k p) n -> p k n", p=128)
    w2v = I["w_ff2"][l].rearrange("(k p) n -> p k n", p=128)
    with kb.scope() as st:
        rows = load_rows(o, st, "F_row", [(I["ln2_g"][l:l + 1, :], []), (I["ln2_b"][l:l + 1, :], [])])
        growp = Pool(kb, st, "F_grow", 1, [128, D], F32)
        cur_r = [None, None, None]
        ln = ln_epilogue(o, st, "F_ln")
        wp = Pool(kb, st, "F_w", 3, [128, 16, 512], BF16)
        xp = Pool(kb, st, "F_x", 1, [128, 4, D], F32)
        hT = st.enter_context(sbt(nc, "F_hT", [128, KD, 512], BF16))
        hidT = st.enter_context(sbt(nc, "F_hidT", [128, KF, 512], BF16))
        rp = Pool(kb, st, "F_r", 2, [128, 512], F32)
        tp = Pool(kb, st, "F_t", 2, [128, 512], F32)
        for b in range(c.NB):
            for (t0, t1) in tok_tiles(c):
                if last and t1 <= c.CTX:
                    continue
                n = t1 - t0
                nt = n // 128
                r = row_of(c, b, t0)
                if cur_r[0] != r:
                    gtile, gkey = growp.get()
                    kb.dma("sp", gtile[:], S["modrow"][l, r:r + 1, 5 * D:6 * D].broadcast_to([128, D]), R=[("modrow", l)], W=[gkey])
                    cur_r[0], cur_r[1], cur_r[2] = r, gtile, gkey
                xt, xk = xp.get()
                kb.dma("sp", xt[:, 0:nt, :], S["x1"][b, t0:t1, :].rearrange("(n p) d -> p n d", p=128), R=[("x1", b, tt_) for tt_ in range(t0 // 128, t1 // 128)], W=[xk])
                for kd in range(KD):
                    pt, pk = ps.get()
                    for ci in range(nt):
                        kb.op("pe", lambda h, ci=ci, kd=kd: h.transpose(out=pt[:, ci * 128:(ci + 1) * 128], in_=xt[:, ci, kd * 128:(kd + 1) * 128],
                                                                        identity=o["ident"][:]), R=[xk, "ident"], W=[pk])
                    kb.op("act", lambda h, kd=kd: h.activation(out=hT[:, kd, 0:n], in_=pt[:, 0:n], func=AF.Identity,
                                                               bias=o["modcol"][:, l, 3, kd, r:r + 1], scale=o["modcol"][:, l, 4, kd, r:r + 1]),
                          R=[pk, ("modcol", l)], W=[("F_hT", kd)])
                for (f0, f1) in tgroups(0, DFF, 512):
                    wt, wk = wp.get()
                    kb.dma("pool", wt[:, 0:KD, 0:f1 - f0], w1v[:, :, f0:f1], W=[wk])
                    for (c0, c1) in tgroups(f0, f1, 128):
                        pt, pk = ps.get()
                        for k in range(KD):
                            kb.op("pe", lambda h, k=k: h.matmul(out=pt[:, 0:n], lhsT=wt[:, k, c0 - f0:c1 - f0], rhs=hT[:, k, 0:n],
                                                                start=(k == 0), stop=(k == KD - 1)), R=[wk, ("F_hT", k)], W=[pk])
                        rt, rk = rp.get()
                        kb.op("act", lambda h: h.activation(out=rt[:, 0:n], in_=pt[:, 0:n], func=AF.Relu), R=[pk], W=[rk])
                        kb.op("dve", lambda h, c0=c0: h.tensor_tensor(out=hidT[:, c0 // 128, 0:n], in0=rt[:, 0:n], in1=rt[:, 0:n], op=ALU.mult),
                              R=[rk], W=[("F_hidT", c0 // 128)])
                for (d0, d1) in tgroups(0, D):
                    banks = [ps.get() for _ in range(nt)]
                    for q0 in range(0, KF, 16):
                        wt, wk = wp.get()
                        kq = min(16, KF - q0)
                        kb.dma("pool", wt[:, 0:kq, 0:d1 - d0], w2v[:, q0:q0 + kq, d0:d1], W=[wk])
                        for ci in range(nt):
                            pt, pk = banks[ci]
                            for k in range(kq):
                                kb.op("pe", lambda h, k=k, ci=ci: h.matmul(out=pt[:, 0:d1 - d0], lhsT=hidT[:, q0 + k, ci * 128:(ci + 1) * 128],
                                                                           rhs=wt[:, k, 0:d1 - d0], start=(q0 + k == 0), stop=(q0 + k == KF - 1)),
                                      R=[("F_hidT", q0 + k), wk], W=[pk])
                    for ci in range(nt):
                        pt, pk = banks[ci]
                        tt, tk = tp.get()
                        kb.op("dve", lambda h: h.tensor_tensor(out=tt[:, 0:d1 - d0], in0=pt[:, 0:d1 - d0], in1=cur_r[1][:, d0:d1], op=ALU.mult),
                              R=[pk, cur_r[2]], W=[tk])
                        kb.op("dve", lambda h, ci=ci: h.scalar_tensor_tensor(out=xt[:, ci, d0:d1], in0=xt[:, ci, d0:d1], scalar=c.alpha,
                                                                             in1=tt[:, 0:d1 - d0], op0=ALU.mult, op1=ALU.add),
                              R=[xk, tk], W=[xk])
                for ci in range(nt):
                    ta = t0 + ci * 128
                    if last:
                        stores = [(o["out"][b, ta - c.CTX:ta - c.CTX + 128, :], ("out", b), xt[:, ci, :])]
                    else:
                        stores = [(S["xres"][b, ta:ta + 128, :], ("xsrc", l + 1, b, ta // 128), xt[:, ci, :])]
                    ln(xt[:, ci, :], xk, rows[0][0][:], rows[0][1], rows[1][0][:], rows[1][1], stores)


def load_cols(o, dst, vec, key, n):
    o["kb"].dma("sp", dst, vec.rearrange("(k p) -> p k", p=128), W=[key], allow_slow_non_contiguous=True)


TWO_PI = 2.0 * math.pi


def range_reduce(kb, x, kf, ki, keys):
    R = list(keys)
    kb.op("dve", lambda h: h.tensor_scalar_mul(out=kf, in0=x, scalar1=1.0 / TWO_PI), R=R, W=R)
    kb.op("dve", lambda h: h.tensor_copy(out=ki, in_=kf), R=R, W=R)
    kb.op("dve", lambda h: h.tensor_copy(out=kf, in_=ki), R=R, W=R)
    kb.op("dve", lambda h: h.scalar_tensor_tensor(out=x, in0=kf, scalar=-TWO_PI, in1=x, op0=ALU.mult, op1=ALU.add), R=R, W=R)
    kb.op("dve", lambda h: h.tensor_scalar(out=kf, in0=x, scalar1=0.0, scalar2=TWO_PI, op0=ALU.is_lt, op1=ALU.mult), R=R, W=R)
    kb.op("dve", lambda h: h.scalar_tensor_tensor(out=x, in0=x, scalar=-math.pi, in1=kf, op0=ALU.add, op1=ALU.add), R=R, W=R)
    kb.op("dve", lambda h: h.tensor_scalar(out=x, in0=x, scalar1=-math.pi, scalar2=math.pi, op0=ALU.max, op1=ALU.min), R=R, W=R)


def phase_S5(o, l):
    c, nc, kb, I, S, ps = o["c"], o["nc"], o["kb"], o["I"], o["S"], o["ps"]
    T, W, KW, NB = c.T, c.W, c.KW, c.NB
    NT = c.G // 2
    TC = min(256, c.CTX)
    NC = T // TC
    NCC = c.CTX // TC
    n2 = 2 * NT
    ident = o["ident"]
    with kb.scope() as st:
        wk_ = st.enter_context(sbt(nc, "S_wk", [128, 12, n2], F32))
        BTz = st.enter_context(sbt(nc, "S_BTz", [128, 2, 2, KW, 4, 128], BF16))
        CTz = st.enter_context(sbt(nc, "S_CTz", [128, 2, 2, KW, 4, 128], BF16))
        rri = st.enter_context(sbt(nc, "S_rri", [128, 1, max(n2, 2 * min(256, c.CTX))], I32))
        rrf = st.enter_context(sbt(nc, "S_rrf", [128, 2 * min(256, c.CTX)], F32))
        spc = kb.scope()
        sp = spc.__enter__()
        ari = sp.enter_context(sbt(nc, "S_ari", [n2, 3, 128], F32))
        lds = sp.enter_context(sbt(nc, "S_lds", [n2, 2], F32))
        prm = sp.enter_context(sbt(nc, "S_prm", [128, 3, n2], F32))
        kb.dma("sp", ari[:, 0, :], I["s5_a_re"][l].rearrange("d (t g) p -> (d t) (g p)", g=2), W=["S_ari"])
        kb.dma("sp", ari[:, 1, :], I["s5_a_im"][l].rearrange("d (t g) p -> (d t) (g p)", g=2), W=["S_ari"])
        kb.dma("sp", lds[:], I["s5_log_dt"][l].rearrange("d (t g) -> (d t) g", g=2), W=["S_lds"])
        kb.op("dve", lambda h: h.tensor_copy(out=ari[:, 2, :].rearrange("r (g p) -> r g p", g=2),
                                             in_=lds[:].unsqueeze(2).broadcast_to([n2, 2, 64])), R=["S_lds", "S_ari"], W=["S_ari"])
        for i in range(3):
            pt, pk = ps.get()
            kb.op("pe", lambda h, i=i: h.transpose(out=pt[:, 0:n2], in_=ari[:, i, :], identity=ident[0:n2, 0:n2]), R=["S_ari", "ident"], W=[pk])
            kb.op("dve", lambda h, i=i: h.tensor_copy(out=prm[:, i, :], in_=pt[:, 0:n2]), R=[pk], W=["S_prm"])
        a_re, a_im, ld = prm[:, 0, :], prm[:, 1, :], prm[:, 2, :]
        dt_, mag, ang, thr, sn, cs, abm1, den, t1, t2, co_re, co_im = [wk_[:, i, :] for i in range(12)]
        K_ = "S_wk"

        def dv(fn, R=(K_, "S_prm"), W=(K_,), en="dve"):
            kb.op(en, fn, R=list(R), W=list(W))
        dv(lambda h: h.activation(out=dt_, in_=ld, func=AF.Exp), en="act")
        dv(lambda h: h.tensor_tensor(out=mag, in0=dt_, in1=a_re, op=ALU.mult))
        dv(lambda h: h.activation(out=mag, in_=mag, func=AF.Exp), en="act")
        dv(lambda h: h.tensor_tensor(out=ang, in0=dt_, in1=a_im, op=ALU.mult))
        dv(lambda h: h.tensor_scalar_add(out=thr, in0=ang, scalar1=math.pi + TWO_PI))
        range_reduce(kb, thr, t1, rri[:, 0, 0:n2], [K_, "S_rri"])
        dv(lambda h: h.activation(out=sn, in_=thr, func=AF.Sin), en="act")
        dv(lambda h: h.tensor_scalar_add(out=cs, in0=ang, scalar1=1.5 * math.pi + TWO_PI))
        range_reduce(kb, cs, t1, rri[:, 0, 0:n2], [K_, "S_rri"])
        dv(lambda h: h.activation(out=cs, in_=cs, func=AF.Sin), en="act")
        dv(lambda h: h.tensor_tensor(out=cs, in0=cs, in1=mag, op=ALU.mult))
        dv(lambda h: h.tensor_tensor(out=sn, in0=sn, in1=mag, op=ALU.mult))
        dv(lambda h: h.tensor_scalar_add(out=abm1, in0=cs, scalar1=-1.0))
        dv(lambda h: h.tensor_tensor(out=den, in0=a_re, in1=a_re, op=ALU.mult))
        dv(lambda h: h.tensor_tensor(out=t1, in0=a_im, in1=a_im, op=ALU.mult))
        dv(lambda h: h.tensor_tensor(out=den, in0=den, in1=t1, op=ALU.add))
        dv(lambda h: h.reciprocal(out=den, in_=den))
        dv(lambda h: h.tensor_tensor(out=t1, in0=abm1, in1=a_re, op=ALU.mult))
        dv(lambda h: h.tensor_tensor(out=t2, in0=sn, in1=a_im, op=ALU.mult))
        dv(lambda h: h.tensor_tensor(out=t1, in0=t1, in1=t2, op=ALU.add))
        dv(lambda h: h.tensor_tensor(out=co_re, in0=t1, in1=den, op=ALU.mult))
        dv(lambda h: h.tensor_tensor(out=t1, in0=sn, in1=a_re, op=ALU.mult))
        dv(lambda h: h.tensor_tensor(out=t2, in0=abm1, in1=a_im, op=ALU.mult))
        dv(lambda h: h.tensor_tensor(out=t1, in0=t1, in1=t2, op=ALU.subtract))
        dv(lambda h: h.tensor_tensor(out=co_im, in0=t1, in1=den, op=ALU.mult))
        bnat = sp.enter_context(sbt(nc, "S_bnat", [128, 2, n2, 16], F32))
        bb = sp.enter_context(sbt(nc, "S_bb", [128, 4, n2, 16], F32))
        bblk = sp.enter_context(sbt(nc, "S_bblk", [128, 2, 2, KW, 128], F32))
        BT = sp.enter_context(sbt(nc, "S_BT", [128, 2, 2, KW, 128], BF16))
        CT = sp.enter_context(sbt(nc, "S_CT", [128, 2, 2, KW, 128], BF16))
        for d_ in range(2):
            kb.dma("sp", bnat[:, 0, d_ * NT:(d_ + 1) * NT, :], I["s5_b_re"][l, d_].rearrange("(t g) p c -> (g p) t c", g=2), W=["S_bnat"])
            kb.dma("sp", bnat[:, 1, d_ * NT:(d_ + 1) * NT, :], I["s5_b_im"][l, d_].rearrange("(t g) p c -> (g p) t c", g=2), W=["S_bnat"])
        cor = co_re.unsqueeze(2).broadcast_to([128, n2, 16])
        coi = co_im.unsqueeze(2).broadcast_to([128, n2, 16])
        RB = ["S_bnat", K_, "S_bb"]
        kb.op("dve", lambda h: h.tensor_tensor(out=bb[:, 0], in0=bnat[:, 0], in1=cor, op=ALU.mult), R=RB, W=["S_bb"])
        kb.op("dve", lambda h: h.tensor_tensor(out=bb[:, 1], in0=bnat[:, 1], in1=coi, op=ALU.mult), R=RB, W=["S_bb"])
        kb.op("dve", lambda h: h.tensor_tensor(out=bb[:, 0], in0=bb[:, 0], in1=bb[:, 1], op=ALU.subtract), R=RB, W=["S_bb"])
        kb.op("dve", lambda h: h.tensor_tensor(out=bb[:, 2], in0=bnat[:, 1], in1=cor, op=ALU.mult), R=RB, W=["S_bb"])
        kb.op("dve", lambda h: h.tensor_tensor(out=bb[:, 3], in0=bnat[:, 0], in1=coi, op=ALU.mult), R=RB, W=["S_bb"])
        kb.op("dve", lambda h: h.tensor_tensor(out=bb[:, 2], in0=bb[:, 2], in1=bb[:, 3], op=ALU.add), R=RB, W=["S_bb"])
        kb.op("pool", lambda h: h.memset(bblk[:], 0.0), W=["S_bblk"])
        for ri, src in ((0, 0), (1, 2)):
            for g in range(2):
                pr = slice(64 * g, 64 * g + 64)
                kb.op("dve", lambda h, ri=ri, src=src, g=g, pr=pr: h.tensor_copy(
                    out=bblk[pr, ri].rearrange("p d k (q g c) -> p d k q g c", q=4, g=2)[:, :, :, :, g, :],
                    in_=bb[pr, src].rearrange("p (d k q) c -> p d k q c", d=2, q=4)), R=["S_bb", "S_bblk"], W=["S_bblk"])
        for ri in range(2):
            for d in range(2):
                for k in range(KW):
                    pt, pk = ps.get()
                    kb.op("pe", lambda h, ri=ri, d=d, k=k: h.transpose(out=pt[:, 0:128], in_=bblk[:, ri, d, k, :], identity=ident[:]),
                          R=["S_bblk", "ident"], W=[pk])
                    kb.op("act", lambda h, ri=ri, d=d, k=k: h.activation(out=BT[:, ri, d, k, :], in_=pt[:, 0:128], func=AF.Identity), R=[pk], W=["S_BT"])
        cnat = sp.enter_context(sbt(nc, "S_cnat", [128, 2, 2, KW, 64], F32))
        kb.dma("sp", cnat[:, 0], I["s5_c_re"][l].rearrange("d (k g) c p -> (g c) d k p", g=8), W=["S_cnat"])
        kb.dma("sp", cnat[:, 1], I["s5_c_im"][l].rearrange("d (k g) c p -> (g c) d k p", g=8), W=["S_cnat"])
        cmask = o["cmask"]
        for ri in range(2):
            for g in range(2):
                kb.op("dve", lambda h, ri=ri, g=g: h.tensor_scalar(
                    out=bblk[:, ri].rearrange("p d k (g m) -> p d k g m", g=2)[:, :, :, g, :],
                    in0=cnat[:, ri], scalar1=cmask[:, g:g + 1], scalar2=None, op0=ALU.mult), R=["S_cnat", "S_bblk", "cmask"], W=["S_bblk"])
        for ri in range(2):
            for d in range(2):
                for k in range(KW):
                    pt, pk = ps.get()
                    kb.op("pe", lambda h, ri=ri, d=d, k=k: h.transpose(out=pt[:, 0:128], in_=bblk[:, ri, d, k, :], identity=ident[:]),
                          R=["S_bblk", "ident"], W=[pk])
                    kb.op("act", lambda h, ri=ri, d=d, k=k: h.activation(out=CT[:, ri, d, k, :], in_=pt[:, 0:128], func=AF.Identity), R=[pk], W=["S_CT"])
        kb.op("pool", lambda h: h.memset(CTz[:].rearrange("p a b k q m -> p (a b k q m)"), 0.0), W=["S_CTz"])
        for q in range(4):
            for ri in range(2):
                kb.op("dve", lambda h, q=q, ri=ri: h.tensor_scalar(out=BTz[:, ri, :, :, q, :], in0=BT[:, ri], scalar1=o["qmask"][:, q:q + 1], scalar2=None, op0=ALU.mult),
                      R=["S_BT", "qmask", "S_BTz"], W=["S_BTz"])
                kb.op("dve", lambda h, q=q, ri=ri: h.tensor_copy(out=CTz[:, ri, :, :, q, 32 * q:32 * q + 32], in_=CT[:, ri, :, :, 32 * q:32 * q + 32]),
                      R=["S_CT", "S_CTz"], W=["S_CTz"])
        spc.__exit__(None, None, None)
        dcol = st.enter_context(sbt(nc, "S_dcol", [128, KW], F32))
        gbcol = st.enter_context(sbt(nc, "S_gbcol", [128, KW], F32))
        load_cols(o, dcol[:], I["s5_d"][l], "S_dcol", KW)
        load_cols(o, gbcol[:], I["s5_glu_b"][l], "S_gbcol", KW)
        gw = st.enter_context(sbt(nc, "S_gw", [128, KW, W], BF16))
        kb.dma("pool", gw[:], I["s5_glu_w"][l].rearrange("(k p) n -> p k n", p=128), W=["S_gw"])
        smc = kb.scope()
        sm = smc.__enter__()
        uT = [sm.enter_context(sbt(nc, f"S_u{b}", [128, T], BF16)) for b in range(NB)]
        yacc = [sm.enter_context(sbt(nc, f"S_y{b}", [128, T], F32)) for b in range(NB)]
        wre = [sm.enter_context(sbt(nc, f"S_wre{b}", [128, T], F32)) for b in range(NB)]
        wim = [sm.enter_context(sbt(nc, f"S_wim{b}", [128, T], F32)) for b in range(NB)]
        zop = Pool(kb, sm, "S_zo", 2, [128, T], BF16)
        tmpp = Pool(kb, sm, "S_tmp", 8, [128, 512], F32)
        pr_ = [sm.enter_context(sbt(nc, f"S_p{i}", [128, T], BF16)) for i in range(4)]
        tab = sm.enter_context(sbt(nc, "S_tab", [128, 2, TC], F32))
        tabs = sm.enter_context(sbt(nc, "S_tabs", [128, 2, TC], F32))
        xst = [sm.enter_context(sbt(nc, f"S_xst{b}", [128, 4], F32)) for b in range(NB)]
        ebf = sm.enter_context(sbt(nc, "S_ebf", [128, 3, T], BF16))
        zbf = sm.enter_context(sbt(nc, "S_zbf", [128, 2, T], BF16))
        iota = o["ciota"]
        for kc in range(KW):
            for b in range(NB):
                kb.dma("pool", uT[b][:], S["zT"][b, c.o_u + kc * 128:c.o_u + (kc + 1) * 128, :], R=[("zT", b, c.o_u + kc * 128)], W=[("S_u", b)])
            for d in range(2):
                for q in range(4):
                    ti = kc * 4 + q
                    col = d * NT + ti
                    th = thr[:, col:col + 1]
                    kb.op("dve", lambda h, d=d, th=th: h.tensor_scalar(out=tabs[:, 0, :], in0=iota[:, d, :], scalar1=th, scalar2=1.5 * math.pi + 130 * TWO_PI,
                                                                       op0=ALU.mult, op1=ALU.add), R=[K_, "ciota", "S_tabs"], W=["S_tabs"])
                    kb.op("dve", lambda h, d=d, th=th: h.tensor_scalar(out=tabs[:, 1, :], in0=iota[:, d, :], scalar1=th, scalar2=-1.0,
                                                                       op0=ALU.mult, op1=ALU.mult), R=[K_, "ciota", "S_tabs"], W=["S_tabs"])
                    kb.op("dve", lambda h: h.tensor_scalar_add(out=tabs[:, 1, :], in0=tabs[:, 1, :], scalar1=math.pi + 130 * TWO_PI), R=["S_tabs"], W=["S_tabs"])
                    range_reduce(kb, tabs[:].rearrange("p a t -> p (a t)"), rrf[:, 0:2 * TC], rri[:, 0, 0:2 * TC], ["S_tabs", "S_rri", "S_rrf"])
                    kb.op("act", lambda h: h.activation(out=tab[:], in_=tabs[:], func=AF.Sin), R=["S_tabs", "S_tab"], W=["S_tab"])
                    Er, Ei = tab[:, 0, :], tab[:, 1, :]
                    for i_, (src_, sc_) in enumerate(((Er, 1.0), (Ei, 1.0), (Er, -1.0))):
                        kb.op("act", lambda h, i_=i_, src_=src_, sc_=sc_: h.activation(out=ebf[:, i_, :].rearrange("p (n t) -> p n t", t=TC),
                                                                                       in_=src_.unsqueeze(1).broadcast_to([128, NC, TC]), func=AF.Identity, scale=sc_),
                              R=["S_tab", "S_ebf"], W=["S_ebf"])
                    lastc = TC - 1 if d == 0 else 0
                    rho = mag[:, col:col + 1]
                    if d == 0:
                        order = list(range(NC))
                    else:
                        order = list(range(NCC - 1, -1, -1)) + list(range(NC - 1, NCC - 1, -1))
                    rhob = rho.broadcast_to([128, TC])
                    v3 = lambda ap: ap.rearrange("p (n t) -> p n t", t=TC)

                    def stage_w(b):
                        kwr, kwi = ("S_wre", b), ("S_wim", b)
                        for (g0, g1) in tgroups(0, T, 1024):
                            n = g1 - g0
                            pre = [ps.get() for _ in range((n + 511) // 512)]
                            pim = [ps.get() for _ in range((n + 511) // 512)]
                            for ri, banks in ((0, pre), (1, pim)):
                                for j, (pt, pk) in enumerate(banks):
                                    a0 = g0 + j * 512
                                    a1 = min(a0 + 512, g1)
                                    kb.op("pe", lambda h, ri=ri, pt=pt, a0=a0, a1=a1: h.matmul(
                                        out=pt[:, 0:a1 - a0], lhsT=BTz[:, ri, d, kc, q, :], rhs=uT[b][:, a0:a1],
                                        start=True, stop=True), R=["S_BTz", ("S_u", b)], W=[pk])
                            for j in range(len(pre)):
                                a0 = g0 + j * 512
                                a1 = min(a0 + 512, g1)
                                m = a1 - a0
                                nk = m // TC
                                (pr0, pk0), (pi0, pk1) = pre[j], pim[j]
                                Erb = Er.unsqueeze(1).broadcast_to([128, nk, TC])
                                Eib = Ei.unsqueeze(1).broadcast_to([128, nk, TC])
                                (ta, tak), (tb, tbk) = tmpp.get(), tmpp.get()
                                (br, brk), (bi, bik) = tmpp.get(), tmpp.get()
                                ta_, tb_, br_, bi_ = ta[:, 0:m], tb[:, 0:m], br[:, 0:m], bi[:, 0:m]
                                kb.op("act", lambda h: h.activation(out=br_, in_=pr0[:, 0:m], func=AF.Identity), R=[pk0, brk], W=[brk])
                                kb.op("act", lambda h: h.activation(out=bi_, in_=pi0[:, 0:m], func=AF.Identity), R=[pk1, bik], W=[bik])
                                kb.op("dve", lambda h: h.tensor_tensor(out=v3(ta_), in0=v3(bi_), in1=Eib, op=ALU.mult), R=[bik, "S_tab", tak], W=[tak])
                                kb.op("dve", lambda h: h.tensor_tensor(out=v3(wre[b][:, a0:a1]), in0=v3(br_), in1=Erb, op=ALU.mult), R=[brk, "S_tab", kwr], W=[kwr])
                                kb.op("dve", lambda h: h.tensor_tensor(out=wre[b][:, a0:a1], in0=wre[b][:, a0:a1], in1=ta_, op=ALU.subtract), R=[kwr, tak], W=[kwr])
                                kb.op("dve", lambda h: h.tensor_tensor(out=v3(tb_), in0=v3(br_), in1=Eib, op=ALU.mult), R=[brk, "S_tab", tbk], W=[tbk])
                                kb.op("dve", lambda h: h.tensor_tensor(out=v3(wim[b][:, a0:a1]), in0=v3(bi_), in1=Erb, op=ALU.mult), R=[bik, "S_tab", kwi], W=[kwi])
                                kb.op("dve", lambda h: h.tensor_tensor(out=wim[b][:, a0:a1], in0=wim[b][:, a0:a1], in1=tb_, op=ALU.add), R=[kwi, tbk], W=[kwi])

                    def stage_scan(b, oi, ch):
                        kwr, kwi, kx = ("S_wre", b), ("S_wim", b), ("S_xst", b)
                        xs_ = xst[b]
                        sl = slice(ch * TC, (ch + 1) * TC)
                        zr, zi = wre[b][:, sl], wim[b][:, sl]
                        if d == 1:
                            zr, zi = zr[:, ::-1], zi[:, ::-1]
                        if oi == 0:
                            ir, ii = 0.0, 0.0
                        else:
                            ir, ii = xs_[:, 0:1], xs_[:, 1:2]
                        kb.op("dve", lambda h: h.tensor_tensor_scan(out=zr, data0=rhob, data1=zr, initial=ir, op0=ALU.mult, op1=ALU.add), R=[kwr, K_, kx], W=[kwr])
                        kb.op("dve", lambda h: h.tensor_tensor_scan(out=zi, data0=rhob, data1=zi, initial=ii, op0=ALU.mult, op1=ALU.add), R=[kwi, K_, kx], W=[kwi])
                        if oi < len(order) - 1:
                            pos = ch * TC + (TC - 1 if d == 0 else 0)
                            zrl, zil = wre[b][:, pos:pos + 1], wim[b][:, pos:pos + 1]
                            Erl, Eil = Er[:, lastc:lastc + 1], Ei[:, lastc:lastc + 1]
                            kb.op("dve", lambda h: h.tensor_tensor(out=xs_[:, 2:3], in0=zil, in1=Eil, op=ALU.mult), R=[kwi, "S_tab", kx], W=[kx])
                            kb.op("dve", lambda h: h.tensor_tensor(out=xs_[:, 3:4], in0=zrl, in1=Eil, op=ALU.mult), R=[kwr, "S_tab", kx], W=[kx])
                            kb.op("dve", lambda h: h.scalar_tensor_tensor(out=xs_[:, 0:1], in0=zrl, scalar=Erl, in1=xs_[:, 2:3], op0=ALU.mult, op1=ALU.add), R=[kwr, "S_tab", kx], W=[kx])
                            kb.op("dve", lambda h: h.scalar_tensor_tensor(out=xs_[:, 1:2], in0=zil, scalar=Erl, in1=xs_[:, 3:4], op0=ALU.mult, op1=ALU.subtract), R=[kwi, "S_tab", kx], W=[kx])

                    def stage_out(b):
                        kwr, kwi = ("S_wre", b), ("S_wim", b)
                        kb.op("act", lambda h: h.activation(out=zbf[:, 0, :], in_=wre[b][:], func=AF.Identity), R=[kwr, "S_zbf"], W=["S_zbf"])
                        kb.op("act", lambda h: h.activation(out=zbf[:, 1, :], in_=wim[b][:], func=AF.Identity), R=[kwi, "S_zbf"], W=["S_zbf"])
                        for i_, (zi_, ei_) in enumerate(((0, 0), (1, 1), (1, 2), (0, 1))):
                            kb.op("dve", lambda h, i_=i_, zi_=zi_, ei_=ei_: h.tensor_tensor(out=pr_[i_][:], in0=zbf[:, zi_, :], in1=ebf[:, ei_, :], op=ALU.mult),
                                  R=["S_zbf", "S_ebf", ("S_p", i_)], W=[("S_p", i_)])
                        for (a0, a1) in tgroups(0, T):
                            pt, pk = ps.get()
                            for i in range(4):
                                ri = 0 if i < 2 else 1
                                kb.op("pe", lambda h, i=i, ri=ri: h.matmul(out=pt[:, 0:a1 - a0], lhsT=CTz[:, ri, d, kc, q, :], rhs=pr_[i][:, a0:a1],
                                                                           start=(i == 0), stop=(i == 3)), R=["S_CTz", ("S_p", i)], W=[pk])
                            if d == 0 and q == 0:
                                kb.op("act", lambda h: h.activation(out=yacc[b][:, a0:a1], in_=pt[:, 0:a1 - a0], func=AF.Identity), R=[pk, ("S_y", b)], W=[("S_y", b)])
                            else:
                                kb.op("dve", lambda h: h.tensor_tensor(out=yacc[b][:, a0:a1], in0=yacc[b][:, a0:a1], in1=pt[:, 0:a1 - a0], op=ALU.add),
                                      R=[pk, ("S_y", b)], W=[("S_y", b)])

                    for b in range(NB):
                        stage_w(b)
                    for oi, ch in enumerate(order):
                        for b in range(NB):
                            stage_scan(b, oi, ch)
                    for b in range(NB):
                        stage_out(b)
            for b in range(NB):
                ya = yacc[b][:]
                kw_ = ("S_wre", b)
                kb.op("dve", lambda h: h.scalar_tensor_tensor(out=ya, in0=uT[b][:], scalar=dcol[:, kc:kc + 1], in1=ya, op0=ALU.mult, op1=ALU.add),
                      R=[("S_u", b), ("S_y", b), "S_dcol"], W=[("S_y", b)])
                kb.op("dve", lambda h: h.tensor_tensor(out=wre[b][:], in0=ya, in1=ya, op=ALU.mult), R=[("S_y", b), kw_], W=[kw_])
                kb.op("dve", lambda h: h.tensor_scalar(out=wre[b][:], in0=wre[b][:], scalar1=0.044715, scalar2=1.0, op0=ALU.mult, op1=ALU.add), R=[kw_], W=[kw_])
                kb.op("dve", lambda h: h.tensor_tensor(out=wre[b][:], in0=wre[b][:], in1=ya, op=ALU.mult), R=[("S_y", b), kw_], W=[kw_])
                kb.op("act", lambda h: h.activation(out=wre[b][:], in_=wre[b][:], func=AF.Sigmoid, scale=2.0 * math.sqrt(2.0 / math.pi)), R=[kw_], W=[kw_])
                zo, zok = zop.get()
                kb.op("dve", lambda h: h.tensor_tensor(out=zo[:], in0=wre[b][:], in1=ya, op=ALU.mult), R=[("S_y", b), kw_, zok], W=[zok])
                kb.dma("sp", S["s5z"][b, kc * 128:(kc + 1) * 128, :], zo[:], R=[zok], W=[("s5z", b, kc)])
        smc.__exit__(None, None, None)
        zTt = [[st.enter_context(sbt(nc, f"S_z{b}_{k}", [128, T], BF16)) for k in range(KW)] for b in range(NB)]
        for b in range(NB):
            for k in range(KW):
                kb.dma("sp", zTt[b][k][:], S["s5z"][b, k * 128:(k + 1) * 128, :], R=[("s5z", b, k)], W=[("S_z", b, k)])
        op_ = Pool(kb, st, "S_o", 2, [128, T], BF16)
        sg_ = Pool(kb, st, "S_sg", 2, [128, 512], F32)
        for b in range(NB):
            for co in range(KW):
                ot, ok = op_.get()
                for (a0, a1) in tgroups(0, T):
                    pt, pk = ps.get()
                    for k in range(KW):
                        kb.op("pe", lambda h, k=k: h.matmul(out=pt[:, 0:a1 - a0], lhsT=gw[:, k, co * 128:(co + 1) * 128], rhs=zTt[b][k][:, a0:a1],
                                                            start=(k == 0), stop=(k == KW - 1)), R=["S_gw", ("S_z", b, k)], W=[pk])
                    sgt, sgk = sg_.get()
                    kb.op("act", lambda h: h.activation(out=sgt[:, 0:a1 - a0], in_=pt[:, 0:a1 - a0], func=AF.Sigmoid, bias=gbcol[:, co:co + 1]), R=[pk, "S_gbcol"], W=[sgk])
                    kb.op("dve", lambda h: h.tensor_tensor(out=ot[:, a0:a1], in0=sgt[:, 0:a1 - a0], in1=zTt[b][co][:, a0:a1], op=ALU.mult), R=[sgk, ("S_z", b, co)], W=[ok])
                kb.dma("sp", S["ys5T"][b, co * 128:(co + 1) * 128, :], ot[:], R=[ok], W=[("ys5T", b, co)])


def conv3(o, y, x, wcol, bcol, np_, rkeys, ykey):
    c, kb = o["c"], o["kb"]
    CTX, T, GW = c.CTX, c.T, c.GRID_W
    kb.op("act", lambda h: h.activation(out=y[0:np_, :], in_=x[0:np_, :], func=AF.Identity, bias=bcol, scale=wcol[1]), R=rkeys + [ykey], W=[ykey])
    lat = lambda ap: ap[0:np_, CTX:T].rearrange("p (r w) -> p r w", w=GW)
    for (tap, dst0, src0) in ((0, 1, 0), (2, 0, 1)):
        wc = wcol[tap]
        kb.op("dve", lambda h: h.scalar_tensor_tensor(out=y[0:np_, dst0:dst0 + CTX - 1], in0=x[0:np_, src0:src0 + CTX - 1], scalar=wc,
                                                      in1=y[0:np_, dst0:dst0 + CTX - 1], op0=ALU.mult, op1=ALU.add), R=rkeys + [ykey], W=[ykey])
        kb.op("dve", lambda h: h.scalar_tensor_tensor(out=lat(y)[:, :, dst0:dst0 + GW - 1], in0=lat(x)[:, :, src0:src0 + GW - 1], scalar=wc,
                                                      in1=lat(y)[:, :, dst0:dst0 + GW - 1], op0=ALU.mult, op1=ALU.add), R=rkeys + [ykey], W=[ykey])


def phase_ML(o, l):
    c, nc, kb, I, S, ps = o["c"], o["nc"], o["kb"], o["I"], o["S"], o["ps"]
    T, W, KW, NB, H, DH, DT, NCH, CTX = c.T, c.W, c.KW, c.NB, c.H, c.DH, c.DT, c.NCH, c.CTX
    NCC = CTX // 128
    H2 = 2 * H
    ident, identb = o["ident"], o["identb"]
    DV = DH + 1
    with kb.scope() as st:
        cw = st.enter_context(sbt(nc, "ML_cw", [DT, 3, 4 * H], F32))
        cb = st.enter_context(sbt(nc, "ML_cb", [DT, 4 * H], F32))
        for t_ in range(3):
            kb.dma("sp", cw[:, t_, :], I["ml_conv_w"][l, t_, :].rearrange("(m p) -> p m", p=DT), W=["ML_cw"], allow_slow_non_contiguous=True)
        kb.dma("sp", cb[:], I["ml_conv_b"][l].rearrange("(m p) -> p m", p=DT), W=["ML_cb"], allow_slow_non_contiguous=True)
        ngcol = st.enter_context(sbt(nc, "ML_ng", [128, KW], F32))
        load_cols(o, ngcol[:], I["ml_norm_g"][l], "ML_ng", KW)
        gbi = st.enter_context(sbt(nc, "ML_gbi", [H2, 1], F32))
        gbf = st.enter_context(sbt(nc, "ML_gbf", [H2, 1], F32))
        for d in range(2):
            kb.dma("sp", gbi[d * H:(d + 1) * H, :], I["ml_gate_b"][l, 2 * d, :].rearrange("(h o) -> h o", o=1), W=["ML_gb"])
            kb.dma("sp", gbf[d * H:(d + 1) * H, :], I["ml_gate_b"][l, 2 * d + 1, :].rearrange("(h o) -> h o", o=1), W=["ML_gb"])
        negm = o["negmask"]
        dirm = o["dirmask"]
        for b in range(NB):
            with kb.scope() as sb_:
                colA = sb_.enter_context(sbt(nc, "ML_colA", [128, NCH, H2], F32))
                expm = sb_.enter_context(sbt(nc, "ML_expm", [128, NCH, H2], F32))
                vtok = sb_.enter_context(sbt(nc, "ML_vtok", [128, NCH, H, DV], BF16))
                hn = sb_.enter_context(sbt(nc, "ML_hn", [128, NCH, W], BF16))
                with kb.scope() as sg:
                    Gi = sg.enter_context(sbt(nc, "ML_Gi", [H2, T], F32))
                    Gf = sg.enter_context(sbt(nc, "ML_Gf", [H2, T], F32))
                    Fn = sg.enter_context(sbt(nc, "ML_Fn", [H2, T], F32))
                    Fr = sg.enter_context(sbt(nc, "ML_Fr", [H2, T], F32))
                    ones = sg.enter_context(sbt(nc, "ML_ones", [H2, T], F32))
                    for d in range(2):
                        kb.dma("sp", Gi[d * H:(d + 1) * H, :], S["zT"][b, c.o_gt + 2 * d * H:c.o_gt + (2 * d + 1) * H, :], R=[("zT", b, c.o_gt)], W=["ML_Gi"])
                        kb.dma("sp", Gf[d * H:(d + 1) * H, :], S["zT"][b, c.o_gt + (2 * d + 1) * H:c.o_gt + (2 * d + 2) * H, :], R=[("zT", b, c.o_gt)], W=["ML_Gf"])
                    kb.op("pool", lambda h: h.memset(ones[:], 1.0), W=["ML_ones"])
                    kb.op("dve", lambda h: h.tensor_scalar_add(out=Gi[:], in0=Gi[:], scalar1=gbi[:, 0:1]), R=["ML_Gi", "ML_gb"], W=["ML_Gi"])
                    kb.op("dve", lambda h: h.tensor_scalar(out=Gf[:], in0=Gf[:], scalar1=gbf[:, 0:1], scalar2=-1.0, op0=ALU.add, op1=ALU.mult), R=["ML_Gf", "ML_gb"], W=["ML_Gf"])
                    kb.op("act", lambda h: h.activation(out=Gf[:], in_=Gf[:], func=AF.Exp), R=["ML_Gf"], W=["ML_Gf"])
                    kb.op("act", lambda h: h.activation(out=Gf[:], in_=Gf[:], func=AF.Ln, bias=1.0), R=["ML_Gf"], W=["ML_Gf"])
                    kb.op("dve", lambda h: h.tensor_scalar_mul(out=Gf[:], in0=Gf[:], scalar1=-1.0), R=["ML_Gf"], W=["ML_Gf"])

                    def scan2(outn, outr, src, op1, kn, kr, ksrc):
                        kb.op("dve", lambda h: h.tensor_tensor_scan(out=outn[:], data0=ones[:], data1=src[:], initial=0.0, op0=ALU.mult, op1=op1),
                              R=[ksrc, "ML_ones", kn], W=[kn])
                        kb.op("dve", lambda h: h.tensor_tensor_scan(out=outr[:, 0:CTX][:, ::-1], data0=ones[:, 0:CTX], data1=src[:, 0:CTX][:, ::-1], initial=0.0,
                                                                    op0=ALU.mult, op1=op1), R=[ksrc, "ML_ones", kr], W=[kr])
                        kb.op("dve", lambda h: h.tensor_tensor_scan(out=outr[:, CTX:T][:, ::-1], data0=ones[:, CTX:T], data1=src[:, CTX:T][:, ::-1], initial=outr[:, 0:1],
                                                                    op0=ALU.mult, op1=op1), R=[ksrc, "ML_ones", kr], W=[kr])

                    def select(dst, fn_, fr_, kd, kn, kr):
                        kb.op("dve", lambda h: h.tensor_scalar(out=fn_[:], in0=fn_[:], scalar1=dirm[0:H2, 0:1], scalar2=None, op0=ALU.mult), R=[kn, "dirmask"], W=[kn])
                        kb.op("dve", lambda h: h.scalar_tensor_tensor(out=dst[:], in0=fr_[:], scalar=dirm[0:H2, 1:2], in1=fn_[:], op0=ALU.mult, op1=ALU.add),
                              R=[kn, kr, "dirmask", kd], W=[kd])
                    scan2(Fn, Fr, Gf, ALU.add, "ML_Fn", "ML_Fr", "ML_Gf")
                    select(Gf, Fn, Fr, "ML_Gf", "ML_Fn", "ML_Fr")
                    kb.op("dve", lambda h: h.tensor_tensor(out=Gi[:], in0=Gi[:], in1=Gf[:], op=ALU.subtract), R=["ML_Gi", "ML_Gf"], W=["ML_Gi"])
                    scan2(Fn, Fr, Gi, ALU.max, "ML_Fn", "ML_Fr", "ML_Gi")
                    select(Fn, Fn, Fr, "ML_Fn", "ML_Fn", "ML_Fr")
                    kb.op("dve", lambda h: h.scalar_tensor_tensor(out=Fr[:], in0=Gf[:], scalar=-1.0, in1=Fn[:], op0=ALU.mult, op1=ALU.subtract),
                          R=["ML_Gf", "ML_Fn", "ML_Fr"], W=["ML_Fr"])
                    kb.op("dve", lambda h: h.tensor_scalar_mul(out=Fn[:], in0=Fn[:], scalar1=-1.0), R=["ML_Fn"], W=["ML_Fn"])
                    kb.dma("sp", S["mlg"][b], Fn[:], R=["ML_Fn"], W=[("mlg", b)])
                    for (src, dst, func, ks) in ((Gi, colA, AF.Identity, "ML_Gi"), (Fr, expm, AF.Exp, "ML_Fr")):
                        for n0 in range(0, NCH, 32):
                            n1 = min(NCH, n0 + 32)
                            pt, pk = ps.get()
                            for n in range(n0, n1):
                                kb.op("pe", lambda h, n=n: h.transpose(out=pt[:, (n - n0) * H2:(n - n0 + 1) * H2], in_=src[:, n * 128:(n + 1) * 128],
                                                                       identity=ident[0:H2, 0:H2]), R=[ks, "ident"], W=[pk])
                            kb.op("act", lambda h: h.activation(out=dst[:, n0:n1, :].rearrange("p n r -> p (n r)"), in_=pt[:, 0:(n1 - n0) * H2], func=func),
                                  R=[pk], W=[("ML_col", id(dst))])
                with kb.scope() as sv:
                    vp = Pool(kb, sv, "ML_vT", 2, [128, T], F32)
                    kb.op("pool", lambda h: h.memset(vtok[:].rearrange("p n h v -> p (n h v)"), 1.0), W=["ML_vtok"])
                    vflat = vtok[:].rearrange("p n h v -> p n (h v)")
                    for kc in range(KW):
                        vt, vk = vp.get()
                        kb.dma("sp", vt[:], S["zT"][b, c.o_v + kc * 128:c.o_v + (kc + 1) * 128, :], R=[("zT", b, c.o_v + kc * 128)], W=[vk])
                        for n0 in range(0, NCH, 4):
                            n1 = min(NCH, n0 + 4)
                            pt, pk = ps.get()
                            for n in range(n0, n1):
                                kb.op("pe", lambda h, n=n: h.transpose(out=pt[:, (n - n0) * 128:(n - n0 + 1) * 128], in_=vt[:, n * 128:(n + 1) * 128], identity=ident[:]),
                                      R=[vk, "ident"], W=[pk])
                            c0 = kc * 128
                            while c0 < (kc + 1) * 128:
                                hh = c0 // DH
                                c1 = min((kc + 1) * 128, (hh + 1) * DH)
                                w0 = c0 - hh * DH
                                kb.op("dve", lambda h, c0=c0, c1=c1, hh=hh, w0=w0: h.tensor_copy(
                                    out=vflat[:, n0:n1, hh * DV + w0:hh * DV + w0 + (c1 - c0)],
                                    in_=pt[:, 0:(n1 - n0) * 128].rearrange("p (n c) -> p n c", c=128)[:, :, c0 - kc * 128:c1 - kc * 128]), R=[pk, "ML_vtok"], W=["ML_vtok"])
                                c0 = c1
                for hd in range(H):
                    with kb.scope() as sh:
                        qT = [sh.enter_context(sbt(nc, f"ML_q{j}", [DT, T], BF16)) for j in range(2)]
                        kT = [sh.enter_context(sbt(nc, f"ML_k{j}", [DT, T], BF16)) for j in range(2)]
                        hsum = sh.enter_context(sbt(nc, "ML_hsum", [128, NCH, DH], F32))
                        with kb.scope() as sc:
                            xp = Pool(kb, sc, "ML_xp", 2, [DT, T], F32)
                            yp = Pool(kb, sc, "ML_yp", 2, [DT, T], F32)
                            sgp = Pool(kb, sc, "ML_sgp", 2, [DT, T], F32)
                            for a in range(2):
                                for j in range(2):
                                    r0 = c.o_qk + a * W + hd * DH + j * DT
                                    xt, xk = xp.get()
                                    yt, yk = yp.get()
                                    sgt, sgk = sgp.get()
                                    kb.dma("sp", xt[:], S["zT"][b, r0:r0 + DT, :], R=[("zT", b, (r0 // 128) * 128), ("zT", b, ((r0 + DT - 1) // 128) * 128)], W=[xk])
                                    mi = (a * H + hd) * 2 + j
                                    conv3(o, yt, xt, [cw[:, t_, mi:mi + 1] for t_ in range(3)], cb[:, mi:mi + 1], DT, [xk, "ML_cw", "ML_cb"], yk)
                                    kb.op("act", lambda h: h.activation(out=sgt[:], in_=yt[:], func=AF.Sigmoid), R=[yk], W=[sgk])
                                    dst = qT[j] if a == 0 else kT[j]
                                    sc_ = 1.0 if a == 0 else DH ** -0.5
                                    kb.op("dve", lambda h, dst=dst, sc_=sc_: h.scalar_tensor_tensor(out=dst[:], in0=yt[:], scalar=sc_, in1=sgt[:], op0=ALU.mult, op1=ALU.mult),
                                          R=[yk, sgk], W=[("ML_qk", a, j)])
                        gbp = Pool(kb, sh, "ML_GB", 1, [128, T], F32)
                        dtp = Pool(kb, sh, "ML_DT", 3, [128, 512], F32)
                        ptp = Pool(kb, sh, "ML_PT", 3, [128, 512], BF16)
                        e1p = Pool(kb, sh, "ML_E1", 2, [128, 128], F32)
                        rcp = Pool(kb, sh, "ML_rc", 2, [128, 2], F32)
                        for d in range(2):
                            r_ = d * H + hd
                            gb, gbk = gbp.get()
                            kb.dma("sp", gb[:], S["mlg"][b, r_:r_ + 1, :].broadcast_to([128, T]), R=[("mlg", b)], W=[gbk])
                            if d == 0:
                                pos = {n: n for n in range(NCH)}
                            else:
                                pos = {}
                                for i_, n in enumerate(list(range(NCC - 1, -1, -1)) + list(range(NCH - 1, NCC - 1, -1))):
                                    pos[n] = i_
                            for (t0, t1) in tok_tiles(c):
                                cts = list(range(t0 // 128, t1 // 128))
                                accs = {ct: ps.hold() for ct in cts}
                                keys_for = {ct: [cs for cs in range(NCH) if pos[cs] <= pos[ct]] for ct in cts}
                                all_cs = sorted(set(sum(keys_for.values(), [])), key=lambda n: pos[n])
                                pend = []
                                for cs in all_cs:
                                    al = [ct for ct in cts if pos[cs] <= pos[ct]]
                                    a0, a1 = min(al) * 128, (max(al) + 1) * 128
                                    n = a1 - a0
                                    spt, spk = ps.get()
                                    for j in range(2):
                                        kb.op("pe", lambda h, j=j: h.matmul(out=spt[:, 0:n], lhsT=kT[j][:, cs * 128:(cs + 1) * 128], rhs=qT[j][:, a0:a1],
                                                                            start=(j == 0), stop=(j == 1)), R=[("ML_qk", 1, j), ("ML_qk", 0, j)], W=[spk])
                                    while pend:
                                        pend.pop(0)()
                                    dtt, dtk = dtp.get()
                                    acol = colA[:, cs, r_:r_ + 1]
                                    for ct in al:
                                        o0 = ct * 128 - a0
                                        if ct == cs:
                                            e1, e1k = e1p.get()
                                            kb.op("dve", lambda h, ct=ct: h.scalar_tensor_tensor(out=e1[:], in0=gb[:, ct * 128:(ct + 1) * 128], scalar=acol, in1=negm[:, d, :],
                                                                                                op0=ALU.add, op1=ALU.add), R=[gbk, ("ML_col", id(colA)), "negmask"], W=[e1k])
                                            kb.op("act", lambda h, o0=o0: h.activation(out=dtt[:, o0:o0 + 128], in_=e1[:], func=AF.Exp), R=[e1k, dtk], W=[dtk])
                                    nd = [ct for ct in al if ct != cs]
                                    if nd:
                                        b0, b1 = min(nd) * 128, (max(nd) + 1) * 128
                                        kb.op("act", lambda h: h.activation(out=dtt[:, b0 - a0:b1 - a0], in_=gb[:, b0:b1], func=AF.Exp, bias=acol),
                                              R=[gbk, ("ML_col", id(colA)), dtk], W=[dtk])
                                    ptt, ptk = ptp.get()
                                    kb.op("dve", lambda h: h.tensor_tensor(out=ptt[:, 0:n], in0=spt[:, 0:n], in1=dtt[:, 0:n], op=ALU.mult), R=[spk, dtk], W=[ptk])
                                    def mkpv(al=al, a0=a0, cs=cs, ptt=ptt, ptk=ptk):
                                        def f():
                                            for ct in al:
                                                o0 = ct * 128 - a0
                                                at, ak = accs[ct]
                                                kb.op("pe", lambda h, o0=o0, at=at: h.matmul(out=at[:, 0:DV], lhsT=ptt[:, o0:o0 + 128], rhs=vtok[:, cs, hd, :],
                                                                                             start=(pos[cs] == 0), stop=(cs == ct)), R=[ptk, "ML_vtok"], W=[ak])
                                        return f
                                    pend.append(mkpv())
                                while pend:
                                    pend.pop(0)()
                                for ct in cts:
                                    ps.release(accs[ct][1])
                                for ct in cts:
                                    at, ak = accs[ct]
                                    rc, rck = rcp.get()
                                    kb.op("act", lambda h: h.activation(out=rc[:, 0:1], in_=at[:, DH:DH + 1], func=AF.Abs), R=[ak, rck], W=[rck])
                                    kb.op("dve", lambda h: h.tensor_tensor(out=rc[:, 0:1], in0=rc[:, 0:1], in1=expm[:, ct, r_:r_ + 1], op=ALU.max),
                                          R=[rck, ("ML_col", id(expm))], W=[rck])
                                    kb.op("dve", lambda h: h.reciprocal(out=rc[:, 1:2], in_=rc[:, 0:1]), R=[rck], W=[rck])
                                    if d == 0:
                                        kb.op("act", lambda h, ct=ct: h.activation(out=hsum[:, ct, :], in_=at[:, 0:DH], func=AF.Identity, scale=rc[:, 1:2]), R=[ak, rck, "ML_hsum"], W=["ML_hsum"])
                                    else:
                                        kb.op("dve", lambda h, ct=ct: h.scalar_tensor_tensor(out=hsum[:, ct, :], in0=at[:, 0:DH], scalar=rc[:, 1:2], in1=hsum[:, ct, :],
                                                                                            op0=ALU.mult, op1=ALU.add), R=[ak, rck, "ML_hsum"], W=["ML_hsum"])
                        stp = Pool(kb, sh, "ML_st", 2, [128, 8], F32)
                        for n in range(NCH):
                            stt, stk = stp.get()
                            kb.op("dve", lambda h, n=n: h.bn_stats(out=stt[:, 0:6], in_=hsum[:, n, :]), R=["ML_hsum"], W=[stk])
                            kb.op("dve", lambda h: h.bn_aggr(out=stt[:, 6:8], in_=stt[:, 0:6]), R=[stk], W=[stk])
                            kb.op("dve", lambda h: h.tensor_scalar_add(out=stt[:, 0:1], in0=stt[:, 7:8], scalar1=LN_EPS), R=[stk], W=[stk])
                            kb.op("act", lambda h: h.activation(out=stt[:, 0:1], in_=stt[:, 0:1], func=AF.Ln), R=[stk], W=[stk])
                            kb.op("act", lambda h: h.activation(out=stt[:, 0:1], in_=stt[:, 0:1], func=AF.Exp, scale=-0.5), R=[stk], W=[stk])
                            kb.op("dve", lambda h, n=n: h.tensor_scalar(out=hn[:, n, hd * DH:(hd + 1) * DH], in0=hsum[:, n, :], scalar1=stt[:, 6:7], scalar2=stt[:, 0:1],
                                                                        op0=ALU.subtract, op1=ALU.mult), R=["ML_hsum", stk], W=["ML_hn"])
                with kb.scope() as so:
                    op_ = Pool(kb, so, "ML_o", 2, [128, T], F32)
                    yo_ = Pool(kb, so, "ML_yo", 2, [128, T], BF16)
                    for kc in range(KW):
                        ot, ok = op_.get()
                        yo, yok = yo_.get()
                        kb.dma("sp", ot[:], S["zT"][b, c.o_o + kc * 128:c.o_o + (kc + 1) * 128, :], R=[("zT", b, c.o_o + kc * 128)], W=[ok])
                        kb.op("act", lambda h: h.activation(out=ot[:], in_=ot[:], func=AF.Sigmoid), R=[ok], W=[ok])
                        for n0 in range(0, NCH, 4):
                            n1 = min(NCH, n0 + 4)
                            pt, pk = ps.get()
                            ptb = pt[:].bitcast(BF16)
                            for n in range(n0, n1):
                                kb.op("pe", lambda h, n=n: h.transpose(out=ptb[:, (n - n0) * 128:(n - n0 + 1) * 128], in_=hn[:, n, kc * 128:(kc + 1) * 128], identity=identb[:]),
                                      R=["ML_hn", "identb"], W=[pk])
                            kb.op("dve", lambda h: h.scalar_tensor_tensor(out=yo[:, n0 * 128:n1 * 128], in0=ptb[:, 0:(n1 - n0) * 128], scalar=ngcol[:, kc:kc + 1],
                                                                          in1=ot[:, n0 * 128:n1 * 128], op0=ALU.mult, op1=ALU.mult), R=[pk, ok, "ML_ng"], W=[yok])
                        kb.dma("sp", S["ymlT"][b, kc * 128:(kc + 1) * 128, :], yo[:], R=[yok], W=[("ymlT", b, kc)])


def hy_consts(L):
    import ml_dtypes
    NL = L // 128
    t = np.arange(L, dtype=np.float64)
    f = np.arange(L, dtype=np.float64)
    th = np.pi * np.outer(t, 2 * f + 1) / (2 * L)
    C, Sn = np.cos(th), np.sin(th)
    def ftile(M):
        return M.reshape(NL, 128, NL, 128).transpose(2, 1, 0, 3)
    F = np.stack([ftile(C), ftile(Sn)], 2)
    TG = min(512, L)
    NG = L // TG
    def itile(M):
        return M.T.reshape(NL, 128, NG, TG).transpose(2, 1, 0, 3)
    Iv = np.stack([itile(C), itile(Sn)], 2)
    pos = np.arange(L, dtype=np.float64)
    tt = pos / (L - 1)
    bands = np.linspace(1e-4, 15.0, 16)
    ang = (2.0 * np.pi / L) * pos[:, None] * bands[None, :]
    feats = np.concatenate([tt[:, None], np.cos(ang), np.sin(ang)], -1)
    lagt = tt.reshape(NL, 128).T
    lagmask = np.ones((128, NL), np.float32)
    lagmask[0, 0] = 0.0
    return {f"dftF{L}": F.astype(ml_dtypes.bfloat16), f"dftI{L}": Iv.astype(ml_dtypes.bfloat16),
            f"hyfeat{L}": np.ascontiguousarray(feats.T).astype(np.float32), f"lagt{L}": np.ascontiguousarray(lagt).astype(np.float32),
            f"lagmask{L}": lagmask}


def phase_HY(o, l, last):
    c, nc, kb, I, S, ps = o["c"], o["nc"], o["kb"], o["I"], o["S"], o["ps"]
    T, W, KW, NB, CTX, SEQ = c.T, c.W, c.KW, c.NB, c.CTX, c.SEQ
    ident = o["ident"]
    insts = [(SEQ, CTX)] + ([] if (last and not getattr(c, "force_ctx", False)) else [(CTX, 0)])
    with kb.scope() as st:
        cw = st.enter_context(sbt(nc, "HY_cw", [128, 3, 3 * KW], F32))
        cb = st.enter_context(sbt(nc, "HY_cb", [128, 3 * KW], F32))
        for t_ in range(3):
            kb.dma("sp", cw[:, t_, :], I["hy_conv_w"][l, t_, :].rearrange("(m p) -> p m", p=128), W=["HY_cw"], allow_slow_non_contiguous=True)
        kb.dma("sp", cb[:], I["hy_conv_b"][l].rearrange("(m p) -> p m", p=128), W=["HY_cb"], allow_slow_non_contiguous=True)
        xp = Pool(kb, st, "HY_xp", 2, [128, T], F32)
        yp = Pool(kb, st, "HY_yp", 2, [128, T], F32)
        for b in range(NB):
            for m in range(3 * KW):
                xt, xk = xp.get()
                yt, yk = yp.get()
                kb.dma("sp", xt[:], S["zT"][b, m * 128:(m + 1) * 128, :], R=[("zT", b, m * 128)], W=[xk])
                conv3(o, yt, xt, [cw[:, t_, m:m + 1] for t_ in range(3)], cb[:, m:m + 1], 128, [xk, "HY_cw", "HY_cb"], yk)
                kb.dma("sp", S["hyc"][b, m * 128:(m + 1) * 128, :], yt[:], R=[yk], W=[("hyc", b, m)])
    for (L, toff) in insts:
        NL = L // 128
        khat = S[f"khat{L}"]
        with kb.scope() as st:
            w1 = st.enter_context(sbt(nc, "HY_w1", [c.FEAT, c.HID], F32))
            w2 = st.enter_context(sbt(nc, "HY_w2", [c.HID, c.HID], F32))
            w3 = st.enter_context(sbt(nc, "HY_w3", [c.HID, 4 * W], F32))
            cols = st.enter_context(sbt(nc, "HY_cols", [c.HID, 8], F32))
            featT = st.enter_context(sbt(nc, "HY_featT", [c.FEAT, L], F32))
            h1T = st.enter_context(sbt(nc, "HY_h1T", [c.HID, L], F32))
            h2T = st.enter_context(sbt(nc, "HY_h2T", [c.HID, L], F32))
            lagt = st.enter_context(sbt(nc, "HY_lagt", [128, NL], F32))
            lagm = st.enter_context(sbt(nc, "HY_lagm", [128, NL], F32))
            adec = st.enter_context(sbt(nc, "HY_adec", [128, 2, W], F32))
            winf = st.enter_context(sbt(nc, "HY_winf", [128, 2, W], F32))
            hk = st.enter_context(sbt(nc, "HY_hk", [128, 2, W], F32))
            HS = st.enter_context(sbt(nc, "HY_HS", [128, NL, 2, W], BF16))
            kb.dma("sp", w1[:], I["hy_ffn_w1"][l], W=["HY_w"])
            kb.dma("sp", w2[:], I["hy_ffn_w2"][l], W=["HY_w"])
            kb.dma("sp", w3[:], I["hy_ffn_w3"][l], W=["HY_w"])
            for i_, nm in enumerate(("hy_ffn_b1", "hy_ffn_b2", "hy_sin_freq")):
                kb.dma("sp", cols[:, i_:i_ + 1], I[nm][l].rearrange("(h o) -> h o", o=1), W=["HY_cols"])
            kb.dma("sp", featT[:], I[f"hyfeat{L}"], W=["HY_featT"])
            kb.dma("sp", lagt[:], I[f"lagt{L}"], W=["HY_lagt"])
            kb.dma("sp", lagm[:], I[f"lagmask{L}"], W=["HY_lagm"])
            kb.dma("sp", adec[:].rearrange("p o w -> p (o w)"), I["hy_decay"][l].rearrange("o w -> (o w)").rearrange("(a n) -> a n", a=1).broadcast_to([128, 2 * W]), W=["HY_adec"])
            kb.op("act", lambda h: h.activation(out=adec[:], in_=adec[:], func=AF.Abs), R=["HY_adec"], W=["HY_adec"])
            kb.op("dve", lambda h: h.tensor_scalar_mul(out=lagt[:], in0=lagt[:], scalar1=-1.0), R=["HY_lagt"], W=["HY_lagt"])
            for i_ in range(2):
                kb.op("dve", lambda h, i_=i_: h.tensor_tensor(out=cols[:, 3 + i_:4 + i_], in0=cols[:, i_:i_ + 1], in1=cols[:, 2:3], op=ALU.mult), R=["HY_cols"], W=["HY_cols"])
                kb.op("dve", lambda h, i_=i_: h.tensor_scalar_add(out=cols[:, 3 + i_:4 + i_], in0=cols[:, 3 + i_:4 + i_], scalar1=9.0 * math.pi), R=["HY_cols"], W=["HY_cols"])
            tmpp = Pool(kb, st, "HY_tmp", 2, [c.HID, 512], F32)
            hrf = st.enter_context(sbt(nc, "HY_rrf", [c.HID, 512], F32))
            hri = st.enter_context(sbt(nc, "HY_rri", [c.HID, 512], I32))
            for (wm, kdim, src, dst, fbc, ksrc, kdst) in ((w1, c.FEAT, featT, h1T, 3, "HY_featT", "HY_h1T"), (w2, c.HID, h1T, h2T, 4, "HY_h1T", "HY_h2T")):
                for (a0, a1) in tgroups(0, L):
                    n = a1 - a0
                    pt, pk = ps.get()
                    kb.op("pe", lambda h: h.matmul(out=pt[0:c.HID, 0:n], lhsT=wm[0:kdim, :], rhs=src[0:kdim, a0:a1], start=True, stop=True), R=["HY_w", ksrc], W=[pk])
                    tt, tk = tmpp.get()
                    kb.op("dve", lambda h: h.tensor_scalar(out=tt[:, 0:n], in0=pt[0:c.HID, 0:n], scalar1=cols[:, 2:3], scalar2=cols[:, fbc:fbc + 1], op0=ALU.mult, op1=ALU.add),
                          R=[pk, "HY_cols"], W=[tk])
                    range_reduce(kb, tt[:, 0:n], hrf[:, 0:n], hri[:, 0:n], [tk, "HY_rr"])
                    kb.op("act", lambda h: h.activation(out=dst[:, a0:a1], in_=tt[:, 0:n], func=AF.Sin), R=[tk, kdst], W=[kdst])
            fp_ = Pool(kb, st, "HY_F", 2, [128, 2, NL, 128], BF16)
            ksp = Pool(kb, st, "HY_ks", 2, [128, 2, W], F32)
            for o_ in range(2):
                for dc in range(NL):
                    for dr in range(2):
                        kb.op("act", lambda h, dr=dr: h.activation(out=winf[:, dr, :], in_=adec[:, o_, :], func=AF.Exp, scale=lagt[:, dc:dc + 1]),
                              R=["HY_adec", "HY_lagt", "HY_winf"], W=["HY_winf"])
                    hkf = hk[:].rearrange("p d w -> p (d w)")
                    wff = winf[:].rearrange("p d w -> p (d w)")
                    for (a0, a1) in tgroups(0, 2 * W):
                        pt, pk = ps.get()
                        kb.op("pe", lambda h: h.matmul(out=pt[:, 0:a1 - a0], lhsT=h2T[:, dc * 128:(dc + 1) * 128], rhs=w3[:, o_ * 2 * W + a0:o_ * 2 * W + a1], start=True, stop=True),
                              R=["HY_h2T", "HY_w"], W=[pk])
                        kb.op("dve", lambda h: h.tensor_tensor(out=hkf[:, a0:a1], in0=pt[:, 0:a1 - a0], in1=wff[:, a0:a1], op=ALU.mult), R=[pk, "HY_winf", "HY_hk"], W=["HY_hk"])
                    mcol = lagm[:, dc:dc + 1]
                    kb.op("dve", lambda h: h.scalar_tensor_tensor(out=HS[:, dc, 0, :], in0=hk[:, 1, :], scalar=mcol, in1=hk[:, 0, :], op0=ALU.mult, op1=ALU.add),
                          R=["HY_hk", "HY_lagm", "HY_HS"], W=["HY_HS"])
                    kb.op("dve", lambda h: h.scalar_tensor_tensor(out=HS[:, dc, 1, :], in0=hk[:, 1, :], scalar=mcol, in1=hk[:, 0, :], op0=ALU.mult, op1=ALU.subtract),
                          R=["HY_hk", "HY_lagm", "HY_HS"], W=["HY_HS"])
                for fc in range(NL):
                    ft, fk = fp_.get()
                    kb.dma("sp", ft[:], I[f"dftF{L}"][fc], W=[fk])
                    kst, kk = ksp.get()
                    for ri in range(2):
                        for (a0, a1) in tgroups(0, W):
                            pt, pk = ps.get()
                            for dc in range(NL):
                                kb.op("pe", lambda h, dc=dc: h.matmul(out=pt[:, 0:a1 - a0], lhsT=ft[:, ri, dc, :], rhs=HS[:, dc, ri, a0:a1], start=(dc == 0), stop=(dc == NL - 1)),
                                      R=[fk, "HY_HS"], W=[pk])
                            kb.op("act", lambda h: h.activation(out=kst[:, ri, a0:a1], in_=pt[:, 0:a1 - a0], func=AF.Identity, scale=1.0 / L), R=[pk, kk], W=[kk])
                    kb.dma("sp", khat[fc * 128:(fc + 1) * 128, o_], kst[:], R=[kk], W=[("khat", L, fc, o_)])
    with kb.scope() as st0:
        bcol = st0.enter_context(sbt(nc, "HY_bcol", [128, 2, KW], F32))
        for o_ in range(2):
            load_cols(o, bcol[:, o_, :], I["hy_bias"][l, o_, :], "HY_bcol", KW)
        for (L, toff) in insts:
            NL = L // 128
            TG = min(512, L)
            NG = L // TG
            khat = S[f"khat{L}"]
            for o_ in range(2):
                src = S["hyc"] if o_ == 0 else S["hyy1"]
                srck = (lambda b, k: ("hyc", b, k)) if o_ == 0 else (lambda b, k: ("hyy1", b, k, L))
                for b in range(NB):
                    with kb.scope() as st:
                        Y = st.enter_context(sbt(nc, "HY_Y", [128, NL, 2, W], BF16))
                        with kb.scope() as s1:
                            utok = s1.enter_context(sbt(nc, "HY_utok", [128, NL, W], BF16))
                            xin = Pool(kb, s1, "HY_xin", 2, [128, L], F32)
                            for kc in range(KW):
                                xt, xk = xin.get()
                                kb.dma("sp", xt[:], src[b, kc * 128:(kc + 1) * 128, toff:toff + L], R=[srck(b, kc)], W=[xk])
                                for n0 in range(0, NL, 4):
                                    n1 = min(NL, n0 + 4)
                                    pt, pk = ps.get()
                                    for n in range(n0, n1):
                                        kb.op("pe", lambda h, n=n: h.transpose(out=pt[:, (n - n0) * 128:(n - n0 + 1) * 128], in_=xt[:, n * 128:(n + 1) * 128], identity=ident[:]),
                                              R=[xk, "ident"], W=[pk])
                                    kb.op("act", lambda h: h.activation(out=utok[:, n0:n1, kc * 128:(kc + 1) * 128], in_=pt[:, 0:(n1 - n0) * 128].rearrange("p (n c) -> p n c", c=128),
                                                                        func=AF.Identity), R=[pk, "HY_utok"], W=["HY_utok"])
                            fp_ = Pool(kb, s1, "HY_F2", 2, [128, 2, NL, 128], BF16)
                            kp_ = Pool(kb, s1, "HY_K", 2, [128, 2, W], F32)
                            tp_ = Pool(kb, s1, "HY_t", 4, [128, 512], F32)
                            for fc in range(NL):
                                ft, fk = fp_.get()
                                kb.dma("sp", ft[:], I[f"dftF{L}"][fc], W=[fk])
                                kt, kk = kp_.get()
                                kb.dma("sp", kt[:], khat[fc * 128:(fc + 1) * 128, o_], R=[("khat", L, fc, o_)], W=[kk])
                                for (a0, a1) in tgroups(0, W):
                                    n = a1 - a0
                                    (pa, pak), (pb, pbk) = ps.get(), ps.get()
                                    for cs_, (pt, pk) in enumerate(((pa, pak), (pb, pbk))):
                                        for tc in range(NL):
                                            kb.op("pe", lambda h, tc=tc, cs_=cs_, pt=pt: h.matmul(out=pt[:, 0:n], lhsT=ft[:, cs_, tc, :], rhs=utok[:, tc, a0:a1],
                                                                                              start=(tc == 0), stop=(tc == NL - 1)), R=[fk, "HY_utok"], W=[pk])
                                    t1, t1k = tp_.get()
                                    t2, t2k = tp_.get()
                                    kr, ki = kt[:, 0, a0:a1], kt[:, 1, a0:a1]
                                    kb.op("dve", lambda h: h.tensor_tensor(out=t1[:, 0:n], in0=pa[:, 0:n], in1=kr, op=ALU.mult), R=[pak, kk], W=[t1k])
                                    kb.op("dve", lambda h: h.tensor_tensor(out=t2[:, 0:n], in0=pb[:, 0:n], in1=ki, op=ALU.mult), R=[pbk, kk], W=[t2k])
                                    kb.op("dve", lambda h: h.tensor_tensor(out=Y[:, fc, 0, a0:a1], in0=t1[:, 0:n], in1=t2[:, 0:n], op=ALU.add), R=[t1k, t2k], W=[("HY_Y", fc)])
                                    t3, t3k = tp_.get()
                                    t4, t4k = tp_.get()
                                    kb.op("dve", lambda h: h.tensor_tensor(out=t3[:, 0:n], in0=pb[:, 0:n], in1=kr, op=ALU.mult), R=[pbk, kk], W=[t3k])
                                    kb.op("dve", lambda h: h.tensor_tensor(out=t4[:, 0:n], in0=pa[:, 0:n], in1=ki, op=ALU.mult), R=[pak, kk], W=[t4k])
                                    kb.op("dve", lambda h: h.tensor_tensor(out=Y[:, fc, 1, a0:a1], in0=t3[:, 0:n], in1=t4[:, 0:n], op=ALU.subtract), R=[t3k, t4k], W=[("HY_Y", fc)])
                        ip_ = Pool(kb, st, "HY_I", 2, [128, 2, NL, TG], BF16)
                        up_ = Pool(kb, st, "HY_u", 3, [128, TG], F32)
                        gp_ = Pool(kb, st, "HY_g", 3, [128, TG], F32)
                        op_ = Pool(kb, st, "HY_o", 3, [128, TG], F32 if o_ == 0 else BF16)
                        for tg in range(NG):
                            it, ik = ip_.get()
                            kb.dma("sp", it[:], I[f"dftI{L}"][tg], W=[ik])
                            q0 = toff + tg * TG
                            for kc in range(KW):
                                pt, pk = ps.get()
                                i_ = 0
                                for fc in range(NL):
                                    for cs_ in range(2):
                                        kb.op("pe", lambda h, fc=fc, cs_=cs_: h.matmul(out=pt[:, 0:TG], lhsT=Y[:, fc, cs_, kc * 128:(kc + 1) * 128], rhs=it[:, cs_, fc, :],
                                                                                       start=(i_ == 0), stop=(i_ == 2 * NL - 1)), R=[("HY_Y", fc), ik], W=[pk])
                                        i_ += 1
                                ut, uk = up_.get()
                                gt, gk = gp_.get()
                                ot, ok = op_.get()
                                kb.dma("sp", ut[:], src[b, kc * 128:(kc + 1) * 128, q0:q0 + TG], R=[srck(b, kc)], W=[uk])
                                grow = (1 + o_) * W + kc * 128
                                kb.dma("sp", gt[:], S["hyc"][b, grow:grow + 128, q0:q0 + TG], R=[("hyc", b, grow // 128)], W=[gk])
                                kb.op("dve", lambda h: h.scalar_tensor_tensor(out=ut[:], in0=ut[:], scalar=bcol[:, o_, kc:kc + 1], in1=pt[:, 0:TG], op0=ALU.mult, op1=ALU.add),
                                      R=[uk, pk, "HY_bcol"], W=[uk])
                                kb.op("dve", lambda h: h.tensor_tensor(out=ot[:], in0=ut[:], in1=gt[:], op=ALU.mult), R=[uk, gk, ok], W=[ok])
                                if o_ == 0:
                                    kb.dma("sp", S["hyy1"][b, kc * 128:(kc + 1) * 128, q0:q0 + TG], ot[:], R=[ok], W=[("hyy1", b, kc, L, tg)])
                                else:
                                    kb.dma("sp", S["yhyT"][b, kc * 128:(kc + 1) * 128, q0:q0 + TG], ot[:], R=[ok], W=[("yhyT", b, kc, L, tg)])


WEIGHT_KEYS = ("w_mod", "b_mod", "w_in", "hy_conv_w", "hy_conv_b", "hy_ffn_w1", "hy_ffn_b1", "hy_ffn_w2", "hy_ffn_b2",
               "hy_ffn_w3", "hy_sin_freq", "hy_decay", "hy_bias", "ml_conv_w", "ml_conv_b", "ml_gate_b", "ml_norm_g",
               "s5_a_re", "s5_a_im", "s5_log_dt", "s5_b_re", "s5_b_im", "s5_c_re", "s5_c_im", "s5_d", "s5_glu_w", "s5_glu_b",
               "w_hy_out", "w_ml_out", "w_s5_out", "w_out", "ln1_g", "ln1_b", "ln2_g", "ln2_b", "w_ff1", "w_ff2")


def make_in_maps(cfg, inputs, ncores):
    consts = host_consts(cfg)
    x = np.asarray(inputs["x"], np.float32)
    ctx = np.asarray(inputs["ctx"], np.float32)
    cvec = np.asarray(inputs["c"], np.float32)
    cctx = np.asarray(inputs["c_ctx"], np.float32)
    wts = {k: np.ascontiguousarray(np.asarray(inputs[k], np.float32)) for k in WEIGHT_KEYS}
    maps = []
    for i in range(ncores):
        bs = [cfg.NB * i + j for j in range(cfg.NB)]
        m = dict(consts)
        m.update(wts)
        m["xin"] = np.ascontiguousarray(np.stack([np.concatenate([ctx[b], x[b]], 0) for b in bs]))
        m["cloc"] = np.ascontiguousarray(np.stack([cvec[bs[0]], cvec[bs[1]], cctx]))
        maps.append(m)
    return maps


_PROG = {}


def kernel(**inputs):
    cfg = Cfg()
    if "nc" not in _PROG:
        _PROG["nc"] = build_program(cfg)
    nc = _PROG["nc"]
    ncores = 8
    in_maps = make_in_maps(cfg, inputs, ncores)
    res = run_bass_kernel_spmd(nc, in_maps, core_ids=list(range(ncores)))
    out = np.concatenate([np.asarray(r["out"], np.float32) for r in res.results], axis=0)
    return out
```
